# Optimizing a Trainium2 kernel written in Bass

```python
import math
import jax
import jax.numpy as jnp
from jax import lax
import numpy as np

D_MODEL = 1024
BATCH = 8
SEQ = 8192
DEPTH = 2

GRID_W = 64
CTX_LEN = 256
N_EVEN = (DEPTH + 1) // 2
N_ODD = DEPTH // 2
N_MOD = 9
D_FF = 2816
FFN_RES = 0.5
EPS = 1e-6
ROPE_BASE = 10000.0
Q_BLOCK = 128
NEG_INF = -1e30

MLA_HEADS = 8
MLA_Q_LORA = 256
MLA_KV_LORA = 128
MLA_NOPE = 64
MLA_ROPE = 32
MLA_V = 64
MLA_SCALE = (MLA_NOPE + MLA_ROPE) ** -0.5

SWA_HEADS = 8
SWA_KV_HEADS = 2
SWA_GROUP = SWA_HEADS // SWA_KV_HEADS
SWA_HEAD_DIM = 64
SWA_WINDOW = 128
SWA_SCALE = SWA_HEAD_DIM ** -0.5

ATTN_IN = MLA_Q_LORA + MLA_KV_LORA + MLA_ROPE + (SWA_HEADS + 2 * SWA_KV_HEADS) * SWA_HEAD_DIM
ATTN_OUT = MLA_HEADS * MLA_V + SWA_HEADS * SWA_HEAD_DIM

S5_WIDTH = D_MODEL
S5_GROUP = 16
S5_GROUPS = S5_WIDTH // S5_GROUP
S5_STATE = 64
S5_CHUNK = 128
S5_MAX_RE = -1e-4

kernel_name = 'hybrid_mla_swa_s5_macaron_dit'


def rmsnorm(x, g):
    xf = x.astype(jnp.float32)
    y = xf * lax.rsqrt(jnp.mean(xf * xf, axis=-1, keepdims=True) + EPS)
    return (y * g.astype(jnp.float32)).astype(x.dtype)


def modulate(h, g, shift, scale):
    return rmsnorm(h, g) * (1 + scale) + shift


def swiglu(h, w13, w2):
    gate, up = jnp.split(h @ w13, 2, axis=-1)
    return (jax.nn.silu(gate) * up) @ w2


def ffn_sublayer(h, m, j, g_pre, g_post, w13, w2):
    a = modulate(h, g_pre, m[3 * j], m[3 * j + 1])
    return h + FFN_RES * m[3 * j + 2] * rmsnorm(swiglu(a, w13, w2), g_post)


def axial_rope_table(rows, cols, d_rot):
    d_axis = d_rot // 2
    inv = ROPE_BASE ** (-jnp.arange(0, d_axis, 2, dtype=jnp.float32) / d_axis)
    ang = jnp.concatenate([rows.astype(jnp.float32)[:, None] * inv,
                           cols.astype(jnp.float32)[:, None] * inv], axis=-1)
    return jnp.cos(ang), jnp.sin(ang)


def apply_rope(x, table):
    cos, sin = table
    bshape = cos.shape[:1] + (1,) * (x.ndim - 3) + cos.shape[1:]
    cos = cos.reshape(bshape).astype(x.dtype)
    sin = sin.reshape(bshape).astype(x.dtype)
    x1, x2 = jnp.split(x, 2, axis=-1)
    return jnp.concatenate([x1 * cos - x2 * sin, x1 * sin + x2 * cos], axis=-1)


def to_blocks(t):
    b, s = t.shape[:2]
    return jnp.moveaxis(t.reshape((b, s // Q_BLOCK, Q_BLOCK) + t.shape[2:]), 1, 0)


def from_blocks(t):
    t = jnp.moveaxis(t, 0, 1)
    return t.reshape((t.shape[0], t.shape[1] * t.shape[2]) + t.shape[3:])


def softmax_attention(q, k, v, scale):
    s = jnp.einsum('bqhd,bkhd->bhqk', q, k, preferred_element_type=jnp.float32) * scale
    p = jax.nn.softmax(s, axis=-1).astype(v.dtype)
    return jnp.einsum('bhqk,bkhd->bqhd', p, v)


def sink_attention(q, k, v, sink, mask):
    s = jnp.einsum('bqkgd,bjkd->bkgqj', q, k, preferred_element_type=jnp.float32) * SWA_SCALE
    if mask is not None:
        s = jnp.where(mask, s, NEG_INF)
    sink_col = jnp.broadcast_to(sink.astype(jnp.float32)[None, :, :, None, None], s.shape[:-1] + (1,))
    p = jax.nn.softmax(jnp.concatenate([sink_col, s], axis=-1), axis=-1)[..., 1:]
    return jnp.einsum('bkgqj,bjkd->bqkgd', p.astype(v.dtype), v)


def attn_project(h, w_in, q_norm, w_uq, kv_norm, w_ukv, rope_a, rope_b):
    b, n = h.shape[:2]
    sizes = [MLA_Q_LORA, MLA_KV_LORA, MLA_ROPE, SWA_HEADS * SWA_HEAD_DIM, SWA_KV_HEADS * SWA_HEAD_DIM]
    cuts = [int(v) for v in np.cumsum(sizes)]
    cq, ckv, kpe, qs, ks, vs = jnp.split(h @ w_in, cuts, axis=-1)
    q = (rmsnorm(cq, q_norm) @ w_uq).reshape(b, n, MLA_HEADS, MLA_NOPE + MLA_ROPE)
    kv = (rmsnorm(ckv, kv_norm) @ w_ukv).reshape(b, n, MLA_HEADS, MLA_NOPE + MLA_V)
    q_nope, q_pe = q[..., :MLA_NOPE], q[..., MLA_NOPE:]
    k_nope, v_a = kv[..., :MLA_NOPE], kv[..., MLA_NOPE:]
    k_pe = kpe.reshape(b, n, 1, MLA_ROPE)
    q_s = qs.reshape(b, n, SWA_KV_HEADS, SWA_GROUP, SWA_HEAD_DIM)
    k_s = ks.reshape(b, n, SWA_KV_HEADS, SWA_HEAD_DIM)
    v_s = vs.reshape(b, n, SWA_KV_HEADS, SWA_HEAD_DIM)
    if rope_a is not None:
        q_pe = apply_rope(q_pe, rope_a)
        k_pe = apply_rope(k_pe, rope_a)
        q_s = apply_rope(q_s, rope_b)
        k_s = apply_rope(k_s, rope_b)
    q_a = jnp.concatenate([q_nope, q_pe], axis=-1)
    k_a = jnp.concatenate([k_nope, jnp.broadcast_to(k_pe, (b, n, MLA_HEADS, MLA_ROPE))], axis=-1)
    return q_a, k_a, v_a, q_s, k_s, v_s


def attn_mixer(h_ctx, h_lat, rope_a, rope_b, w_in, q_norm, w_uq, kv_norm, w_ukv, sink, w_out, ctx_out):
    qa_c, ka_c, va_c, qs_c, ks_c, vs_c = attn_project(h_ctx, w_in, q_norm, w_uq, kv_norm, w_ukv, None, None)
    qa_l, ka_l, va_l, qs_l, ks_l, vs_l = attn_project(h_lat, w_in, q_norm, w_uq, kv_norm, w_ukv, rope_a, rope_b)
    b, n = h_lat.shape[:2]
    n_ctx = h_ctx.shape[1]
    sink_kg = sink.reshape(SWA_KV_HEADS, SWA_GROUP)
    k_all = jnp.concatenate([ka_c, ka_l], axis=1)
    v_all = jnp.concatenate([va_c, va_l], axis=1)
    o_a = from_blocks(lax.map(lambda qb: softmax_attention(qb, k_all, v_all, MLA_SCALE), to_blocks(qa_l)))
    pad = ((0, 0), (SWA_WINDOW, SWA_WINDOW), (0, 0), (0, 0))
    k_pad = jnp.pad(ks_l, pad)
    v_pad = jnp.pad(vs_l, pad)
    span = Q_BLOCK + 2 * SWA_WINDOW

    def swa_block(args):
        i, qb = args
        start = i * Q_BLOCK
        k_loc = lax.dynamic_slice_in_dim(k_pad, start, span, axis=1)
        v_loc = lax.dynamic_slice_in_dim(v_pad, start, span, axis=1)
        q_pos = start + jnp.arange(Q_BLOCK)
        k_pos = start - SWA_WINDOW + jnp.arange(span)
        band = (jnp.abs(q_pos[:, None] - k_pos[None, :]) <= SWA_WINDOW) & (k_pos >= 0)[None, :] & (k_pos < n)[None, :]
        mask = jnp.concatenate([jnp.ones((Q_BLOCK, n_ctx), dtype=bool), band], axis=1)
        return sink_attention(qb, jnp.concatenate([ks_c, k_loc], axis=1),
                              jnp.concatenate([vs_c, v_loc], axis=1), sink_kg, mask)

    o_b = from_blocks(lax.map(swa_block, (jnp.arange(n // Q_BLOCK), to_blocks(qs_l))))
    y_lat = jnp.concatenate([o_a.reshape(b, n, -1), o_b.reshape(b, n, -1)], axis=-1) @ w_out
    y_ctx = None
    if ctx_out:
        o_a_c = softmax_attention(qa_c, ka_c, va_c, MLA_SCALE)
        o_b_c = sink_attention(qs_c, ks_c, vs_c, sink_kg, None)
        y_ctx = jnp.concatenate([o_a_c.reshape(b, n_ctx, -1), o_b_c.reshape(b, n_ctx, -1)], axis=-1) @ w_out
    return y_ctx, y_lat


def s5_discretise(lam_re, lam_im, b_re, b_im, log_step):
    lam_re = jnp.minimum(lam_re, S5_MAX_RE)
    dt = jnp.exp(log_step)[:, None]
    mag = jnp.exp(lam_re * dt)
    a_re = mag * jnp.cos(lam_im * dt)
    a_im = mag * jnp.sin(lam_im * dt)
    den = lam_re * lam_re + lam_im * lam_im
    f_re = ((a_re - 1.0) * lam_re + a_im * lam_im) / den
    f_im = (a_im * lam_re - (a_re - 1.0) * lam_im) / den
    bb_re = f_re[..., None] * b_re - f_im[..., None] * b_im
    bb_im = f_re[..., None] * b_im + f_im[..., None] * b_re
    return a_re, a_im, bb_re, bb_im


def complex_affine_combine(e1, e2):
    a1r, a1i, b1r, b1i = e1
    a2r, a2i, b2r, b2i = e2
    return (a2r * a1r - a2i * a1i, a2r * a1i + a2i * a1r,
            a2r * b1r - a2i * b1i + b2r, a2r * b1i + a2i * b1r + b2i)


def s5_scan(u, a_re, a_im, bb_re, bb_im, c_re, c_im, h_re, h_im, emit):
    b, n = u.shape[:2]
    chunks = jnp.moveaxis(u.reshape(b, n // S5_CHUNK, S5_CHUNK, S5_GROUPS, S5_GROUP), 1, 0)

    def step(carry, u_c):
        hr, hi = carry
        bu_re = jnp.einsum('blgh,gph->blgp', u_c, bb_re)
        bu_im = jnp.einsum('blgh,gph->blgp', u_c, bb_im)
        ar = jnp.broadcast_to(a_re, bu_re.shape)
        ai = jnp.broadcast_to(a_im, bu_re.shape)
        pa_re, pa_im, s_re, s_im = lax.associative_scan(complex_affine_combine, (ar, ai, bu_re, bu_im), axis=1)
        x_re = s_re + pa_re * hr[:, None] - pa_im * hi[:, None]
        x_im = s_im + pa_re * hi[:, None] + pa_im * hr[:, None]
        new = (x_re[:, -1], x_im[:, -1])
        if not emit:
            return new, None
        y = jnp.einsum('blgp,ghp->blgh', x_re, c_re) - jnp.einsum('blgp,ghp->blgh', x_im, c_im)
        return new, y

    h_end, ys = lax.scan(step, (h_re, h_im), chunks)
    y = jnp.moveaxis(ys, 0, 1).reshape(b, n, S5_WIDTH) if emit else None
    return y, h_end


def s5_mixer(h_ctx, h_lat, w_in, lam_re, lam_im, b_re, b_im, c_re, c_im, log_step, d_skip, w_glu, ctx_out):
    f32 = jnp.float32
    b = h_lat.shape[0]
    u_ctx = (h_ctx @ w_in).astype(f32)
    u_lat = (h_lat @ w_in).astype(f32)
    dsk = d_skip.astype(f32)
    y_lat = u_lat * dsk
    y_ctx = u_ctx * dsk if ctx_out else None
    for dr in range(2):
        a_re, a_im, bb_re, bb_im = s5_discretise(lam_re[dr].astype(f32), lam_im[dr].astype(f32),
                                                 b_re[dr].astype(f32), b_im[dr].astype(f32),
                                                 log_step[dr].astype(f32))
        cr = c_re[dr].astype(f32)
        ci = c_im[dr].astype(f32)
        uc = u_ctx.reshape(b, -1, S5_GROUPS, S5_GROUP)
        ul = u_lat.reshape(b, -1, S5_GROUPS, S5_GROUP)
        if dr == 1:
            uc, ul = uc[:, ::-1], ul[:, ::-1]
        zero = jnp.zeros((b, S5_GROUPS, S5_STATE), f32)
        yc, h_ctx_end = s5_scan(uc, a_re, a_im, bb_re, bb_im, cr, ci, zero, zero, ctx_out)
        yl, _ = s5_scan(ul, a_re, a_im, bb_re, bb_im, cr, ci, h_ctx_end[0], h_ctx_end[1], True)
        if dr == 1:
            yl = yl[:, ::-1]
            yc = yc[:, ::-1] if ctx_out else None
        y_lat = y_lat + yl
        if ctx_out:
            y_ctx = y_ctx + yc

    def glu_out(y):
        a, g = jnp.split(jax.nn.gelu(y).astype(h_lat.dtype) @ w_glu, 2, axis=-1)
        return a * jax.nn.sigmoid(g)

    return (glu_out(y_ctx) if ctx_out else None), glu_out(y_lat)


def setup_inputs(seed: int = 0) -> dict:
    key = jax.random.key(seed)
    keys = iter(jax.random.split(key, 32))
    f32 = jnp.float32

    def nrm(shape, scale):
        return scale * jax.random.normal(next(keys), shape, f32)

    def gain(shape):
        return 1.0 + 0.01 * jax.random.normal(next(keys), shape, f32)

    G, P, H = S5_GROUPS, S5_STATE, S5_GROUP
    inp = {}
    inp['x'] = nrm((BATCH, SEQ, D_MODEL), 1.0)
    inp['c'] = nrm((BATCH, D_MODEL), 1.0)
    inp['ctx'] = nrm((BATCH, CTX_LEN, D_MODEL), 1.0)
    inp['c_ctx'] = nrm((D_MODEL,), 1.0)
    inp['mod_w'] = nrm((DEPTH, D_MODEL, N_MOD * D_MODEL), D_MODEL ** -0.5)
    inp['mod_b'] = nrm((DEPTH, N_MOD * D_MODEL), 0.01)
    inp['norm_pre'] = gain((DEPTH, 3, D_MODEL))
    inp['norm_post'] = gain((DEPTH, 3, D_MODEL))
    inp['ffn_w13'] = nrm((DEPTH, 2, D_MODEL, 2 * D_FF), D_MODEL ** -0.5)
    inp['ffn_w2'] = nrm((DEPTH, 2, D_FF, D_MODEL), D_FF ** -0.5)
    inp['attn_w_in'] = nrm((N_EVEN, D_MODEL, ATTN_IN), D_MODEL ** -0.5)
    inp['mla_q_norm'] = gain((N_EVEN, MLA_Q_LORA))
    inp['mla_w_uq'] = nrm((N_EVEN, MLA_Q_LORA, MLA_HEADS * (MLA_NOPE + MLA_ROPE)), MLA_Q_LORA ** -0.5)
    inp['mla_kv_norm'] = gain((N_EVEN, MLA_KV_LORA))
    inp['mla_w_ukv'] = nrm((N_EVEN, MLA_KV_LORA, MLA_HEADS * (MLA_NOPE + MLA_V)), MLA_KV_LORA ** -0.5)
    inp['swa_sink'] = nrm((N_EVEN, SWA_HEADS), 0.5)
    inp['attn_w_out'] = nrm((N_EVEN, ATTN_OUT, D_MODEL), ATTN_OUT ** -0.5)
    inp['s5_w_in'] = nrm((N_ODD, D_MODEL, S5_WIDTH), D_MODEL ** -0.5)
    inp['s5_lambda_re'] = -0.5 + nrm((N_ODD, 2, G, P), 0.01)
    inp['s5_lambda_im'] = np.pi * jnp.arange(P, dtype=f32) + nrm((N_ODD, 2, G, P), 0.01)
    inp['s5_b_re'] = nrm((N_ODD, 2, G, P, H), (2 * H) ** -0.5)
    inp['s5_b_im'] = nrm((N_ODD, 2, G, P, H), (2 * H) ** -0.5)
    inp['s5_c_re'] = nrm((N_ODD, 2, G, H, P), (2 * P) ** -0.5)
    inp['s5_c_im'] = nrm((N_ODD, 2, G, H, P), (2 * P) ** -0.5)
    inp['s5_log_step'] = jax.random.uniform(next(keys), (N_ODD, 2, G), f32, math.log(1e-3), math.log(1e-1))
    inp['s5_d'] = nrm((N_ODD, S5_WIDTH), 1.0)
    inp['s5_w_glu'] = nrm((N_ODD, S5_WIDTH, 2 * D_MODEL), S5_WIDTH ** -0.5)
    return inp


def reference(x, c, ctx, c_ctx, mod_w, mod_b, norm_pre, norm_post, ffn_w13, ffn_w2,
              attn_w_in, mla_q_norm, mla_w_uq, mla_kv_norm, mla_w_ukv, swa_sink, attn_w_out,
              s5_w_in, s5_lambda_re, s5_lambda_im, s5_b_re, s5_b_im, s5_c_re, s5_c_im,
              s5_log_step, s5_d, s5_w_glu):
    b, n = x.shape[:2]
    ROWS = n // GRID_W
    rows = jnp.repeat(jnp.arange(ROWS), GRID_W)
    cols = jnp.tile(jnp.arange(GRID_W), ROWS)
    rope_a = axial_rope_table(rows, cols, MLA_ROPE)
    rope_b = axial_rope_table(rows, cols, SWA_HEAD_DIM)
    h_lat, h_ctx = x, ctx
    for l in range(DEPTH):
        last = l == DEPTH - 1
        m_lat = jnp.moveaxis((jax.nn.silu(c) @ mod_w[l] + mod_b[l]).reshape(b, N_MOD, 1, D_MODEL), 1, 0)
        m_ctx = (jax.nn.silu(c_ctx) @ mod_w[l] + mod_b[l]).reshape(N_MOD, 1, 1, D_MODEL)
        h_lat = ffn_sublayer(h_lat, m_lat, 0, norm_pre[l, 0], norm_post[l, 0], ffn_w13[l, 0], ffn_w2[l, 0])
        h_ctx = ffn_sublayer(h_ctx, m_ctx, 0, norm_pre[l, 0], norm_post[l, 0], ffn_w13[l, 0], ffn_w2[l, 0])
        a_lat = modulate(h_lat, norm_pre[l, 1], m_lat[3], m_lat[4])
        a_ctx = modulate(h_ctx, norm_pre[l, 1], m_ctx[3], m_ctx[4])
        if l % 2 == 0:
            e = l // 2
            y_ctx, y_lat = attn_mixer(a_ctx, a_lat, rope_a, rope_b, attn_w_in[e], mla_q_norm[e], mla_w_uq[e],
                                      mla_kv_norm[e], mla_w_ukv[e], swa_sink[e], attn_w_out[e], not last)
        else:
            o = l // 2
            y_ctx, y_lat = s5_mixer(a_ctx, a_lat, s5_w_in[o], s5_lambda_re[o], s5_lambda_im[o], s5_b_re[o],
                                    s5_b_im[o], s5_c_re[o], s5_c_im[o], s5_log_step[o], s5_d[o], s5_w_glu[o],
                                    not last)
        h_lat = h_lat + m_lat[5] * rmsnorm(y_lat, norm_post[l, 1])
        h_lat = ffn_sublayer(h_lat, m_lat, 2, norm_pre[l, 2], norm_post[l, 2], ffn_w13[l, 1], ffn_w2[l, 1])
        if not last:
            h_ctx = h_ctx + m_ctx[5] * rmsnorm(y_ctx, norm_post[l, 1])
            h_ctx = ffn_sublayer(h_ctx, m_ctx, 2, norm_pre[l, 2], norm_post[l, 2], ffn_w13[l, 1], ffn_w2[l, 1])
    return h_lat
```

```python
import numpy as np
import concourse.bass as bass
import concourse.mybir as mybir
from concourse.bass_utils import run_bass_kernel_spmd
from contextlib import ExitStack

F32 = mybir.dt.float32
BF16 = mybir.dt.bfloat16
AF = mybir.ActivationFunctionType
ALU = mybir.AluOpType
AX = mybir.AxisListType
ENGS = ['pe', 'act', 'dve', 'pool', 'sp']

D = 1024
KT = 8
DFF = 2816
JT = 22
NCTX = 256
EPS = 1e-6


class Buf:
    _serial = [0]

    def __init__(self, name, t=None):
        Buf._serial[0] += 1
        self.uid = Buf._serial[0]
        self.name = name
        self.t = t
        self.w = None
        self.r = []
        self.dsem = None
        self.dcnt = 0

    def __getitem__(self, idx):
        return self.t[idx]


class Prog:
    def __init__(self, nc):
        self.nc = nc
        self.es = ExitStack()
        self.sem = {e: self.es.enter_context(nc.semaphore("s_" + e)) for e in ENGS}
        self.cnt = {e: 0 for e in ENGS}
        self.ops = {e: [] for e in ENGS}
        self.waited = {e: {} for e in ENGS}
        self.dbufs = []
        self.scopes = []
        self.nm = 0
        self.dsem_pool = []

    def _stack(self):
        return self.scopes[-1] if self.scopes else self.es

    def sb(self, name, shape, dt):
        self.nm += 1
        t = self._stack().enter_context(self.nc.sbuf_tensor("%s_%d" % (name, self.nm), list(shape), dt))
        return Buf(name, t)

    def ps(self, name, shape=(128, 512), dt=F32):
        self.nm += 1
        t = self._stack().enter_context(self.nc.psum_tensor("%s_%d" % (name, self.nm), list(shape), dt))
        return Buf(name, t)

    def view(self, buf, t):
        return Buf(buf.name + "_v", t)

    def views(self, buf, k):
        return [Buf("%s_%d" % (buf.name, i), buf.t[:, i, :]) for i in range(k)]

    def open_scope(self):
        self.scopes.append(ExitStack())

    def close_scope(self):
        self.flush()
        depth = len(self.scopes)
        keep = []
        for b in self.dbufs:
            if getattr(b, 'scope_depth', 0) >= depth:
                self.dsem_pool.append((b.dsem, b.dcnt))
                b.dsem = None
            else:
                keep.append(b)
        self.dbufs = keep
        self.scopes.pop().close()

    def _need(self, eng, dep, waits):
        if dep is None:
            return
        kind, key, val = dep
        if kind == 'e' and key == 'pe' and eng == 'pe':
            return
        k = (kind, key if kind == 'e' else key.uid)
        if self.waited[eng].get(k, 0) >= val:
            return
        self.waited[eng][k] = val
        sem = self.sem[key] if kind == 'e' else key.dsem
        waits.append((sem, val))

    def _deps(self, eng, reads, writes):
        waits = []
        for b in reads:
            self._need(eng, b.w, waits)
        for b in writes:
            self._need(eng, b.w, waits)
            for r in b.r:
                self._need(eng, r, waits)
        return waits

    def op(self, eng, fn, reads=(), writes=(), inc=True):
        waits = self._deps(eng, reads, writes)
        if inc:
            self.cnt[eng] += 1
            tick = ('e', eng, self.cnt[eng])
        else:
            tick = ('e', eng, self.cnt[eng] + 1)
        self.ops[eng].append((waits, fn, (self.sem[eng], 1) if inc else None))
        for b in reads:
            b.r.append(tick)
        for b in writes:
            b.w = tick
            b.r = []
        return tick

    def dma(self, eng, fn, reads=(), writes=(), owner=None):
        waits = self._deps(eng, reads, writes)
        if owner is None:
            owner = (list(writes) + list(reads))[0]
        if owner.dsem is None:
            if self.dsem_pool:
                owner.dsem, owner.dcnt = self.dsem_pool.pop()
            else:
                self.nsem = getattr(self, 'nsem', 0) + 1
                owner.dsem = self.es.enter_context(self.nc.semaphore("d%d" % self.nsem))
            self.dbufs.append(owner)
            owner.scope_depth = len(self.scopes)
        owner.dcnt += 16
        tick = ('d', owner, owner.dcnt)
        self.ops[eng].append((waits, fn, (owner.dsem, 16)))
        for b in reads:
            b.r.append(tick)
        for b in writes:
            b.w = tick
            b.r = []
        return tick

    def barrier(self):
        for e in ENGS:
            waits = []
            for e2 in ENGS:
                if e2 != e and self.cnt[e2] > 0:
                    self._need(e, ('e', e2, self.cnt[e2]), waits)
            for b in self.dbufs:
                self._need(e, ('d', b, b.dcnt), waits)
            if waits:
                self.ops[e].append((waits, None, None))

    def sync_all(self):
        self.barrier()

    def flush(self):
        nc = self.nc
        self.barrier()
        ops = self.ops
        self.ops = {e: [] for e in ENGS}
        self.simulate(ops)

        def run(lst, e):
            for waits, fn, inc in lst:
                for sem, val in waits:
                    e.wait_ge(sem, val)
                if fn is not None:
                    ins = fn(e)
                    if inc is not None:
                        ins.then_inc(inc[0], inc[1])

        with nc.Block() as block:
            @block.tensor
            def _(e):
                run(ops['pe'], e)

            @block.scalar
            def _(e):
                run(ops['act'], e)

            @block.vector
            def _(e):
                run(ops['dve'], e)

            @block.gpsimd
            def _(e):
                run(ops['pool'], e)

            @block.sync
            def _(e):
                run(ops['sp'], e)

    def simulate(self, ops):
        if not hasattr(self, 'simv'):
            self.simv = {}
        ptr = {e: 0 for e in ENGS}
        while True:
            prog = False
            for e in ENGS:
                while ptr[e] < len(ops[e]):
                    waits, fn, inc = ops[e][ptr[e]]
                    if all(self.simv.get(id(s), 0) >= v for s, v in waits):
                        if inc is not None:
                            self.simv[id(inc[0])] = self.simv.get(id(inc[0]), 0) + inc[1]
                        ptr[e] += 1
                        prog = True
                    else:
                        break
            if all(ptr[e] == len(ops[e]) for e in ENGS):
                return
            if not prog:
                for e in ENGS:
                    if ptr[e] < len(ops[e]):
                        waits, fn, inc = ops[e][ptr[e]]
                        print("DEADLOCK", e, ptr[e], [(s, v, self.simv.get(id(s), 0)) for s, v in waits])
                raise RuntimeError("deadlock in sync plan")

    def finish(self):
        self.flush()
        self.es.close()


def mm(p, out, out_ap, lhs, lhs_ap, rhs, rhs_ap, start, stop):
    p.op('pe', lambda e: e.matmul(out_ap, lhs_ap, rhs_ap, start=start, stop=stop),
         reads=[lhs, rhs], writes=[out], inc=stop)


_rr = [0]


def cast_any(p, out, out_ap, in_, in_ap):
    i = _rr[0] % 3
    _rr[0] += 1
    if i == 0:
        p.op('dve', lambda e: e.tensor_copy(out_ap, in_ap), reads=[in_], writes=[out])
    elif i == 1:
        p.op('pool', lambda e: e.tensor_copy(out_ap, in_ap), reads=[in_], writes=[out])
    else:
        p.op('act', lambda e: e.copy(out_ap, in_ap), reads=[in_], writes=[out])


def prep_weight(p, nc, name, W, K, groups):
    ktin = K // 128
    wtot = sum(g[1] for g in groups[0])
    G = len(groups)
    scr = nc.dram_tensor(name, [G, 128, ktin * wtot], BF16).ap()
    Wv = W.rearrange("(kt p) n -> p kt n", p=128)
    p.open_scope()
    f = [p.sb("pwf", [128, ktin, wtot], F32) for _ in range(2)]
    b = [p.sb("pwb", [128, ktin, wtot], BF16) for _ in range(2)]
    for g, grp in enumerate(groups):
        ft = f[g % 2]
        bt = b[g % 2]
        off = 0
        negs = []
        for it in grp:
            c0, w = it[0], it[1]
            sgn = it[2] if len(it) > 2 else 1
            p.dma('sp', lambda e, ft=ft, off=off, c0=c0, w=w: e.dma_start(out=ft[:, :, off:off + w], in_=Wv[:, :, c0:c0 + w]),
                  writes=[ft])
            if sgn < 0:
                negs.append((off, w))
            off += w
        for (o2, w2) in negs:
            p.op('dve', lambda e, ft=ft, o2=o2, w2=w2: e.tensor_scalar(ft[:, :, o2:o2 + w2], ft[:, :, o2:o2 + w2], -1.0, None, ALU.mult),
                 reads=[ft], writes=[ft])
        cast_any(p, bt, bt[:], ft, ft[:])
        p.dma('pool', lambda e, bt=bt, g=g: e.dma_start(out=scr[g], in_=bt[:].rearrange("p a b -> p (a b)")), reads=[bt])
    p.close_scope()
    return scr


def load_cols(p, dst, dst_ap, src_ap, R, ident, ps, stage):
    p.dma('sp', lambda e: e.dma_start(out=stage[0:R, :], in_=src_ap), writes=[stage])
    mm(p, ps, ps[:, 0:R], stage, stage[0:R, :], ident, ident[0:R, 0:R], True, True)
    p.op('dve', lambda e: e.tensor_copy(dst_ap, ps[:, 0:R]), reads=[ps], writes=[dst])


def build_consts(p, nc, ident_d, shift_d=None, mprev_d=None, mnext_d=None):
    c = {}
    if shift_d is not None:
        c['shiftI'] = p.sb("shiftI", [128, 64], F32)
        p.dma('sp', lambda e: e.dma_start(out=c['shiftI'][:], in_=shift_d), writes=[c['shiftI']])
        c['mprev'] = p.sb("mprev", [128, 128], F32)
        p.dma('sp', lambda e: e.dma_start(out=c['mprev'][:], in_=mprev_d), writes=[c['mprev']])
        c['mnext'] = p.sb("mnext", [128, 128], F32)
        p.dma('sp', lambda e: e.dma_start(out=c['mnext'][:], in_=mnext_d), writes=[c['mnext']])
    c['ident'] = p.sb("ident", [128, 128], F32)
    p.dma('sp', lambda e: e.dma_start(out=c['ident'][:], in_=ident_d), writes=[c['ident']])
    c['ones_bf'] = p.sb("ones_bf", [128, 128], BF16)
    p.op('dve', lambda e: e.memset(c['ones_bf'][:], 1.0), writes=[c['ones_bf']])
    c['eps'] = p.sb("epsc", [128, 1], F32)
    p.op('dve', lambda e: e.memset(c['eps'][:], EPS), writes=[c['eps']])
    return c


def mod_phase(p, nc, C, cvec_d, modw_d, modb_d, npre_d, npost_d, L):
    CO = [p.sb("CO%d" % l, [128, 3, 3, 2, 8], F32) for l in range(L)]
    p.open_scope()
    ident = C['ident']
    stage = p.sb("stage", [128, 128], F32)
    pst = p.ps("pst")
    cT = p.sb("cT", [128, 16], F32)
    load_cols(p, cT, cT[:], cvec_d, 16, ident, pst, stage)
    sc = p.sb("sc", [128, 8, 2], F32)
    for kind in range(2):
        p.op('act', lambda e, kind=kind: e.activation(sc[:, :, kind], cT[:, kind * 8:(kind + 1) * 8], AF.Silu),
             reads=[cT], writes=[sc])
    wbuf = [p.sb("mwb", [128, 8, 1024], F32) for _ in range(2)]
    psM = p.ps("psM")
    M = p.sb("M", [128, 72, 2], F32)
    mb = p.sb("mb", [128, 72], F32)
    gpre = p.sb("gpre", [128, 24], F32)
    gpost = p.sb("gpost", [128, 24], F32)
    t8 = p.sb("t8", [128, 8], F32)
    for l in range(L):
        mwv = modw_d[l].rearrange("(kt p) n -> p kt n", p=128)
        for j in range(9):
            wb = wbuf[(l * 9 + j) % 2]
            p.dma('sp', lambda e, wb=wb, j=j, mwv=mwv: e.dma_start(out=wb[:], in_=mwv[:, :, j * 1024:(j + 1) * 1024]), writes=[wb])
            for q in range(8):
                col = (j * 8 + q) * 2
                for kt in range(8):
                    mm(p, psM, psM[:, col:col + 2], wb, wb[:, kt, q * 128:(q + 1) * 128], sc, sc[:, kt, :], kt == 0, kt == 7)
        load_cols(p, mb, mb[:], modb_d[l], 72, ident, pst, stage)
        load_cols(p, gpre, gpre[:], npre_d[l], 24, ident, pst, stage)
        load_cols(p, gpost, gpost[:], npost_d[l], 24, ident, pst, stage)
        for kind in range(2):
            p.op('dve', lambda e, kind=kind: e.tensor_tensor(M[:, :, kind], psM[:, 0:144].rearrange("p (q k) -> p q k", k=2)[:, :, kind], mb[:], ALU.add),
                 reads=[psM, mb], writes=[M])
        co = CO[l]
        for s in range(3):
            coef = 1.0 if s == 1 else 0.5
            for kind in range(2):
                p.op('dve', lambda e, s=s, kind=kind: e.tensor_scalar(t8[:], M[:, (3 * s + 1) * 8:(3 * s + 2) * 8, kind], 1.0, None, ALU.add),
                     reads=[M], writes=[t8])
                p.op('dve', lambda e, s=s, kind=kind, co=co: e.tensor_tensor(co[:, s, 0, kind, :], t8[:], gpre[:, s * 8:(s + 1) * 8], ALU.mult),
                     reads=[t8, gpre], writes=[co])
                p.op('dve', lambda e, s=s, kind=kind, co=co: e.tensor_copy(co[:, s, 1, kind, :], M[:, (3 * s) * 8:(3 * s + 1) * 8, kind]),
                     reads=[M], writes=[co])
                p.op('dve', lambda e, s=s, kind=kind, coef=coef: e.tensor_scalar(t8[:], M[:, (3 * s + 2) * 8:(3 * s + 3) * 8, kind], coef, None, ALU.mult),
                     reads=[M], writes=[t8])
                p.op('dve', lambda e, s=s, kind=kind, co=co: e.tensor_tensor(co[:, s, 2, kind, :], t8[:], gpost[:, s * 8:(s + 1) * 8], ALU.mult),
                     reads=[t8, gpost], writes=[co])
    p.close_scope()
    return CO


def norm_stats(p, C, sq, nk, n, pstat, rstd, inv_dim):
    for kt in range(nk):
        mm(p, pstat, pstat[:, :n], C['ones_bf'], C['ones_bf'][:], sq[kt], sq[kt][:, :n], kt == 0, kt == nk - 1)
    p.op('act', lambda e: e.activation(rstd[:, :n], pstat[:, :n], AF.Sqrt, bias=C['eps'][:, 0:1], scale=inv_dim),
         reads=[pstat, C['eps']], writes=[rstd])
    p.op('dve', lambda e: e.reciprocal(rstd[:, :n], rstd[:, :n]), reads=[rstd], writes=[rstd])


def modulate(p, C, hT, n, co, s, kind, sq, pstat, rstd, tmps, aT):
    for kt in range(KT):
        p.op('act', lambda e, kt=kt: e.activation(sq[kt][:, :n], hT[kt][:, :n], AF.Square), reads=[hT[kt]], writes=[sq[kt]])
    norm_stats(p, C, sq, KT, n, pstat, rstd, 1.0 / D)
    for kt in range(KT):
        tmp = tmps[kt % len(tmps)]
        p.op('dve', lambda e, kt=kt, tmp=tmp: e.tensor_tensor(tmp[:, :n], hT[kt][:, :n], rstd[:, :n], ALU.mult),
             reads=[hT[kt], rstd], writes=[tmp])
        p.op('pool', lambda e, kt=kt, tmp=tmp: e.tensor_scalar(aT[kt][:, :n], tmp[:, :n], co[:, s, 0, kind, kt:kt + 1], co[:, s, 1, kind, kt:kt + 1], ALU.mult, ALU.add),
             reads=[tmp, co], writes=[aT[kt]])


def post_residual(p, C, hT, ysb, n, co, s, kind, sq, pstat, rstd, tmps):
    norm_stats(p, C, sq, KT, n, pstat, rstd, 1.0 / D)
    for kt in range(KT):
        tmp = tmps[kt % len(tmps)]
        p.op('pool', lambda e, kt=kt, tmp=tmp: e.tensor_tensor(tmp[:, :n], ysb[kt][:, :n], rstd[:, :n], ALU.mult),
             reads=[ysb[kt], rstd], writes=[tmp])
        p.op('dve', lambda e, kt=kt, tmp=tmp: e.scalar_tensor_tensor(hT[kt][:, :n], tmp[:, :n], co[:, s, 2, kind, kt:kt + 1], hT[kt][:, :n], ALU.mult, ALU.add),
             reads=[tmp, co, hT[kt]], writes=[hT[kt]])


def ffn_phase(p, nc, C, h_in, h_out, w13s, w2s, co, s, chunks, out_off=0):
    p.open_scope()
    NT = 512
    hTf = [p.sb("hT", [128, KT, NT], F32) for _ in range(2)]
    hTs = [p.views(b, KT) for b in hTf]
    aTs = [p.views(p.sb("aT", [128, KT, NT], BF16), KT) for _ in range(2)]
    sq = p.views(p.sb("sq", [128, KT, NT], BF16), KT)
    rstd = p.sb("rstd", [128, NT], F32)
    rstd2 = p.sb("rstd2", [128, NT], F32)
    tmps = [p.sb("tmp", [128, NT], F32) for _ in range(3)]
    sgs = [p.sb("sg", [128, NT], F32) for _ in range(2)]
    HT = p.sb("HT", [128, JT, NT], BF16)
    HTj = [p.view(HT, HT.t[:, j, :]) for j in range(JT)]
    ysb = p.views(p.sb("ysb", [128, KT, NT], F32), KT)
    w13b = [p.sb("w13b", [128, KT, 256], BF16) for _ in range(3)]
    w2b = [p.sb("w2b", [128, JT, 128], BF16) for _ in range(2)]
    pstat = p.ps("pstat")
    pg = [p.ps("pg") for _ in range(2)]
    pu = [p.ps("pu") for _ in range(2)]
    py = [p.ps("py") for _ in range(2)]
    hin_v = h_in.rearrange("(kt p) t -> p kt t", p=128)
    hout_v = h_out.rearrange("(kt p) t -> p kt t", p=128)

    def pre(ci):
        t0, n, kind = chunks[ci]
        hT = hTs[ci % 2]
        hf = hTf[ci % 2]
        p.dma('sp', lambda e: e.dma_start(out=hf[:, :, :n], in_=hin_v[:, :, t0:t0 + n]), writes=hT)
        modulate(p, C, hT, n, co, s, kind, sq, pstat, rstd, tmps, aTs[ci % 2])

    wcnt = [0, 0]
    pre(0)
    for ci, (t0, n, kind) in enumerate(chunks):
        hT = hTs[ci % 2]
        aT = aTs[ci % 2]
        for j in range(JT):
            wb = w13b[wcnt[0] % 3]
            wcnt[0] += 1
            p.dma('sp', lambda e, wb=wb, j=j: e.dma_start(out=wb[:].rearrange("p a b -> p (a b)"), in_=w13s[j]), writes=[wb])
            g = pg[j % 2]
            u = pu[j % 2]
            for kt in range(KT):
                mm(p, g, g[:, :n], wb, wb[:, kt, 0:128], aT[kt], aT[kt][:, :n], kt == 0, kt == KT - 1)
            for kt in range(KT):
                mm(p, u, u[:, :n], wb, wb[:, kt, 128:256], aT[kt], aT[kt][:, :n], kt == 0, kt == KT - 1)
            sg = sgs[j % 2]
            p.op('act', lambda e, g=g, sg=sg: e.activation(sg[:, :n], g[:, :n], AF.Silu), reads=[g], writes=[sg])
            p.op('dve', lambda e, u=u, sg=sg, j=j: e.tensor_tensor(HT[:, j, :n], sg[:, :n], u[:, :n], ALU.mult),
                 reads=[sg, u], writes=[HTj[j]])
        if ci + 1 < len(chunks):
            pre(ci + 1)
        for i in range(KT):
            wb = w2b[wcnt[1] % 2]
            wcnt[1] += 1
            p.dma('sp', lambda e, wb=wb, i=i: e.dma_start(out=wb[:].rearrange("p a b -> p (a b)"), in_=w2s[i]), writes=[wb])
            y = py[i % 2]
            for j in range(JT):
                mm(p, y, y[:, :n], wb, wb[:, j, :], HTj[j], HT[:, j, :n], j == 0, j == JT - 1)
            p.op('dve', lambda e, y=y, i=i: e.tensor_copy(ysb[i][:, :n], y[:, :n]), reads=[y], writes=[ysb[i]])
            p.op('act', lambda e, i=i: e.activation(sq[i][:, :n], ysb[i][:, :n], AF.Square), reads=[ysb[i]], writes=[sq[i]])
        post_residual(p, C, hT, ysb, n, co, s, kind, sq, pstat, rstd2, tmps)
        hf = hTf[ci % 2]
        p.dma('pool', lambda e, hf=hf, t0=t0, n=n: e.dma_start(out=hout_v[:, :, t0 - out_off:t0 - out_off + n], in_=hf[:, :, :n]), reads=hT)
    p.close_scope()


MLA_SCALE = 96 ** -0.5
SWA_SCALE = 64 ** -0.5
W_CQ, W_CKV, W_KPE, W_QS, W_KS, W_VS = 0, 256, 384, 416, 928, 1056


def attn_weight_groups():
    g = []
    g.append([(0, 128)])
    g.append([(128, 128)])
    g.append([(256, 128)])
    g.append([(320, 96), (0, 32)])
    g.append([(320, 64), (400, 16, -1), (384, 16), (0, 32)])
    for i in range(4):
        g.append([(W_QS + 128 * i, 128)])
    for i in range(4):
        b = W_QS + 128 * i
        g.append([(b + 32, 32, -1), (b, 32), (b + 96, 32, -1), (b + 64, 32)])
    for kv in range(2):
        b = W_KS + 64 * kv
        g.append([(b, 64), (b, 64)])
    for kv in range(2):
        b = W_KS + 64 * kv
        g.append([(b + 32, 32, -1), (b, 32), (b + 32, 32, -1), (b, 32)])
    g.append([(W_VS, 128)])
    return g


def wuq_groups():
    g = []
    for h in range(8):
        g.append([(96 * h, 96), (0, 32)])
    for h in range(8):
        g.append([(96 * h, 64), (96 * h + 80, 16, -1), (96 * h + 64, 16), (0, 32)])
    return g


def attn_proj_phase(p, nc, C, h_in, co, chunks, T, win_s, wuq_s, wk_s, wv_s, qn_d, kvn_d, ropeA, ropeB, scr):
    p.open_scope()
    NT = 512
    ident = C['ident']
    hTf = [p.sb("hT", [128, KT, NT], F32) for _ in range(2)]
    hTs = [p.views(b, KT) for b in hTf]
    aT = p.views(p.sb("aT", [128, KT, NT], BF16), KT)
    sq = p.views(p.sb("sq", [128, KT, NT], BF16), KT)
    rstd = p.sb("rstd", [128, NT], F32)
    rq = p.sb("rq", [128, NT], F32)
    rkv = p.sb("rkv", [128, NT], F32)
    tmps = [p.sb("tmp", [128, NT], F32) for _ in range(3)]
    pstat = p.ps("pstat")
    pa = [p.ps("pa") for _ in range(3)]
    pb = [p.ps("pb") for _ in range(3)]
    win = p.sb("win", [128, 18, KT, 128], BF16)
    p.dma('sp', lambda e: e.dma_start(out=win[:].rearrange("p g a b -> p g (a b)"), in_=win_s.rearrange("g p x -> p g x")), writes=[win])
    wuq = p.sb("wuq", [128, 16, 2, 128], BF16)
    p.dma('sp', lambda e: e.dma_start(out=wuq[:].rearrange("p g a b -> p g (a b)"), in_=wuq_s.rearrange("g p x -> p g x")), writes=[wuq])
    wk = p.sb("wk", [128, 512], BF16)
    p.dma('sp', lambda e: e.dma_start(out=wk[:], in_=wk_s[0]), writes=[wk])
    wv = p.sb("wv", [128, 512], BF16)
    p.dma('sp', lambda e: e.dma_start(out=wv[:], in_=wv_s[0]), writes=[wv])
    stage = p.sb("stage", [128, 128], F32)
    qn = p.sb("qn", [128, 2], F32)
    kvn = p.sb("kvn", [128, 1], F32)
    load_cols(p, qn, qn[:], qn_d, 2, ident, pstat, stage)
    load_cols(p, kvn, kvn[:], kvn_d, 1, ident, pstat, stage)
    cqn = p.views(p.sb("cqn", [128, 2, NT], BF16), 2)
    ckvn = p.sb("ckvn", [128, NT], BF16)
    tA = p.sb("tA", [128, 2, NT], F32)
    tB = p.sb("tB", [128, 2, NT], F32)
    qsb = p.sb("qsb", [128, NT], F32)
    r1 = p.sb("r1", [128, NT], F32)
    r2 = p.sb("r2", [128, NT], F32)
    r3 = p.sb("r3", [128, NT], F32)
    Qst = [p.sb("Qst", [96, NT], BF16) for _ in range(2)]
    Kst = p.sb("Kst", [96, 8, NT], BF16)
    kpe = p.sb("kpe", [96, NT], BF16)
    Vst = p.sb("Vst", [128, 4, 8, 128], BF16)
    VSst = p.sb("VSst", [128, 4, 2, 128], BF16)
    p.op('pool', lambda e: e.memset(Vst[:], 1.0), writes=[Vst])
    p.op('pool', lambda e: e.memset(VSst[:], 1.0), writes=[VSst])
    Sst = [p.sb("Sst", [128, NT], BF16) for _ in range(2)]
    hin_v = h_in.rearrange("(kt p) t -> p kt t", p=128)

    def proj(dst_ps, gidx, n):
        for kt in range(KT):
            mm(p, dst_ps, dst_ps[:, :n], win, win[:, gidx, kt, :], aT[kt], aT[kt][:, :n], kt == 0, kt == KT - 1)

    def rope_rows(ps_x, ps_r, tab, lo, hi, n, out_buf, out_ap, scale):
        p.op('dve', lambda e: e.tensor_tensor(r1[lo:hi, :n], ps_x[lo:hi, :n], tab[lo:hi, 0, :n], ALU.mult), reads=[ps_x, tab], writes=[r1])
        p.op('dve', lambda e: e.tensor_tensor(r2[lo:hi, :n], ps_r[lo:hi, :n], tab[lo:hi, 1, :n], ALU.mult), reads=[ps_r, tab], writes=[r2])
        p.op('pool', lambda e: e.tensor_tensor(r3[lo:hi, :n], r1[lo:hi, :n], r2[lo:hi, :n], ALU.add), reads=[r1, r2], writes=[r3])
        p.op('act', lambda e: e.activation(out_ap, r3[lo:hi, :n], AF.Copy, scale=scale), reads=[r3], writes=[out_buf])

    for ci, (t0, n, kind) in enumerate(chunks):
        hT = hTs[ci % 2]
        hf = hTf[ci % 2]
        p.dma('sp', lambda e, hf=hf, t0=t0, n=n: e.dma_start(out=hf[:, :, :n], in_=hin_v[:, :, t0:t0 + n]), writes=hT)
        p.dma('sp', lambda e, t0=t0, n=n: e.dma_start(out=tA[64:96, :, :n], in_=ropeA[:, :, t0:t0 + n]), writes=[tA])
        p.dma('sp', lambda e, t0=t0, n=n: e.dma_start(out=tB[:, :, :n], in_=ropeB[:, :, t0:t0 + n]), writes=[tB])
        modulate(p, C, hT, n, co, 1, kind, sq, pstat, rstd, tmps, aT)
        for i in range(2):
            proj(pa[i], i, n)
            p.op('dve', lambda e, i=i: e.tensor_copy(tmps[i][:, :n], pa[i][:, :n]), reads=[pa[i]], writes=[tmps[i]])
            p.op('act', lambda e, i=i: e.activation(sq[i][:, :n], tmps[i][:, :n], AF.Square), reads=[tmps[i]], writes=[sq[i]])
        norm_stats(p, C, sq, 2, n, pstat, rq, 1.0 / 256)
        for i in range(2):
            p.op('dve', lambda e, i=i: e.tensor_tensor(tmps[i][:, :n], tmps[i][:, :n], rq[:, :n], ALU.mult), reads=[tmps[i], rq], writes=[tmps[i]])
            p.op('pool', lambda e, i=i: e.tensor_scalar(cqn[i][:, :n], tmps[i][:, :n], qn[:, i:i + 1], None, ALU.mult), reads=[tmps[i], qn], writes=[cqn[i]])
        proj(pa[2], 2, n)
        p.op('dve', lambda e: e.tensor_copy(tmps[2][:, :n], pa[2][:, :n]), reads=[pa[2]], writes=[tmps[2]])
        p.op('act', lambda e: e.activation(sq[2][:, :n], tmps[2][:, :n], AF.Square), reads=[tmps[2]], writes=[sq[2]])
        norm_stats(p, C, sq[2:3], 1, n, pstat, rkv, 1.0 / 128)
        p.op('dve', lambda e: e.tensor_tensor(tmps[2][:, :n], tmps[2][:, :n], rkv[:, :n], ALU.mult), reads=[tmps[2], rkv], writes=[tmps[2]])
        p.op('pool', lambda e: e.tensor_scalar(ckvn[:, :n], tmps[2][:, :n], kvn[:, 0:1], None, ALU.mult), reads=[tmps[2], kvn], writes=[ckvn])
        proj(pa[0], 3, n)
        proj(pb[0], 4, n)
        rope_rows(pa[0], pb[0], tA, 64, 96, n, kpe, kpe[64:96, :n], 1.0)
        for h in range(8):
            pk = pa[1 + h % 2]
            mm(p, pk, pk[0:64, :n], wk, wk[:, 64 * h:64 * h + 64], ckvn, ckvn[:, :n], True, True)
            p.op('dve', lambda e, pk=pk, h=h: e.tensor_copy(Kst[0:64, h, :n], pk[0:64, :n]), reads=[pk], writes=[Kst])
            p.op('pool', lambda e, h=h: e.tensor_copy(Kst[64:96, h, :n], kpe[64:96, :n]), reads=[kpe], writes=[Kst])
        p.dma('pool', lambda e, t0=t0, n=n: e.dma_start(out=scr['KT'].rearrange("h r t -> r h t")[:, :, t0:t0 + n], in_=Kst[:, :, :n]), reads=[Kst])
        for h in range(8):
            pq = pa[h % 2]
            pqr = pb[h % 2]
            Q = Qst[h % 2]
            for kt in range(2):
                mm(p, pq, pq[:, :n], wuq, wuq[:, h, kt, :], cqn[kt], cqn[kt][:, :n], kt == 0, kt == 1)
            for kt in range(2):
                mm(p, pqr, pqr[:, :n], wuq, wuq[:, 8 + h, kt, :], cqn[kt], cqn[kt][:, :n], kt == 0, kt == 1)
            p.op('dve', lambda e, pq=pq: e.tensor_copy(qsb[0:64, :n], pq[0:64, :n]), reads=[pq], writes=[qsb])
            rope_rows(pq, pqr, tA, 64, 96, n, Q, Q[64:96, :n], MLA_SCALE)
            p.op('act', lambda e, Q=Q: e.activation(Q[0:64, :n], qsb[0:64, :n], AF.Copy, scale=MLA_SCALE), reads=[qsb], writes=[Q])
            p.dma('pool', lambda e, Q=Q, h=h, t0=t0, n=n: e.dma_start(out=scr['QT'][h, :, t0:t0 + n], in_=Q[:, :n]), reads=[Q])
        nb = n // 128
        for tb in range(nb):
            pv = pa[tb % 2]
            mm(p, pv, pv[:, 0:512], ckvn, ckvn[:, tb * 128:(tb + 1) * 128], wv, wv[:, :], True, True)
            p.op('dve', lambda e, pv=pv, tb=tb: e.tensor_copy(Vst[:, tb, :, 0:64], pv[:, 0:512].rearrange("p (h d) -> p h d", d=64)), reads=[pv], writes=[Vst])
        p.dma('pool', lambda e, t0=t0, nb=nb: e.dma_start(out=scr['V'][t0 // 128:t0 // 128 + nb].rearrange("b p h d -> p b h d"), in_=Vst[:, 0:nb]), reads=[Vst])
        for i in range(4):
            proj(pa[i % 2], 5 + i, n)
            proj(pb[i % 2], 9 + i, n)
            S = Sst[i % 2]
            rope_rows(pa[i % 2], pb[i % 2], tB, 0, 128, n, S, S[:, :n], SWA_SCALE)
            p.dma('pool', lambda e, S=S, i=i, t0=t0, n=n: e.dma_start(out=scr['QS'][i, :, t0:t0 + n], in_=S[:, :n]), reads=[S])
        for kv in range(2):
            proj(pa[kv], 13 + kv, n)
            proj(pb[kv], 15 + kv, n)
            S = Sst[kv]
            rope_rows(pa[kv], pb[kv], tB, 0, 128, n, S, S[:, :n], 1.0)
            p.dma('pool', lambda e, S=S, kv=kv, t0=t0, n=n: e.dma_start(out=scr['KS'][kv, :, t0:t0 + n], in_=S[:, :n]), reads=[S])
        for tb in range(nb):
            pv = pa[2]
            for kt in range(KT):
                mm(p, pv, pv[:, 0:128], aT[kt], aT[kt][:, tb * 128:(tb + 1) * 128], win, win[:, 17, kt, :], kt == 0, kt == KT - 1)
            p.op('dve', lambda e, pv=pv, tb=tb: e.tensor_copy(VSst[:, tb, :, 0:64], pv[:, 0:128].rearrange("p (h d) -> p h d", d=64)), reads=[pv], writes=[VSst])
        p.dma('pool', lambda e, t0=t0, nb=nb: e.dma_start(out=scr['VS'][t0 // 128:t0 // 128 + nb].rearrange("b p h d -> p b h d"), in_=VSst[:, 0:nb]), reads=[VSst])
    p.close_scope()


def normalize_store(p, C, pO, n, Osb, pR, On, dst_ap, extra_den=None, act_recip=False):
    p.op('dve', lambda e: e.tensor_copy(Osb[:, :n], pO[:, :n]), reads=[pO], writes=[Osb])
    if extra_den is not None:
        p.op('dve', lambda e: e.tensor_scalar(Osb[64:128, :n], Osb[64:128, :n], extra_den, None, ALU.add), reads=[Osb], writes=[Osb])
    if act_recip:
        p.op('act', lambda e: e.activation(Osb[64:128, :n], Osb[64:128, :n], AF.Ln), reads=[Osb], writes=[Osb])
        p.op('act', lambda e: e.activation(Osb[64:128, :n], Osb[64:128, :n], AF.Exp, scale=-1.0), reads=[Osb], writes=[Osb])
    else:
        p.op('dve', lambda e: e.reciprocal(Osb[64:128, :n], Osb[64:128, :n]), reads=[Osb], writes=[Osb])
    mm(p, pR, pR[0:64, :n], C['shiftI'], C['shiftI'][:, :], Osb, Osb[:, :n], True, True)
    p.op('dve', lambda e: e.tensor_tensor(On[0:64, :n], Osb[0:64, :n], pR[0:64, :n], ALU.mult), reads=[Osb, pR], writes=[On])
    p.dma('pool', lambda e: e.dma_start(out=dst_ap, in_=On[0:64, :n]), reads=[On])


def mla_phase(p, nc, C, chunks, T, scr):
    p.open_scope()
    NT = 512
    G3 = 3
    NKT = T // 128
    Kh = [p.sb("Kh", [96, T], BF16) for _ in range(2)]
    Vh = [p.sb("Vh", [128, NKT, 128], BF16) for _ in range(2)]
    Qc = [p.sb("Qc", [96, NT], BF16) for _ in range(2)]
    Pt = [p.sb("Pt", [128, G3, NT], BF16) for _ in range(2)]
    Osb = [p.sb("Osb", [128, NT], F32) for _ in range(2)]
    On = [p.sb("On", [64, NT], BF16) for _ in range(2)]
    pS = [p.ps("pS", (128, G3, NT)) for _ in range(2)]
    pO = p.ps("pO")
    pR = p.ps("pR")
    qi = 0
    for h in range(8):
        K = Kh[h % 2]
        V = Vh[h % 2]
        p.dma('sp', lambda e, K=K, h=h: e.dma_start(out=K[:], in_=scr['KT'][h]), writes=[K])
        p.dma('sp', lambda e, V=V, h=h: e.dma_start(out=V[:], in_=scr['V'][:, :, h, :].rearrange("b p d -> p b d")), writes=[V])
        for ci, (t0, n, kind) in enumerate(chunks):
            Q = Qc[qi % 2]
            on = On[qi % 2]
            osb = Osb[qi % 2]
            qi += 1
            p.dma('sp', lambda e, Q=Q, h=h, t0=t0, n=n: e.dma_start(out=Q[:, :n], in_=scr['QT'][h, :, t0:t0 + n]), writes=[Q])
            kts = list(range(2)) if kind == 1 else list(range(NKT))
            groups = [kts[i:i + G3] for i in range(0, len(kts), G3)]

            def smm(gi):
                ps = pS[gi % 2]
                for j, kt in enumerate(groups[gi]):
                    mm(p, ps, ps[:, j, :n], K, K[0:96, kt * 128:(kt + 1) * 128], Q, Q[0:96, :n], True, True)

            smm(0)
            nmm = len(kts)
            done = 0
            for gi, grp in enumerate(groups):
                if gi + 1 < len(groups):
                    smm(gi + 1)
                ps = pS[gi % 2]
                pt = Pt[gi % 2]
                ng = len(grp)
                p.op('act', lambda e, ps=ps, pt=pt, ng=ng, n=n: e.activation(pt[:, 0:ng, :n], ps[:, 0:ng, :n], AF.Exp), reads=[ps], writes=[pt])
                for j, kt in enumerate(grp):
                    mm(p, pO, pO[:, :n], V, V[:, kt, :], pt, pt[:, j, :n], done == 0, done == nmm - 1)
                    done += 1
            normalize_store(p, C, pO, n, osb, pR, on, scr['OT'][h * 64:(h + 1) * 64, t0:t0 + n])
    p.close_scope()


def swa_phase(p, nc, C, chunks, T, scr, sink_d):
    p.open_scope()
    NT = 512
    NKT = T // 128
    KS = p.sb("KS", [128, 2, T], BF16)
    p.dma('sp', lambda e: e.dma_start(out=KS[:], in_=scr['KS'].rearrange("k p t -> p k t")), writes=[KS])
    VS = p.sb("VS", [128, NKT, 2, 128], BF16)
    p.dma('sp', lambda e: e.dma_start(out=VS[:], in_=scr['VS'].rearrange("b p k d -> p b k d")), writes=[VS])
    esink = p.sb("esink", [128, 8], F32)
    p.dma('sp', lambda e: e.dma_start(out=esink[:], in_=sink_d), writes=[esink])
    p.op('act', lambda e: e.activation(esink[:], esink[:], AF.Exp), reads=[esink], writes=[esink])
    QS = [p.sb("QS", [128, 4, NT], BF16) for _ in range(2)]
    Pc4 = [p.sb("Pc", [128, NT], BF16) for _ in range(4)]
    Pl8 = [p.sb("Pl", [128, 384], BF16) for _ in range(8)]
    Osb2 = [p.sb("Osb", [128, NT], F32) for _ in range(2)]
    On = [p.sb("On", [64, NT], BF16) for _ in range(2)]
    pC = [p.ps("pC") for _ in range(2)]
    pL = [p.ps("pL") for _ in range(2)]
    pO = [p.ps("pO") for _ in range(2)]
    pR = p.ps("pR")
    mprev = C['mprev']
    mnext = C['mnext']
    oi = 0
    for ci, (t0, n, kind) in enumerate(chunks):
        Q = QS[ci % 2]
        p.dma('sp', lambda e, Q=Q, t0=t0, n=n: e.dma_start(out=Q[:, :, :n], in_=scr['QS'].rearrange("i p t -> p i t")[:, :, t0:t0 + n]), writes=[Q])
        nb = n // 128
        for hh in range(8):
            i, e2 = hh // 2, hh % 2
            kv = hh // 4
            lo, hi = 64 * e2, 64 * e2 + 64
            po = pO[oi % 2]
            on = On[oi % 2]
            Osb = Osb2[oi % 2]
            Pc = Pc4[2 * (oi % 2):2 * (oi % 2) + 2]
            Pl = Pl8[4 * (oi % 2):4 * (oi % 2) + 4]
            oi += 1
            for c2 in range(2):
                mm(p, pC[c2], pC[c2][:, :n], KS, KS[lo:hi, kv, c2 * 128:(c2 + 1) * 128], Q, Q[lo:hi, i, :n], True, True)
                p.op('act', lambda e, c2=c2, Pc=Pc, n=n: e.activation(Pc[c2][:, :n], pC[c2][:, :n], AF.Exp), reads=[pC[c2]], writes=[Pc[c2]])
            loc = []
            if kind == 0:
                for qb in range(nb):
                    kt_c = (t0 // 128) + qb
                    tiles = [(kt_c - 1, 0), (kt_c, 1), (kt_c + 1, 2)]
                    tiles = [(kt, s) for kt, s in tiles if 2 <= kt < NKT]
                    pl = pL[qb % 2]
                    P = Pl[qb]
                    for kt, s in tiles:
                        mm(p, pl, pl[:, s * 128:(s + 1) * 128], KS, KS[lo:hi, kv, kt * 128:(kt + 1) * 128], Q, Q[lo:hi, i, qb * 128:(qb + 1) * 128], True, True)
                    c0, c1 = tiles[0][1] * 128, tiles[-1][1] * 128 + 128
                    p.op('act', lambda e, pl=pl, P=P, c0=c0, c1=c1: e.activation(P[:, c0:c1], pl[:, c0:c1], AF.Exp), reads=[pl], writes=[P])
                    for kt, s in tiles:
                        if s == 0:
                            p.op('dve', lambda e, P=P: e.tensor_tensor(P[:, 0:128], P[:, 0:128], mprev[:], ALU.mult), reads=[P, mprev], writes=[P])
                        if s == 2:
                            p.op('dve', lambda e, P=P: e.tensor_tensor(P[:, 256:384], P[:, 256:384], mnext[:], ALU.mult), reads=[P, mnext], writes=[P])
                    loc.append(tiles)
            for qb in range(nb):
                items = [(0, Pc[0], qb * 128), (1, Pc[1], qb * 128)]
                if kind == 0:
                    for kt, s in loc[qb]:
                        items.append((kt, Pl[qb], s * 128))
                for k2, (kt, P, c0) in enumerate(items):
                    mm(p, po, po[:, qb * 128:(qb + 1) * 128], VS, VS[:, kt, kv, :], P, P[:, c0:c0 + 128], k2 == 0, k2 == len(items) - 1)
            normalize_store(p, C, po, n, Osb, pR, on, scr['OT'][(8 + hh) * 64:(9 + hh) * 64, t0:t0 + n], extra_den=esink[64:128, hh:hh + 1], act_recip=True)
    p.close_scope()


def mixout_phase(p, nc, C, y_src, wout_s, h_in, h_out, co, s, chunks, nkt_in):
    p.open_scope()
    NT = 512
    hTf = [p.sb("hT", [128, KT, NT], F32) for _ in range(2)]
    hTs = [p.views(b, KT) for b in hTf]
    yTf = [p.sb("yT", [128, nkt_in, NT], BF16) for _ in range(2)]
    sq = p.views(p.sb("sq", [128, KT, NT], BF16), KT)
    rstd2 = p.sb("rstd2", [128, NT], F32)
    tmps = [p.sb("tmp", [128, NT], F32) for _ in range(3)]
    ysb = p.views(p.sb("ysb", [128, KT, NT], F32), KT)
    wo = p.sb("wo", [128, KT, nkt_in, 128], BF16)
    p.dma('sp', lambda e: e.dma_start(out=wo[:].rearrange("p i a b -> p i (a b)"), in_=wout_s.rearrange("i p x -> p i x")), writes=[wo])
    pstat = p.ps("pstat")
    py = [p.ps("py") for _ in range(2)]
    hin_v = h_in.rearrange("(kt p) t -> p kt t", p=128)
    hout_v = h_out.rearrange("(kt p) t -> p kt t", p=128)
    ysrc_v = y_src.rearrange("(kt p) t -> p kt t", p=128)
    for ci, (t0, n, kind) in enumerate(chunks):
        hT = hTs[ci % 2]
        hf = hTf[ci % 2]
        yT = yTf[ci % 2]
        p.dma('sp', lambda e, hf=hf, t0=t0, n=n: e.dma_start(out=hf[:, :, :n], in_=hin_v[:, :, t0:t0 + n]), writes=hT)
        p.dma('sp', lambda e, yT=yT, t0=t0, n=n: e.dma_start(out=yT[:, :, :n], in_=ysrc_v[:, :, t0:t0 + n]), writes=[yT])
        for i in range(KT):
            y = py[i % 2]
            for j in range(nkt_in):
                mm(p, y, y[:, :n], wo, wo[:, i, j, :], yT, yT[:, j, :n], j == 0, j == nkt_in - 1)
            p.op('dve', lambda e, y=y, i=i: e.tensor_copy(ysb[i][:, :n], y[:, :n]), reads=[y], writes=[ysb[i]])
            p.op('act', lambda e, i=i: e.activation(sq[i][:, :n], ysb[i][:, :n], AF.Square), reads=[ysb[i]], writes=[sq[i]])
        post_residual(p, C, hT, ysb, n, co, s, kind, sq, pstat, rstd2, tmps)
        p.dma('pool', lambda e, hf=hf, t0=t0, n=n: e.dma_start(out=hout_v[:, :, t0:t0 + n], in_=hf[:, :, :n]), reads=hT)
    p.close_scope()


S5T = 8
MAGIC = 12582912.0
TWO_PI = 6.283185307179586
PI = 3.141592653589793


def s5_in_phase(p, nc, C, h_in, co, chunks, w5_s, dsk_d, uT_d, y0_d):
    p.open_scope()
    NT = 512
    ident = C['ident']
    hTf = [p.sb("hT", [128, KT, NT], F32) for _ in range(2)]
    hTs = [p.views(b, KT) for b in hTf]
    aT = p.views(p.sb("aT", [128, KT, NT], BF16), KT)
    sq = p.views(p.sb("sq", [128, KT, NT], BF16), KT)
    rstd = p.sb("rstd", [128, NT], F32)
    tmps = [p.sb("tmp", [128, NT], F32) for _ in range(3)]
    pstat = p.ps("pstat")
    pu = [p.ps("pu") for _ in range(2)]
    w5 = p.sb("w5", [128, 8, KT, 128], BF16)
    p.dma('sp', lambda e: e.dma_start(out=w5[:].rearrange("p g a b -> p g (a b)"), in_=w5_s.rearrange("g p x -> p g x")), writes=[w5])
    stage = p.sb("stage", [128, 128], F32)
    dsk = p.sb("dsk", [128, 8], F32)
    load_cols(p, dsk, dsk[:], dsk_d, 8, ident, pstat, stage)
    ub = [p.sb("ub", [128, NT], BF16) for _ in range(2)]
    y0 = [p.sb("y0", [128, NT], F32) for _ in range(2)]
    hin_v = h_in.rearrange("(kt p) t -> p kt t", p=128)
    for ci, (t0, n, kind) in enumerate(chunks):
        hT = hTs[ci % 2]
        hf = hTf[ci % 2]
        p.dma('sp', lambda e, hf=hf, t0=t0, n=n: e.dma_start(out=hf[:, :, :n], in_=hin_v[:, :, t0:t0 + n]), writes=hT)
        modulate(p, C, hT, n, co, 1, kind, sq, pstat, rstd, tmps, aT)
        for ct in range(8):
            ps = pu[ct % 2]
            for kt in range(KT):
                mm(p, ps, ps[:, :n], w5, w5[:, ct, kt, :], aT[kt], aT[kt][:, :n], kt == 0, kt == KT - 1)
            u = ub[ct % 2]
            p.op('dve', lambda e, ps=ps, u=u: e.tensor_copy(u[:, :n], ps[:, :n]), reads=[ps], writes=[u])
            p.dma('pool', lambda e, u=u, ct=ct, t0=t0, n=n: e.dma_start(out=uT_d[ct * 128:(ct + 1) * 128, t0:t0 + n], in_=u[:, :n]), reads=[u])
            if kind == 0:
                y = y0[ct % 2]
                p.op('dve', lambda e, ps=ps, y=y, ct=ct: e.tensor_scalar(y[:, :n], ps[:, :n], dsk[:, ct:ct + 1], None, ALU.mult), reads=[ps, dsk], writes=[y])
                p.dma('pool', lambda e, y=y, ct=ct, t0=t0, n=n: e.dma_start(out=y0_d[ct * 128:(ct + 1) * 128, t0 - NCTX:t0 - NCTX + n], in_=y[:, :n]), reads=[y])
    p.close_scope()


def s5_scan_phase(p, nc, C, T, SEQ, uT_d, y0_d, gT_d, lam_d, B12_d, C1_d, J_d, rmask_d):
    p.open_scope()
    ident = C['ident']
    NSC = T // S5T
    NCC = NCTX // S5T
    NLC = SEQ // S5T
    XW = NSC + 2
    NR = 0
    while (1 << NR) < NSC:
        NR += 1
    NG = 8
    KLIST = list(range(1, 9)) + [8 << r for r in range(1, NR)]
    NK = len(KLIST)
    Jt = p.sb("Jt", [128, 128], F32)
    p.dma('sp', lambda e: e.dma_start(out=Jt[:], in_=J_d), writes=[Jt])
    rmask = p.sb("rmask", [128, 8], F32)
    p.dma('sp', lambda e: e.dma_start(out=rmask[:], in_=rmask_d), writes=[rmask])
    pb = [p.ps("pb%d" % i) for i in range(8)]
    u_de = p.sb("u_de", [128, S5T, NSC], BF16)
    Y = p.sb("Y", [128, SEQ], F32)
    X = p.sb("X", [128, 8, XW], F32)
    Xb = p.sb("Xb", [128, 8, XW], BF16)
    Xg = p.views(X, 8)
    Xbg = p.views(Xb, 8)
    p.op('pool', lambda e: e.memset(X[:], 0.0), writes=[X] + Xg)
    p.op('pool', lambda e: e.memset(Xb[:], 0.0), writes=[Xb] + Xbg)
    CtL = p.sb("CtL", [128, 8, 8, 128], BF16)
    p.op('pool', lambda e: e.memset(CtL[:], 0.0), writes=[CtL])
    CtLv = [[Buf("CtLv", CtL.t[:, t, g, :]) for g in range(NG)] for t in range(8)]
    UL = p.sb("UL", [128, NR * 8, 128], BF16)
    ULv = [Buf("ULv", UL.t[:, i, :]) for i in range(NR * 8)]
    KdL = p.sb("KdL", [128, 8, 128], BF16)
    MBm = p.sb("MBm", [128, 8, 128], F32)
    p.op('pool', lambda e: e.memset(MBm[:], 0.0), writes=[MBm])
    MBv = p.views(MBm, 8)
    ctf = p.sb("ctf", [128, 9, 8, 16], F32)
    ctfv = [[Buf("ctfv", ctf.t[:, k, g, :]) for g in range(NG)] for k in range(9)]
    Qk = [p.sb("Qk", [128, 128], F32) for _ in range(8)]
    Bs1 = p.sb("Bs1", [128, 8, 16], F32)
    Bs2 = p.sb("Bs2", [128, 8, 16], F32)
    Cs1 = p.sb("Cs1", [128, 8, 16], F32)
    Cs2 = p.sb("Cs2", [128, 8, 16], F32)
    tb = {nm: p.sb("tb_" + nm, [128, 64], F32) for nm in
          ['lre', 'lim', 'lst', 'dt', 'lr', 'lrdt', 'mag', 'ang', 'angk', 'red', 'sin', 'cos', 'ar', 'ai', 'nai', 'den', 'am1', 'fr', 'fi', 'nfi', 't1', 't2']}
    tbd = [{nm: p.sb("tbd_" + nm, [128, 64], F32) for nm in ['fr', 'S2']} for _ in range(2)]
    pw = [[{nm: p.sb("pw_" + nm, [128, 64], F32) for nm in (['AR', 'T5', 'T3', 'T4'] if KLIST[ki] <= 8 else ['T3', 'T4'])}
           for ki in range(NK)] for _ in range(2)]

    def ew(eng, fn, reads, writes):
        p.op(eng, fn, reads=reads, writes=writes)

    def reduce_sin(dst, src, shift):
        r = tb['red']
        ew('dve', lambda e: e.tensor_scalar(r[:], src[:], float(shift), None, ALU.add), [src], [r])
        ew('dve', lambda e: e.tensor_scalar(tb['t2'][:], r[:], float(1.0 / TWO_PI), MAGIC, ALU.mult, ALU.add), [r], [tb['t2']])
        ew('dve', lambda e: e.tensor_scalar(tb['t2'][:], tb['t2'][:], -MAGIC, None, ALU.add), [tb['t2']], [tb['t2']])
        ew('dve', lambda e: e.scalar_tensor_tensor(r[:], tb['t2'][:], float(-TWO_PI), r[:], ALU.mult, ALU.add), [tb['t2'], r], [r])
        ew('dve', lambda e: e.tensor_scalar(r[:], r[:], float(PI), float(-PI), ALU.min, ALU.max), [r], [r])
        ew('act', lambda e: e.activation(dst[:], r[:], AF.Sin), [r], [dst])

    def tt(dst, a, b, op, eng='dve'):
        ew(eng, lambda e: e.tensor_tensor(dst[:], a[:], b[:], op), [a, b], [dst])

    def half_copy(dst, top, bot):
        ew('dve', lambda e: e.tensor_copy(dst[0:64, :], top[0:64, :]), [top], [dst])
        ew('dve', lambda e: e.tensor_copy(dst[64:128, :], bot[64:128, :]), [bot], [dst])

    def power(k, dst):
        ew('dve', lambda e: e.tensor_scalar(tb['t1'][:], tb['lrdt'][:], float(k), None, ALU.mult), [tb['lrdt']], [tb['t1']])
        ew('act', lambda e: e.activation(tb['mag'][:], tb['t1'][:], AF.Exp), [tb['t1']], [tb['mag']])
        ew('dve', lambda e: e.tensor_scalar(tb['angk'][:], tb['ang'][:], float(k), None, ALU.mult), [tb['ang']], [tb['angk']])
        reduce_sin(tb['sin'], tb['angk'], 0.0)
        reduce_sin(tb['cos'], tb['angk'], PI / 2)
        tt(tb['ar'], tb['mag'], tb['cos'], ALU.mult)
        tt(tb['ai'], tb['mag'], tb['sin'], ALU.mult)
        ew('dve', lambda e: e.tensor_scalar(tb['nai'][:], tb['ai'][:], -1.0, None, ALU.mult), [tb['ai']], [tb['nai']])

    for d in range(2):
        for i, nm in enumerate(['lre', 'lim', 'lst']):
            p.dma('sp', lambda e, i=i, nm=nm, d=d: e.dma_start(out=tb[nm][:], in_=lam_d[d, i]), writes=[tb[nm]])
        ew('act', lambda e: e.activation(tb['dt'][:], tb['lst'][:], AF.Exp), [tb['lst']], [tb['dt']])
        ew('dve', lambda e: e.tensor_scalar(tb['lr'][:], tb['lre'][:], -1e-4, None, ALU.min), [tb['lre']], [tb['lr']])
        tt(tb['lrdt'], tb['lr'], tb['dt'], ALU.mult)
        tt(tb['ang'], tb['lim'], tb['dt'], ALU.mult)
        for ki, k in enumerate(KLIST):
            power(k, None)
            t = pw[d][ki]
            if 'AR' in t:
                ew('pool', lambda e, t=t: e.tensor_copy(t['AR'][:], tb['ar'][:]), [tb['ar']], [t['AR']])
                half_copy(t['T5'], tb['ai'], tb['nai'])
            half_copy(t['T3'], tb['ar'], tb['nai'])
            half_copy(t['T4'], tb['ai'], tb['ar'])
            if k == 1:
                tt(tb['den'], tb['lr'], tb['lr'], ALU.mult)
                tt(tb['t1'], tb['lim'], tb['lim'], ALU.mult)
                tt(tb['den'], tb['den'], tb['t1'], ALU.add)
                ew('dve', lambda e: e.reciprocal(tb['den'][:], tb['den'][:]), [tb['den']], [tb['den']])
                ew('dve', lambda e: e.tensor_scalar(tb['am1'][:], tb['ar'][:], -1.0, None, ALU.add), [tb['ar']], [tb['am1']])
                tt(tb['t1'], tb['am1'], tb['lr'], ALU.mult)
                tt(tb['t2'], tb['ai'], tb['lim'], ALU.mult)
                tt(tb['t1'], tb['t1'], tb['t2'], ALU.add)
                tt(tbd[d]['fr'], tb['t1'], tb['den'], ALU.mult)
                tt(tb['t1'], tb['ai'], tb['lr'], ALU.mult)
                tt(tb['t2'], tb['am1'], tb['lim'], ALU.mult)
                tt(tb['t1'], tb['t1'], tb['t2'], ALU.subtract)
                tt(tb['fi'], tb['t1'], tb['den'], ALU.mult)
                ew('dve', lambda e: e.tensor_scalar(tb['nfi'][:], tb['fi'][:], -1.0, None, ALU.mult), [tb['fi']], [tb['nfi']])
                half_copy(tbd[d]['S2'], tb['nfi'], tb['fi'])

    Xbflat = Xb[:].rearrange("p a b -> p (a b)")
    for ct in range(8):
        for d in range(2):
            fwd = (d == 0)
            if fwd:
                pieces = [(0, NCC, 0)]
                c = 0
                while c < NLC:
                    w = min(512, NLC - c)
                    pieces.append((NCC + c, w, NCTX + c * S5T))
                    c += w
            else:
                pieces = []
                c = 0
                while c < NLC:
                    w = min(512, NLC - c)
                    pieces.append((c, w, NCTX + c * S5T))
                    c += w
                pieces.append((NLC, NCC, 0))
            if d == 0:
                p.sync_all()
                p.dma('sp', lambda e, ct=ct: e.dma_start(out=Xbflat[:, 0:T], in_=uT_d[ct * 128:(ct + 1) * 128, :]), writes=[Xb])
                uv = Xbflat[:, 0:T].rearrange("p (c s) -> p c s", s=S5T)
                for s in range(S5T):
                    eng = 'dve' if s % 2 == 0 else 'act'
                    if eng == 'dve':
                        ew('dve', lambda e, s=s, uv=uv: e.tensor_copy(u_de[:, s, :], uv[:, :, s]), [Xb], [u_de])
                    else:
                        ew('act', lambda e, s=s, uv=uv: e.copy(u_de[:, s, :], uv[:, :, s]), [Xb], [u_de])
                ew('pool', lambda e: e.memset(Xb[:, :, 0:1], 0.0), [u_de], [Xb])
                ew('pool', lambda e: e.memset(Xb[:, :, XW - 1:XW], 0.0), [u_de], [Xb])
                p.dma('sp', lambda e, ct=ct: e.dma_start(out=Y[:], in_=y0_d[ct * 128:(ct + 1) * 128, :]), writes=[Y])
                p.sync_all()
            p.dma('sp', lambda e, ct=ct, d=d: e.dma_start(out=Bs1[:], in_=B12_d[d, 0, :, ct * 8:(ct + 1) * 8, :]), writes=[Bs1])
            p.dma('sp', lambda e, ct=ct, d=d: e.dma_start(out=Bs2[:], in_=B12_d[d, 1, :, ct * 8:(ct + 1) * 8, :]), writes=[Bs2])
            p.dma('sp', lambda e, ct=ct, d=d: e.dma_start(out=Cs1[:], in_=C1_d[d, 0, :, ct * 8:(ct + 1) * 8, :]), writes=[Cs1])
            p.dma('sp', lambda e, ct=ct, d=d: e.dma_start(out=Cs2[:], in_=C1_d[d, 1, :, ct * 8:(ct + 1) * 8, :]), writes=[Cs2])
            ew('pool', lambda e: e.tensor_scalar(Cs1[64:128], Cs1[64:128], -1.0, None, ALU.mult), [Cs1], [Cs1])
            ew('pool', lambda e: e.tensor_scalar(Cs2[0:64], Cs2[0:64], -1.0, None, ALU.mult), [Cs2], [Cs2])
            t_fr, t_S2 = tbd[d]['fr'], tbd[d]['S2']
            for g in range(NG):
                gg = ct * 8 + g
                blk = slice(16 * g, 16 * g + 16)
                ew('dve', lambda e, g=g, gg=gg, blk=blk, t_fr=t_fr: e.tensor_scalar(MBm[:, g, blk], Bs1[:, g, :], t_fr[:, gg:gg + 1], None, ALU.mult), [Bs1, t_fr], [MBv[g]])
                ew('dve', lambda e, g=g, gg=gg, blk=blk, t_S2=t_S2: e.scalar_tensor_tensor(MBm[:, g, blk], Bs2[:, g, :], t_S2[:, gg:gg + 1], MBm[:, g, blk], ALU.mult, ALU.add), [Bs2, t_S2, MBv[g]], [MBv[g]])
                ew('pool', lambda e, g=g: e.tensor_copy(ctf[:, 0, g, :], Cs1[:, g, :]), [Cs1], [ctfv[0][g]])
            BsT = [p.view(pb[3 + k // 4], pb[3 + k // 4].t[:, (k % 4) * 128:(k % 4 + 1) * 128]) for k in range(8)]
            KdA = [p.view(pb[5 + k // 4], pb[5 + k // 4].t[:, (k % 4) * 128:(k % 4 + 1) * 128]) for k in range(8)]
            qi = 0
            for k in range(0, 9):
                tk = pw[d][k - 1] if k >= 1 else None
                for g in range(NG):
                    gg = ct * 8 + g
                    blk = slice(16 * g, 16 * g + 16)
                    if k >= 1:
                        cv = ctfv[k][g]
                        ew('dve', lambda e, k=k, g=g, gg=gg, tk=tk: e.tensor_scalar(ctf[:, k, g, :], Cs1[:, g, :], tk['AR'][:, gg:gg + 1], None, ALU.mult), [Cs1, tk['AR']], [cv])
                        ew('dve', lambda e, k=k, g=g, gg=gg, tk=tk: e.scalar_tensor_tensor(ctf[:, k, g, :], Cs2[:, g, :], tk['T5'][:, gg:gg + 1], ctf[:, k, g, :], ALU.mult, ALU.add), [Cs2, tk['T5'], cv], [cv])
                        t_idx = (k - 1) if fwd else (8 - k)
                        ew('act', lambda e, k=k, g=g, t_idx=t_idx, blk=blk: e.copy(CtL[:, t_idx, g, blk], ctf[:, k, g, :]), [cv], [CtLv[t_idx][g]])
                    if k <= 7:
                        if k == 0:
                            Qb_, Qap = ident, ident[:, :]
                        else:
                            Qb_ = Qk[qi % 8]
                            qi += 1
                            ew('pool', lambda e, Qb_=Qb_, gg=gg, tk=tk: e.tensor_scalar(Qb_[:, 0:64], Jt[:, 0:64], tk['T3'][:, gg:gg + 1], None, ALU.mult), [Jt, tk['T3']], [Qb_])
                            ew('pool', lambda e, Qb_=Qb_, gg=gg, tk=tk: e.tensor_scalar(Qb_[:, 64:128], Jt[:, 64:128], tk['T4'][:, gg:gg + 1], None, ALU.mult), [Jt, tk['T4']], [Qb_])
                            Qap = Qb_[:, :]
                        mm(p, BsT[k], BsT[k][:, :], MBv[g], MBm[:, g, :], Qb_, Qap, g == 0, g == NG - 1)
                        mm(p, KdA[k], KdA[k][:, blk], MBv[g], MBm[:, g, :], ctfv[k][g], ctf[:, k, g, :], True, True)
            p.sync_all()
            for k in range(8):
                s_idx = (7 - k) if fwd else k
                for g in range(NG):
                    if k < 4:
                        ew('dve', lambda e, k=k, g=g, s_idx=s_idx: e.tensor_scalar(UL[:, s_idx * 8 + g, :], BsT[k][:, :], rmask[:, g:g + 1], None, ALU.mult), [BsT[k], rmask], [ULv[s_idx * 8 + g]])
                    else:
                        ew('act', lambda e, k=k, g=g, s_idx=s_idx: e.activation(UL[:, s_idx * 8 + g, :], BsT[k][:, :], AF.Copy, scale=rmask[:, g:g + 1]), [BsT[k], rmask], [ULv[s_idx * 8 + g]])
                if k < 4:
                    ew('dve', lambda e, k=k: e.tensor_copy(KdL[:, k, :], KdA[k][:, :]), [KdA[k]], [KdL])
                else:
                    ew('act', lambda e, k=k: e.copy(KdL[:, k, :], KdA[k][:, :]), [KdA[k]], [KdL])
            p.sync_all()
            bi = 0
            for g in range(NG):
                for (c0, ncol, tok0) in pieces:
                    ps = pb[bi % 4]
                    bi += 1
                    j0 = tok0 // S5T
                    for s in range(S5T):
                        mm(p, ps, ps[:, :ncol], ULv[s * 8 + g], UL[:, s * 8 + g, :], u_de, u_de[:, s, j0:j0 + ncol], s == 0, s == S5T - 1)
                    ew('dve', lambda e, ps=ps, g=g, c0=c0, ncol=ncol: e.tensor_copy(X[:, g, 1 + c0:1 + c0 + ncol], ps[:, :ncol]), [ps], [Xg[g]])
                    ew('act', lambda e, g=g, c0=c0, ncol=ncol: e.copy(Xb[:, g, 1 + c0:1 + c0 + ncol], X[:, g, 1 + c0:1 + c0 + ncol]), [Xg[g]], [Xbg[g]])
            p.sync_all()
            for r in range(NR):
                tk = pw[d][7 + r]
                for g in range(NG):
                    gg = ct * 8 + g
                    uv_ = ULv[r * 8 + g]
                    if (r * NG + g) % 2 == 0:
                        ew('pool', lambda e, r=r, g=g, gg=gg, tk=tk: e.tensor_scalar(UL[:, r * 8 + g, 0:64], Jt[:, 0:64], tk['T3'][:, gg:gg + 1], None, ALU.mult), [Jt, tk['T3']], [uv_])
                        ew('pool', lambda e, r=r, g=g, gg=gg, tk=tk: e.tensor_scalar(UL[:, r * 8 + g, 64:128], Jt[:, 64:128], tk['T4'][:, gg:gg + 1], None, ALU.mult), [Jt, tk['T4']], [uv_])
                    else:
                        ew('act', lambda e, r=r, g=g, gg=gg, tk=tk: e.activation(UL[:, r * 8 + g, 0:64], Jt[:, 0:64], AF.Copy, scale=tk['T3'][:, gg:gg + 1]), [Jt, tk['T3']], [uv_])
                        ew('act', lambda e, r=r, g=g, gg=gg, tk=tk: e.activation(UL[:, r * 8 + g, 64:128], Jt[:, 64:128], AF.Copy, scale=tk['T4'][:, gg:gg + 1]), [Jt, tk['T4']], [uv_])
            p.sync_all()
            bi = 0
            for r in range(NR):
                sh = 1 << r
                L = NSC - sh
                if L <= 0:
                    continue
                if fwd:
                    dst0, src0 = 1 + sh, 1
                else:
                    dst0, src0 = 1, 1 + sh
                segs = []
                c = 0
                while c < L:
                    w = min(512, L - c)
                    segs.append((c, w))
                    c += w
                for g in range(NG):
                    pss = []
                    for (c, w) in segs:
                        ps = pb[bi % 6]
                        bi += 1
                        mm(p, ps, ps[:, :w], ULv[r * 8 + g], UL[:, r * 8 + g, :], Xbg[g], Xb[:, g, src0 + c:src0 + c + w], True, True)
                        pss.append(ps)
                    for (c, w), ps in zip(segs, pss):
                        ew('dve', lambda e, ps=ps, g=g, c=c, w=w, dst0=dst0: e.tensor_tensor(X[:, g, dst0 + c:dst0 + c + w], X[:, g, dst0 + c:dst0 + c + w], ps[:, :w], ALU.add), [Xg[g], ps], [Xg[g]])
                        ew('act', lambda e, g=g, c=c, w=w, dst0=dst0: e.copy(Xb[:, g, dst0 + c:dst0 + c + w], X[:, g, dst0 + c:dst0 + c + w]), [Xg[g]], [Xbg[g]])
            p.sync_all()
            bi = 0
            for (c0, ncol, tok0) in pieces:
                if tok0 < NCTX:
                    continue
                ent0 = c0 if fwd else c0 + 2
                j0 = tok0 // S5T
                yv = Y[:, tok0 - NCTX:tok0 - NCTX + S5T * ncol].rearrange("p (c s) -> p c s", s=S5T)
                for t in range(S5T):
                    ps = pb[bi % 4]
                    bi += 1
                    ss = list(range(0, t + 1)) if fwd else list(range(t, S5T))
                    nmm = NG + len(ss)
                    k2 = 0
                    for g in range(NG):
                        mm(p, ps, ps[:, :ncol], CtLv[t][g], CtL[:, t, g, :], Xbg[g], Xb[:, g, ent0:ent0 + ncol], k2 == 0, k2 == nmm - 1)
                        k2 += 1
                    for s in ss:
                        mm(p, ps, ps[:, :ncol], KdL, KdL[:, abs(t - s), :], u_de, u_de[:, s, j0:j0 + ncol], k2 == 0, k2 == nmm - 1)
                        k2 += 1
                    ew('dve', lambda e, ps=ps, t=t, yv=yv, ncol=ncol: e.tensor_tensor(yv[:, :, t], yv[:, :, t], ps[:, :ncol], ALU.add), [Y, ps], [Y])
            p.sync_all()
        p.op('act', lambda e: e.activation(Xbflat[:, 0:SEQ], Y[:], AF.Gelu), reads=[Y], writes=[Xb])
        p.dma('pool', lambda e, ct=ct: e.dma_start(out=gT_d[ct * 128:(ct + 1) * 128, :], in_=Xbflat[:, 0:SEQ]), reads=[Xb])
    p.close_scope()


def glu_phase(p, nc, C, gT_d, wg_s, h_in, h_out, co, chunks, out_off):
    p.open_scope()
    NT = 512
    hTf = [p.sb("hT", [128, KT, NT], F32) for _ in range(2)]
    hTs = [p.views(b, KT) for b in hTf]
    gTf = [p.sb("gT", [128, KT, NT], BF16) for _ in range(2)]
    sq = p.views(p.sb("sq", [128, KT, NT], BF16), KT)
    rstd2 = p.sb("rstd2", [128, NT], F32)
    tmps = [p.sb("tmp", [128, NT], F32) for _ in range(3)]
    sgs = [p.sb("sg", [128, NT], F32) for _ in range(2)]
    ysb = p.views(p.sb("ysb", [128, KT, NT], F32), KT)
    wg = p.sb("wg", [128, KT, KT, 256], BF16)
    p.dma('sp', lambda e: e.dma_start(out=wg[:].rearrange("p i a b -> p i (a b)"), in_=wg_s.rearrange("i p x -> p i x")), writes=[wg])
    pstat = p.ps("pstat")
    pa = [p.ps("pa") for _ in range(2)]
    pg = [p.ps("pg") for _ in range(2)]
    hin_v = h_in.rearrange("(kt p) t -> p kt t", p=128)
    hout_v = h_out.rearrange("(kt p) t -> p kt t", p=128)
    g_v = gT_d.rearrange("(kt p) t -> p kt t", p=128)
    for ci, (t0, n, kind) in enumerate(chunks):
        hT = hTs[ci % 2]
        hf = hTf[ci % 2]
        gT = gTf[ci % 2]
        p.dma('sp', lambda e, hf=hf, t0=t0, n=n: e.dma_start(out=hf[:, :, :n], in_=hin_v[:, :, t0:t0 + n]), writes=hT)
        p.dma('sp', lambda e, gT=gT, t0=t0, n=n: e.dma_start(out=gT[:, :, :n], in_=g_v[:, :, t0 - NCTX:t0 - NCTX + n]), writes=[gT])
        for i in range(KT):
            a_, g_ = pa[i % 2], pg[i % 2]
            for kt in range(KT):
                mm(p, a_, a_[:, :n], wg, wg[:, i, kt, 0:128], gT, gT[:, kt, :n], kt == 0, kt == KT - 1)
            for kt in range(KT):
                mm(p, g_, g_[:, :n], wg, wg[:, i, kt, 128:256], gT, gT[:, kt, :n], kt == 0, kt == KT - 1)
            sg = sgs[i % 2]
            p.op('act', lambda e, g_=g_, sg=sg: e.activation(sg[:, :n], g_[:, :n], AF.Sigmoid), reads=[g_], writes=[sg])
            p.op('dve', lambda e, a_=a_, sg=sg, i=i: e.tensor_tensor(ysb[i][:, :n], sg[:, :n], a_[:, :n], ALU.mult), reads=[sg, a_], writes=[ysb[i]])
            p.op('act', lambda e, i=i: e.activation(sq[i][:, :n], ysb[i][:, :n], AF.Square), reads=[ysb[i]], writes=[sq[i]])
        post_residual(p, C, hT, ysb, n, co, 1, kind, sq, pstat, rstd2, tmps)
        p.dma('pool', lambda e, hf=hf, t0=t0, n=n: e.dma_start(out=hout_v[:, :, t0 - out_off:t0 - out_off + n], in_=hf[:, :, :n]), reads=hT)
    p.close_scope()


import numpy as np
GRID_W = 64
ROPE_BASE = 10000.0

def rope_tables(seq, nctx):
    t = np.arange(seq)
    rows = (t // GRID_W).astype(np.float32)
    cols = (t % GRID_W).astype(np.float32)
    def tab(d_rot):
        d_axis = d_rot // 2
        inv = (ROPE_BASE ** (-np.arange(0, d_axis, 2, dtype=np.float32) / d_axis)).astype(np.float32)
        ang = np.concatenate([rows[:, None] * inv, cols[:, None] * inv], axis=-1).astype(np.float32)
        return np.cos(ang).astype(np.float32), np.sin(ang).astype(np.float32)
    T = nctx + seq
    ca, sa = tab(32)
    cb, sb_ = tab(64)
    A = np.zeros((32, 2, T), np.float32); A[:, 0, :] = 1.0
    A[:, 0, nctx:] = np.concatenate([ca.T, ca.T], 0)
    A[:, 1, nctx:] = np.concatenate([sa.T, sa.T], 0)
    B = np.zeros((128, 2, T), np.float32); B[:, 0, :] = 1.0
    B[:, 0, nctx:] = np.concatenate([cb.T, cb.T, cb.T, cb.T], 0)
    B[:, 1, nctx:] = np.concatenate([sb_.T, sb_.T, sb_.T, sb_.T], 0)
    return A, B

def const_inputs():
    ident = np.eye(128, dtype=np.float32)
    shiftI = np.zeros((128, 64), np.float32); shiftI[64 + np.arange(64), np.arange(64)] = 1.0
    j = np.arange(128)[:, None]; q = np.arange(128)[None, :]
    mprev = (j >= q).astype(np.float32)
    mnext = (j <= q).astype(np.float32)
    return dict(ident=ident, shiftI=shiftI, mprev=mprev, mnext=mnext)

def s5_layouts(inp):
    lre = inp['s5_lambda_re'][0]; lim = inp['s5_lambda_im'][0]; lst = inp['s5_log_step'][0]
    lam = np.zeros((2, 3, 128, 64), np.float32)
    for d in range(2):
        lam[d, 0] = np.concatenate([lre[d].T, lre[d].T], 0)
        lam[d, 1] = np.concatenate([lim[d].T, lim[d].T], 0)
        lam[d, 2] = np.broadcast_to(lst[d][None, :], (128, 64))
    br = inp['s5_b_re'][0].transpose(0, 2, 1, 3)
    bi = inp['s5_b_im'][0].transpose(0, 2, 1, 3)
    B12 = np.stack([np.concatenate([br, bi], 1), np.concatenate([bi, br], 1)], 1)
    cr = inp['s5_c_re'][0].transpose(0, 3, 1, 2)
    ci = inp['s5_c_im'][0].transpose(0, 3, 1, 2)
    C1 = np.stack([np.concatenate([cr, ci], 1), np.concatenate([ci, cr], 1)], 1)
    r = np.arange(128)
    J = (r[:, None] % 64 == r[None, :] % 64).astype(np.float32)
    rmask = (r[:, None] // 16 == np.arange(8)[None, :]).astype(np.float32)
    return dict(s5_lam=np.ascontiguousarray(lam), s5_B12=np.ascontiguousarray(B12), s5_C1=np.ascontiguousarray(C1), J128=J, rmask=rmask)


SEQ_FULL = 8192
N_CORES = 8


def build_program(SEQ=SEQ_FULL):
    T = NCTX + SEQ
    nc = bass.Bass("TRN2", target_bir_lowering=False)

    def inp(name, shape, dt=F32):
        return nc.dram_tensor(name, list(shape), dt, kind="ExternalInput").ap()

    h0 = inp("h0", [D, T]); cvec = inp("cvec", [16, 128])
    ident = inp("ident", [128, 128]); shiftI = inp("shiftI", [128, 64]); mprev = inp("mprev", [128, 128]); mnext = inp("mnext", [128, 128])
    modw = inp("mod_w", [2, D, 9 * D]); modb = inp("mod_b", [2, 72, 128]); npre = inp("norm_pre", [2, 24, 128]); npost = inp("norm_post", [2, 24, 128])
    w13 = inp("ffn_w13", [2, 2, D, 2 * DFF]); w2 = inp("ffn_w2", [2, 2, DFF, D])
    w_in = inp("attn_w_in", [1, D, 1184]); qn = inp("mla_q_norm", [2, 128]); wuq = inp("mla_w_uq", [1, 256, 768])
    kvn = inp("mla_kv_norm", [1, 128]); wukv = inp("mla_w_ukv", [1, 128, 1024]); sink = inp("sink128", [128, 8]); wout = inp("attn_w_out", [1, D, D])
    ropeA = inp("ropeA", [32, 2, T]); ropeB = inp("ropeB", [128, 2, T])
    w5in = inp("s5_w_in", [1, D, D]); dsk = inp("s5_d", [8, 128]); wglu = inp("s5_w_glu", [1, D, 2 * D])
    lam = inp("s5_lam", [2, 3, 128, 64]); B12 = inp("s5_B12", [2, 2, 128, 64, 16]); C1 = inp("s5_C1", [2, 2, 128, 64, 16])
    J = inp("J128", [128, 128]); rmask = inp("rmask", [128, 8])
    out = nc.dram_tensor("outT", [D, SEQ], F32, kind="ExternalOutput").ap()

    hA = nc.dram_tensor("hA", [D, T], F32).ap()
    hB = nc.dram_tensor("hB", [D, T], F32).ap()
    p = Prog(nc)
    C = build_consts(p, nc, ident, shiftI, mprev, mnext)
    CO = mod_phase(p, nc, C, cvec, modw, modb, npre, npost, 2)
    chunks = [(0, NCTX, 1)] + [(NCTX + 512 * i, 512, 0) for i in range(SEQ // 512)]
    lat_chunks = chunks[1:]
    g13 = [[(j * 128, 128), (DFF + j * 128, 128)] for j in range(JT)]
    g2 = [[(i * 128, 128)] for i in range(KT)]

    def ffn(l, f, s, h_in, h_out, chs, out_off=0):
        w13s = prep_weight(p, nc, "w13s_%d%d" % (l, f), w13[l, f], D, g13)
        w2s = prep_weight(p, nc, "w2s_%d%d" % (l, f), w2[l, f], DFF, g2)
        ffn_phase(p, nc, C, h_in, h_out, w13s, w2s, CO[l], s, chs, out_off)

    ffn(0, 0, 0, h0, hA, chunks)
    win_s = prep_weight(p, nc, "win_s", w_in[0], D, attn_weight_groups())
    wuq_s = prep_weight(p, nc, "wuq_s", wuq[0], 256, wuq_groups())
    wk_s = prep_weight(p, nc, "wk_s", wukv[0], 128, [[(128 * h, 64) for h in range(8)]])
    wv_s = prep_weight(p, nc, "wv_s", wukv[0], 128, [[(128 * h + 64, 64) for h in range(8)]])
    wout_s = prep_weight(p, nc, "wout_s", wout[0], D, g2)
    scr = dict(QT=nc.dram_tensor("QT", [8, 96, T], BF16).ap(), KT=nc.dram_tensor("KTs", [8, 96, T], BF16).ap(),
               V=nc.dram_tensor("Vs", [T // 128, 128, 8, 128], BF16).ap(), QS=nc.dram_tensor("QS", [4, 128, T], BF16).ap(),
               KS=nc.dram_tensor("KS", [2, 128, T], BF16).ap(), VS=nc.dram_tensor("VS", [T // 128, 128, 2, 128], BF16).ap(),
               OT=nc.dram_tensor("OT", [D, T], BF16).ap())
    attn_proj_phase(p, nc, C, hA, CO[0], chunks, T, win_s, wuq_s, wk_s, wv_s, qn, kvn, ropeA, ropeB, scr)
    mla_phase(p, nc, C, chunks, T, scr)
    swa_phase(p, nc, C, chunks, T, scr, sink)
    mixout_phase(p, nc, C, scr['OT'], wout_s, hA, hB, CO[0], 1, chunks, 8)
    ffn(0, 1, 2, hB, hA, chunks)
    ffn(1, 0, 0, hA, hB, chunks)
    w5_s = prep_weight(p, nc, "w5_s", w5in[0], D, g2)
    wg_s = prep_weight(p, nc, "wg_s", wglu[0], D, [[(i * 128, 128), (D + i * 128, 128)] for i in range(KT)])
    uT = nc.dram_tensor("uT", [D, T], BF16).ap()
    y0 = nc.dram_tensor("y0T", [D, SEQ], F32).ap()
    gT = nc.dram_tensor("gT", [D, SEQ], BF16).ap()
    s5_in_phase(p, nc, C, hB, CO[1], chunks, w5_s, dsk, uT, y0)
    s5_scan_phase(p, nc, C, T, SEQ, uT, y0, gT, lam, B12, C1, J, rmask)
    glu_phase(p, nc, C, gT, wg_s, hB, hA, CO[1], lat_chunks, 0)
    ffn(1, 1, 2, hA, out, lat_chunks, NCTX)
    p.finish()
    return nc


def make_in_maps(inputs, SEQ=SEQ_FULL, n_cores=N_CORES):
    inp = {k: np.asarray(v) for k, v in inputs.items()}
    rA, rB = rope_tables(SEQ, NCTX)
    shared = {"mod_w": inp['mod_w'], "mod_b": inp['mod_b'].reshape(2, 72, 128),
              "norm_pre": inp['norm_pre'].reshape(2, 24, 128), "norm_post": inp['norm_post'].reshape(2, 24, 128),
              "ffn_w13": inp['ffn_w13'], "ffn_w2": inp['ffn_w2'],
              "attn_w_in": inp['attn_w_in'], "mla_q_norm": inp['mla_q_norm'].reshape(2, 128), "mla_w_uq": inp['mla_w_uq'],
              "mla_kv_norm": inp['mla_kv_norm'].reshape(1, 128), "mla_w_ukv": inp['mla_w_ukv'],
              "sink128": np.ascontiguousarray(np.tile(inp['swa_sink'][0][None], (128, 1))), "attn_w_out": inp['attn_w_out'],
              "ropeA": rA, "ropeB": rB,
              "s5_w_in": inp['s5_w_in'], "s5_d": inp['s5_d'].reshape(8, 128), "s5_w_glu": inp['s5_w_glu']}
    shared.update(s5_layouts(inp))
    shared.update(const_inputs())
    shared = {k: np.ascontiguousarray(v, dtype=np.float32) for k, v in shared.items()}
    maps = []
    for b in range(n_cores):
        m = dict(shared)
        m["h0"] = np.ascontiguousarray(np.concatenate([inp['ctx'][b].T, inp['x'][b, :SEQ].T], axis=1), dtype=np.float32)
        m["cvec"] = np.ascontiguousarray(np.concatenate([inp['c'][b].reshape(8, 128), inp['c_ctx'].reshape(8, 128)], 0), dtype=np.float32)
        maps.append(m)
    return maps


def kernel(**inputs):
    nc = build_program(SEQ_FULL)
    maps = make_in_maps(inputs, SEQ_FULL, N_CORES)
    res = run_bass_kernel_spmd(nc, maps, core_ids=list(range(N_CORES)))
    out = np.stack([np.ascontiguousarray(r["outT"].T) for r in res.results], axis=0)
    return out.astype(np.float32)
```

```python
import numpy as np
import concourse.bass as bass
import concourse.mybir as mybir
from concourse.bass_utils import run_bass_kernel_spmd
from contextlib import ExitStack

F32 = mybir.dt.float32
BF16 = mybir.dt.bfloat16
AF = mybir.ActivationFunctionType
ALU = mybir.AluOpType
AX = mybir.AxisListType
ENGS = ['pe', 'act', 'dve', 'pool', 'sp']

D = 1024
KT = 8
DFF = 2816
JT = 22
NCTX = 256
EPS = 1e-6


class Buf:
    _serial = [0]

    def __init__(self, name, t=None):
        Buf._serial[0] += 1
        self.uid = Buf._serial[0]
        self.name = name
        self.t = t
        self.w = None
        self.r = []
        self.dsem = None
        self.dcnt = 0

    def __getitem__(self, idx):
        return self.t[idx]


class Prog:
    def __init__(self, nc):
        self.nc = nc
        self.es = ExitStack()
        self.sem = {e: self.es.enter_context(nc.semaphore("s_" + e)) for e in ENGS}
        self.cnt = {e: 0 for e in ENGS}
        self.ops = {e: [] for e in ENGS}
        self.waited = {e: {} for e in ENGS}
        self.dbufs = []
        self.scopes = []
        self.nm = 0
        self.dsem_pool = []

    def _stack(self):
        return self.scopes[-1] if self.scopes else self.es

    def sb(self, name, shape, dt):
        self.nm += 1
        t = self._stack().enter_context(self.nc.sbuf_tensor("%s_%d" % (name, self.nm), list(shape), dt))
        return Buf(name, t)

    def ps(self, name, shape=(128, 512), dt=F32):
        self.nm += 1
        t = self._stack().enter_context(self.nc.psum_tensor("%s_%d" % (name, self.nm), list(shape), dt))
        return Buf(name, t)

    def view(self, buf, t):
        return Buf(buf.name + "_v", t)

    def views(self, buf, k):
        return [Buf("%s_%d" % (buf.name, i), buf.t[:, i, :]) for i in range(k)]

    def open_scope(self):
        self.scopes.append(ExitStack())

    def close_scope(self):
        self.flush()
        depth = len(self.scopes)
        keep = []
        for b in self.dbufs:
            if getattr(b, 'scope_depth', 0) >= depth:
                self.dsem_pool.append((b.dsem, b.dcnt))
                b.dsem = None
            else:
                keep.append(b)
        self.dbufs = keep
        self.scopes.pop().close()

    def _need(self, eng, dep, waits):
        if dep is None:
            return
        kind, key, val = dep
        if kind == 'e' and key == 'pe' and eng == 'pe':
            return
        k = (kind, key if kind == 'e' else key.uid)
        if self.waited[eng].get(k, 0) >= val:
            return
        self.waited[eng][k] = val
        sem = self.sem[key] if kind == 'e' else key.dsem
        waits.append((sem, val))

    def _deps(self, eng, reads, writes):
        waits = []
        for b in reads:
            self._need(eng, b.w, waits)
        for b in writes:
            self._need(eng, b.w, waits)
            for r in b.r:
                self._need(eng, r, waits)
        return waits

    def op(self, eng, fn, reads=(), writes=(), inc=True):
        waits = self._deps(eng, reads, writes)
        if inc:
            self.cnt[eng] += 1
            tick = ('e', eng, self.cnt[eng])
        else:
            tick = ('e', eng, self.cnt[eng] + 1)
        self.ops[eng].append((waits, fn, (self.sem[eng], 1) if inc else None))
        for b in reads:
            b.r.append(tick)
        for b in writes:
            b.w = tick
            b.r = []
        return tick

    def dma(self, eng, fn, reads=(), writes=(), owner=None):
        waits = self._deps(eng, reads, writes)
        if owner is None:
            owner = (list(writes) + list(reads))[0]
        if owner.dsem is None:
            if self.dsem_pool:
                owner.dsem, owner.dcnt = self.dsem_pool.pop()
            else:
                self.nsem = getattr(self, 'nsem', 0) + 1
                owner.dsem = self.es.enter_context(self.nc.semaphore("d%d" % self.nsem))
            self.dbufs.append(owner)
            owner.scope_depth = len(self.scopes)
        owner.dcnt += 16
        tick = ('d', owner, owner.dcnt)
        self.ops[eng].append((waits, fn, (owner.dsem, 16)))
        for b in reads:
            b.r.append(tick)
        for b in writes:
            b.w = tick
            b.r = []
        return tick

    def barrier(self):
        for e in ENGS:
            waits = []
            for e2 in ENGS:
                if e2 != e and self.cnt[e2] > 0:
                    self._need(e, ('e', e2, self.cnt[e2]), waits)
            for b in self.dbufs:
                self._need(e, ('d', b, b.dcnt), waits)
            if waits:
                self.ops[e].append((waits, None, None))

    def sync_all(self):
        self.barrier()

    def flush(self):
        nc = self.nc
        self.barrier()
        ops = self.ops
        self.ops = {e: [] for e in ENGS}
        self.simulate(ops)

        def run(lst, e):
            for waits, fn, inc in lst:
                for sem, val in waits:
                    e.wait_ge(sem, val)
                if fn is not None:
                    ins = fn(e)
                    if inc is not None:
                        ins.then_inc(inc[0], inc[1])

        with nc.Block() as block:
            @block.tensor
            def _(e):
                run(ops['pe'], e)

            @block.scalar
            def _(e):
                run(ops['act'], e)

            @block.vector
            def _(e):
                run(ops['dve'], e)

            @block.gpsimd
            def _(e):
                run(ops['pool'], e)

            @block.sync
            def _(e):
                run(ops['sp'], e)

    def simulate(self, ops):
        if not hasattr(self, 'simv'):
            self.simv = {}
        ptr = {e: 0 for e in ENGS}
        while True:
            prog = False
            for e in ENGS:
                while ptr[e] < len(ops[e]):
                    waits, fn, inc = ops[e][ptr[e]]
                    if all(self.simv.get(id(s), 0) >= v for s, v in waits):
                        if inc is not None:
                            self.simv[id(inc[0])] = self.simv.get(id(inc[0]), 0) + inc[1]
                        ptr[e] += 1
                        prog = True
                    else:
                        break
            if all(ptr[e] == len(ops[e]) for e in ENGS):
                return
            if not prog:
                for e in ENGS:
                    if ptr[e] < len(ops[e]):
                        waits, fn, inc = ops[e][ptr[e]]
                        print("DEADLOCK", e, ptr[e], [(s, v, self.simv.get(id(s), 0)) for s, v in waits])
                raise RuntimeError("deadlock in sync plan")

    def finish(self):
        self.flush()
        self.es.close()


def mm(p, out, out_ap, lhs, lhs_ap, rhs, rhs_ap, start, stop):
    p.op('pe', lambda e: e.matmul(out_ap, lhs_ap, rhs_ap, start=start, stop=stop),
         reads=[lhs, rhs], writes=[out], inc=stop)


_rr = [0]


def cast_any(p, out, out_ap, in_, in_ap):
    i = _rr[0] % 3
    _rr[0] += 1
    if i == 0:
        p.op('dve', lambda e: e.tensor_copy(out_ap, in_ap), reads=[in_], writes=[out])
    elif i == 1:
        p.op('pool', lambda e: e.tensor_copy(out_ap, in_ap), reads=[in_], writes=[out])
    else:
        p.op('act', lambda e: e.copy(out_ap, in_ap), reads=[in_], writes=[out])


PREP_MAX = 2816


def prep_weight_gen(p, nc, name, W, K, groups, stg):
    ktin = K // 128
    wtot = sum(g[1] for g in groups[0])
    G = len(groups)
    scr = nc.dram_tensor(name, [G, 128, ktin * wtot], BF16).ap()
    Wv = W.rearrange("(kt p) n -> p kt n", p=128)
    sz = ktin * wtot
    assert sz <= PREP_MAX

    def gen():
        for g, grp in enumerate(groups):
            i = stg['cnt'][0] % 2
            stg['cnt'][0] += 1
            fb, bb = stg['f'][i], stg['b'][i]
            ft = fb.t[:, 0:sz].rearrange("p (a b) -> p a b", a=ktin)
            bt = bb.t[:, 0:sz]
            off = 0
            negs = []
            for it in grp:
                c0, w = it[0], it[1]
                sgn = it[2] if len(it) > 2 else 1
                p.dma('sp', lambda e, ft=ft, off=off, c0=c0, w=w: e.dma_start(out=ft[:, :, off:off + w], in_=Wv[:, :, c0:c0 + w]),
                      writes=[fb])
                if sgn < 0:
                    negs.append((off, w))
                off += w
            for (o2, w2) in negs:
                p.op('dve', lambda e, ft=ft, o2=o2, w2=w2: e.tensor_scalar(ft[:, :, o2:o2 + w2], ft[:, :, o2:o2 + w2], -1.0, None, ALU.mult),
                     reads=[fb], writes=[fb])
            cast_any(p, bb, bt, fb, fb.t[:, 0:sz])
            p.dma('pool', lambda e, bt=bt, g=g: e.dma_start(out=scr[g], in_=bt), reads=[bb])
            yield

    return scr, gen()


def prep_staging(p):
    return dict(f=[p.sb("pwf", [128, PREP_MAX], F32) for _ in range(2)],
                b=[p.sb("pwb", [128, PREP_MAX], BF16) for _ in range(2)], cnt=[0])


def prep_weight(p, nc, name, W, K, groups):
    p.open_scope()
    stg = prep_staging(p)
    scr, gen = prep_weight_gen(p, nc, name, W, K, groups, stg)
    for _ in gen:
        pass
    p.close_scope()
    return scr


def advance(bg):
    while bg:
        try:
            next(bg[0])
            return
        except StopIteration:
            bg.pop(0)


def load_cols(p, dst, dst_ap, src_ap, R, ident, ps, stage):
    p.dma('sp', lambda e: e.dma_start(out=stage[0:R, :], in_=src_ap), writes=[stage])
    mm(p, ps, ps[:, 0:R], stage, stage[0:R, :], ident, ident[0:R, 0:R], True, True)
    p.op('dve', lambda e: e.tensor_copy(dst_ap, ps[:, 0:R]), reads=[ps], writes=[dst])


def build_consts(p, nc, ident_d, shift_d=None, mprev_d=None, mnext_d=None):
    c = {}
    if shift_d is not None:
        c['shiftI'] = p.sb("shiftI", [128, 64], F32)
        p.dma('sp', lambda e: e.dma_start(out=c['shiftI'][:], in_=shift_d), writes=[c['shiftI']])
        c['mprev'] = p.sb("mprev", [128, 128], F32)
        p.dma('sp', lambda e: e.dma_start(out=c['mprev'][:], in_=mprev_d), writes=[c['mprev']])
        c['mnext'] = p.sb("mnext", [128, 128], F32)
        p.dma('sp', lambda e: e.dma_start(out=c['mnext'][:], in_=mnext_d), writes=[c['mnext']])
    c['ident'] = p.sb("ident", [128, 128], F32)
    p.dma('sp', lambda e: e.dma_start(out=c['ident'][:], in_=ident_d), writes=[c['ident']])
    c['ones_bf'] = p.sb("ones_bf", [128, 128], BF16)
    p.op('dve', lambda e: e.memset(c['ones_bf'][:], 1.0), writes=[c['ones_bf']])
    c['eps'] = p.sb("epsc", [128, 1], F32)
    p.op('dve', lambda e: e.memset(c['eps'][:], EPS), writes=[c['eps']])
    return c


def mod_phase(p, nc, C, cvec_d, modw_d, modb_d, npre_d, npost_d, L):
    CO = [p.sb("CO%d" % l, [128, 3, 3, 2, 8], F32) for l in range(L)]
    p.open_scope()
    ident = C['ident']
    stage = p.sb("stage", [128, 128], F32)
    pst = p.ps("pst")
    cT = p.sb("cT", [128, 16], F32)
    load_cols(p, cT, cT[:], cvec_d, 16, ident, pst, stage)
    sc = p.sb("sc", [128, 8, 2], F32)
    for kind in range(2):
        p.op('act', lambda e, kind=kind: e.activation(sc[:, :, kind], cT[:, kind * 8:(kind + 1) * 8], AF.Silu),
             reads=[cT], writes=[sc])
    wbuf = [p.sb("mwb", [128, 8, 1024], F32) for _ in range(2)]
    psM = p.ps("psM")
    M = p.sb("M", [128, 72, 2], F32)
    mb = p.sb("mb", [128, 72], F32)
    gpre = p.sb("gpre", [128, 24], F32)
    gpost = p.sb("gpost", [128, 24], F32)
    t8 = p.sb("t8", [128, 8], F32)
    for l in range(L):
        mwv = modw_d[l].rearrange("(kt p) n -> p kt n", p=128)
        for j in range(9):
            wb = wbuf[(l * 9 + j) % 2]
            p.dma('sp', lambda e, wb=wb, j=j, mwv=mwv: e.dma_start(out=wb[:], in_=mwv[:, :, j * 1024:(j + 1) * 1024]), writes=[wb])
            for q in range(8):
                col = (j * 8 + q) * 2
                for kt in range(8):
                    mm(p, psM, psM[:, col:col + 2], wb, wb[:, kt, q * 128:(q + 1) * 128], sc, sc[:, kt, :], kt == 0, kt == 7)
        load_cols(p, mb, mb[:], modb_d[l], 72, ident, pst, stage)
        load_cols(p, gpre, gpre[:], npre_d[l], 24, ident, pst, stage)
        load_cols(p, gpost, gpost[:], npost_d[l], 24, ident, pst, stage)
        for kind in range(2):
            p.op('dve', lambda e, kind=kind: e.tensor_tensor(M[:, :, kind], psM[:, 0:144].rearrange("p (q k) -> p q k", k=2)[:, :, kind], mb[:], ALU.add),
                 reads=[psM, mb], writes=[M])
        co = CO[l]
        for s in range(3):
            coef = 1.0 if s == 1 else 0.5
            for kind in range(2):
                p.op('dve', lambda e, s=s, kind=kind: e.tensor_scalar(t8[:], M[:, (3 * s + 1) * 8:(3 * s + 2) * 8, kind], 1.0, None, ALU.add),
                     reads=[M], writes=[t8])
                p.op('dve', lambda e, s=s, kind=kind, co=co: e.tensor_tensor(co[:, s, 0, kind, :], t8[:], gpre[:, s * 8:(s + 1) * 8], ALU.mult),
                     reads=[t8, gpre], writes=[co])
                p.op('dve', lambda e, s=s, kind=kind, co=co: e.tensor_copy(co[:, s, 1, kind, :], M[:, (3 * s) * 8:(3 * s + 1) * 8, kind]),
                     reads=[M], writes=[co])
                p.op('dve', lambda e, s=s, kind=kind, coef=coef: e.tensor_scalar(t8[:], M[:, (3 * s + 2) * 8:(3 * s + 3) * 8, kind], coef, None, ALU.mult),
                     reads=[M], writes=[t8])
                p.op('dve', lambda e, s=s, kind=kind, co=co: e.tensor_tensor(co[:, s, 2, kind, :], t8[:], gpost[:, s * 8:(s + 1) * 8], ALU.mult),
                     reads=[t8, gpost], writes=[co])
    p.close_scope()
    return CO


def norm_stats(p, C, sq, nk, n, pstat, rstd, inv_dim):
    for kt in range(nk):
        mm(p, pstat, pstat[:, :n], C['ones_bf'], C['ones_bf'][:], sq[kt], sq[kt][:, :n], kt == 0, kt == nk - 1)
    p.op('act', lambda e: e.activation(rstd[:, :n], pstat[:, :n], AF.Sqrt, bias=C['eps'][:, 0:1], scale=inv_dim),
         reads=[pstat, C['eps']], writes=[rstd])
    p.op('dve', lambda e: e.reciprocal(rstd[:, :n], rstd[:, :n]), reads=[rstd], writes=[rstd])


def modulate(p, C, hT, n, co, s, kind, sq, pstat, rstd, tmps, aT):
    for kt in range(KT):
        p.op('act', lambda e, kt=kt: e.activation(sq[kt][:, :n], hT[kt][:, :n], AF.Square), reads=[hT[kt]], writes=[sq[kt]])
    norm_stats(p, C, sq, KT, n, pstat, rstd, 1.0 / D)
    for kt in range(KT):
        tmp = tmps[kt % len(tmps)]
        p.op('dve', lambda e, kt=kt, tmp=tmp: e.tensor_tensor(tmp[:, :n], hT[kt][:, :n], rstd[:, :n], ALU.mult),
             reads=[hT[kt], rstd], writes=[tmp])
        p.op('pool', lambda e, kt=kt, tmp=tmp: e.tensor_scalar(aT[kt][:, :n], tmp[:, :n], co[:, s, 0, kind, kt:kt + 1], co[:, s, 1, kind, kt:kt + 1], ALU.mult, ALU.add),
             reads=[tmp, co], writes=[aT[kt]])


def post_residual(p, C, hT, ysb, n, co, s, kind, sq, pstat, rstd, tmps):
    norm_stats(p, C, sq, KT, n, pstat, rstd, 1.0 / D)
    for kt in range(KT):
        tmp = tmps[kt % len(tmps)]
        p.op('pool', lambda e, kt=kt, tmp=tmp: e.tensor_tensor(tmp[:, :n], ysb[kt][:, :n], rstd[:, :n], ALU.mult),
             reads=[ysb[kt], rstd], writes=[tmp])
        p.op('dve', lambda e, kt=kt, tmp=tmp: e.scalar_tensor_tensor(hT[kt][:, :n], tmp[:, :n], co[:, s, 2, kind, kt:kt + 1], hT[kt][:, :n], ALU.mult, ALU.add),
             reads=[tmp, co, hT[kt]], writes=[hT[kt]])


def ffn_phase(p, nc, C, h_in, h_out, w13s, w2s, co, s, chunks, out_off=0, bg_fn=None):
    p.open_scope()
    NT = 512
    hTf = [p.sb("hT", [128, KT, NT], F32) for _ in range(2)]
    hTs = [p.views(b, KT) for b in hTf]
    aTs = [p.views(p.sb("aT", [128, KT, NT], BF16), KT) for _ in range(2)]
    sq = p.views(p.sb("sq", [128, KT, NT], BF16), KT)
    rstd = p.sb("rstd", [128, NT], F32)
    rstd2 = p.sb("rstd2", [128, NT], F32)
    tmps = [p.sb("tmp", [128, NT], F32) for _ in range(3)]
    sgs = [p.sb("sg", [128, NT], F32) for _ in range(2)]
    HT = p.sb("HT", [128, JT, NT], BF16)
    HTj = [p.view(HT, HT.t[:, j, :]) for j in range(JT)]
    ysb = p.views(p.sb("ysb", [128, KT, NT], F32), KT)
    w13b = [p.sb("w13b", [128, KT, 256], BF16) for _ in range(3)]
    w2b = [p.sb("w2b", [128, JT, 128], BF16) for _ in range(2)]
    pstat = p.ps("pstat")
    pg = [p.ps("pg") for _ in range(2)]
    pu = [p.ps("pu") for _ in range(2)]
    py = [p.ps("py") for _ in range(2)]
    hin_v = h_in.rearrange("(kt p) t -> p kt t", p=128)
    hout_v = h_out.rearrange("(kt p) t -> p kt t", p=128)
    bg = bg_fn(prep_staging(p)) if bg_fn is not None else []

    def pre(ci):
        t0, n, kind = chunks[ci]
        hT = hTs[ci % 2]
        hf = hTf[ci % 2]
        p.dma('sp', lambda e: e.dma_start(out=hf[:, :, :n], in_=hin_v[:, :, t0:t0 + n]), writes=hT)
        modulate(p, C, hT, n, co, s, kind, sq, pstat, rstd, tmps, aTs[ci % 2])

    wcnt = [0, 0]
    pre(0)
    for ci, (t0, n, kind) in enumerate(chunks):
        hT = hTs[ci % 2]
        aT = aTs[ci % 2]
        for j in range(JT):
            wb = w13b[wcnt[0] % 3]
            wcnt[0] += 1
            p.dma('sp', lambda e, wb=wb, j=j: e.dma_start(out=wb[:].rearrange("p a b -> p (a b)"), in_=w13s[j]), writes=[wb])
            g = pg[j % 2]
            u = pu[j % 2]
            if j % 2 == 0:
                advance(bg)
            for kt in range(KT):
                mm(p, g, g[:, :n], wb, wb[:, kt, 0:128], aT[kt], aT[kt][:, :n], kt == 0, kt == KT - 1)
            for kt in range(KT):
                mm(p, u, u[:, :n], wb, wb[:, kt, 128:256], aT[kt], aT[kt][:, :n], kt == 0, kt == KT - 1)
            sg = sgs[j % 2]
            p.op('act', lambda e, g=g, sg=sg: e.activation(sg[:, :n], g[:, :n], AF.Silu), reads=[g], writes=[sg])
            p.op('dve', lambda e, u=u, sg=sg, j=j: e.tensor_tensor(HT[:, j, :n], sg[:, :n], u[:, :n], ALU.mult),
                 reads=[sg, u], writes=[HTj[j]])
        if ci + 1 < len(chunks):
            pre(ci + 1)
        for i in range(KT):
            wb = w2b[wcnt[1] % 2]
            wcnt[1] += 1
            p.dma('sp', lambda e, wb=wb, i=i: e.dma_start(out=wb[:].rearrange("p a b -> p (a b)"), in_=w2s[i]), writes=[wb])
            y = py[i % 2]
            for j in range(JT):
                mm(p, y, y[:, :n], wb, wb[:, j, :], HTj[j], HT[:, j, :n], j == 0, j == JT - 1)
            p.op('dve', lambda e, y=y, i=i: e.tensor_copy(ysb[i][:, :n], y[:, :n]), reads=[y], writes=[ysb[i]])
            p.op('act', lambda e, i=i: e.activation(sq[i][:, :n], ysb[i][:, :n], AF.Square), reads=[ysb[i]], writes=[sq[i]])
        post_residual(p, C, hT, ysb, n, co, s, kind, sq, pstat, rstd2, tmps)
        hf = hTf[ci % 2]
        p.dma('pool', lambda e, hf=hf, t0=t0, n=n: e.dma_start(out=hout_v[:, :, t0 - out_off:t0 - out_off + n], in_=hf[:, :, :n]), reads=hT)
    while bg:
        advance(bg)
    p.close_scope()


MLA_SCALE = 96 ** -0.5
SWA_SCALE = 64 ** -0.5
W_CQ, W_CKV, W_KPE, W_QS, W_KS, W_VS = 0, 256, 384, 416, 928, 1056


def attn_weight_groups():
    g = []
    g.append([(0, 128)])
    g.append([(128, 128)])
    g.append([(256, 128)])
    g.append([(320, 96), (0, 32)])
    g.append([(320, 64), (400, 16, -1), (384, 16), (0, 32)])
    for i in range(4):
        g.append([(W_QS + 128 * i, 128)])
    for i in range(4):
        b = W_QS + 128 * i
        g.append([(b + 32, 32, -1), (b, 32), (b + 96, 32, -1), (b + 64, 32)])
    for kv in range(2):
        b = W_KS + 64 * kv
        g.append([(b, 64), (b, 64)])
    for kv in range(2):
        b = W_KS + 64 * kv
        g.append([(b + 32, 32, -1), (b, 32), (b + 32, 32, -1), (b, 32)])
    g.append([(W_VS, 128)])
    return g


def wuq_groups():
    g = []
    for h in range(8):
        g.append([(96 * h, 96), (0, 32)])
    for h in range(8):
        g.append([(96 * h, 64), (96 * h + 80, 16, -1), (96 * h + 64, 16), (0, 32)])
    return g


def attn_proj_phase(p, nc, C, h_in, co, chunks, T, win_s, wuq_s, wk_s, wv_s, qn_d, kvn_d, ropeA, ropeB, scr):
    p.open_scope()
    NT = 512
    ident = C['ident']
    hTf = [p.sb("hT", [128, KT, NT], F32) for _ in range(2)]
    hTs = [p.views(b, KT) for b in hTf]
    aT = p.views(p.sb("aT", [128, KT, NT], BF16), KT)
    sq = p.views(p.sb("sq", [128, KT, NT], BF16), KT)
    rstd = p.sb("rstd", [128, NT], F32)
    rq = p.sb("rq", [128, NT], F32)
    rkv = p.sb("rkv", [128, NT], F32)
    tmps = [p.sb("tmp", [128, NT], F32) for _ in range(3)]
    pstat = p.ps("pstat")
    pa = [p.ps("pa") for _ in range(3)]
    pb = [p.ps("pb") for _ in range(3)]
    win = p.sb("win", [128, 18, KT, 128], BF16)
    p.dma('sp', lambda e: e.dma_start(out=win[:].rearrange("p g a b -> p g (a b)"), in_=win_s.rearrange("g p x -> p g x")), writes=[win])
    wuq = p.sb("wuq", [128, 16, 2, 128], BF16)
    p.dma('sp', lambda e: e.dma_start(out=wuq[:].rearrange("p g a b -> p g (a b)"), in_=wuq_s.rearrange("g p x -> p g x")), writes=[wuq])
    wk = p.sb("wk", [128, 512], BF16)
    p.dma('sp', lambda e: e.dma_start(out=wk[:], in_=wk_s[0]), writes=[wk])
    wv = p.sb("wv", [128, 512], BF16)
    p.dma('sp', lambda e: e.dma_start(out=wv[:], in_=wv_s[0]), writes=[wv])
    stage = p.sb("stage", [128, 128], F32)
    qn = p.sb("qn", [128, 2], F32)
    kvn = p.sb("kvn", [128, 1], F32)
    load_cols(p, qn, qn[:], qn_d, 2, ident, pstat, stage)
    load_cols(p, kvn, kvn[:], kvn_d, 1, ident, pstat, stage)
    cqn = p.views(p.sb("cqn", [128, 2, NT], BF16), 2)
    ckvn = p.sb("ckvn", [128, NT], BF16)
    tA = p.sb("tA", [128, 2, NT], F32)
    tB = p.sb("tB", [128, 2, NT], F32)
    qsb = p.sb("qsb", [128, NT], F32)
    r1 = p.sb("r1", [128, NT], F32)
    r2 = p.sb("r2", [128, NT], F32)
    r3 = p.sb("r3", [128, NT], F32)
    Qst = [p.sb("Qst", [96, NT], BF16) for _ in range(2)]
    Kst = p.sb("Kst", [96, 8, NT], BF16)
    kpe = p.sb("kpe", [96, NT], BF16)
    Vst = p.sb("Vst", [128, 4, 8, 128], BF16)
    VSst = p.sb("VSst", [128, 4, 2, 128], BF16)
    p.op('pool', lambda e: e.memset(Vst[:], 1.0), writes=[Vst])
    p.op('pool', lambda e: e.memset(VSst[:], 1.0), writes=[VSst])
    Sst = [p.sb("Sst", [128, NT], BF16) for _ in range(2)]
    hin_v = h_in.rearrange("(kt p) t -> p kt t", p=128)

    def proj(dst_ps, gidx, n):
        for kt in range(KT):
            mm(p, dst_ps, dst_ps[:, :n], win, win[:, gidx, kt, :], aT[kt], aT[kt][:, :n], kt == 0, kt == KT - 1)

    def rope_rows(ps_x, ps_r, tab, lo, hi, n, out_buf, out_ap, scale):
        p.op('dve', lambda e: e.tensor_tensor(r1[lo:hi, :n], ps_x[lo:hi, :n], tab[lo:hi, 0, :n], ALU.mult), reads=[ps_x, tab], writes=[r1])
        p.op('dve', lambda e: e.tensor_tensor(r2[lo:hi, :n], ps_r[lo:hi, :n], tab[lo:hi, 1, :n], ALU.mult), reads=[ps_r, tab], writes=[r2])
        p.op('pool', lambda e: e.tensor_tensor(r3[lo:hi, :n], r1[lo:hi, :n], r2[lo:hi, :n], ALU.add), reads=[r1, r2], writes=[r3])
        p.op('act', lambda e: e.activation(out_ap, r3[lo:hi, :n], AF.Copy, scale=scale), reads=[r3], writes=[out_buf])

    for ci, (t0, n, kind) in enumerate(chunks):
        hT = hTs[ci % 2]
        hf = hTf[ci % 2]
        p.dma('sp', lambda e, hf=hf, t0=t0, n=n: e.dma_start(out=hf[:, :, :n], in_=hin_v[:, :, t0:t0 + n]), writes=hT)
        p.dma('sp', lambda e, t0=t0, n=n: e.dma_start(out=tA[64:96, :, :n], in_=ropeA[:, :, t0:t0 + n]), writes=[tA])
        p.dma('sp', lambda e, t0=t0, n=n: e.dma_start(out=tB[:, :, :n], in_=ropeB[:, :, t0:t0 + n]), writes=[tB])
        modulate(p, C, hT, n, co, 1, kind, sq, pstat, rstd, tmps, aT)
        for i in range(2):
            proj(pa[i], i, n)
            p.op('dve', lambda e, i=i: e.tensor_copy(tmps[i][:, :n], pa[i][:, :n]), reads=[pa[i]], writes=[tmps[i]])
            p.op('act', lambda e, i=i: e.activation(sq[i][:, :n], tmps[i][:, :n], AF.Square), reads=[tmps[i]], writes=[sq[i]])
        norm_stats(p, C, sq, 2, n, pstat, rq, 1.0 / 256)
        for i in range(2):
            p.op('dve', lambda e, i=i: e.tensor_tensor(tmps[i][:, :n], tmps[i][:, :n], rq[:, :n], ALU.mult), reads=[tmps[i], rq], writes=[tmps[i]])
            p.op('pool', lambda e, i=i: e.tensor_scalar(cqn[i][:, :n], tmps[i][:, :n], qn[:, i:i + 1], None, ALU.mult), reads=[tmps[i], qn], writes=[cqn[i]])
        proj(pa[2], 2, n)
        p.op('dve', lambda e: e.tensor_copy(tmps[2][:, :n], pa[2][:, :n]), reads=[pa[2]], writes=[tmps[2]])
        p.op('act', lambda e: e.activation(sq[2][:, :n], tmps[2][:, :n], AF.Square), reads=[tmps[2]], writes=[sq[2]])
        norm_stats(p, C, sq[2:3], 1, n, pstat, rkv, 1.0 / 128)
        p.op('dve', lambda e: e.tensor_tensor(tmps[2][:, :n], tmps[2][:, :n], rkv[:, :n], ALU.mult), reads=[tmps[2], rkv], writes=[tmps[2]])
        p.op('pool', lambda e: e.tensor_scalar(ckvn[:, :n], tmps[2][:, :n], kvn[:, 0:1], None, ALU.mult), reads=[tmps[2], kvn], writes=[ckvn])
        proj(pa[0], 3, n)
        proj(pb[0], 4, n)
        rope_rows(pa[0], pb[0], tA, 64, 96, n, kpe, kpe[64:96, :n], 1.0)
        for h in range(8):
            pk = pa[1 + h % 2]
            mm(p, pk, pk[0:64, :n], wk, wk[:, 64 * h:64 * h + 64], ckvn, ckvn[:, :n], True, True)
            p.op('dve', lambda e, pk=pk, h=h: e.tensor_copy(Kst[0:64, h, :n], pk[0:64, :n]), reads=[pk], writes=[Kst])
            p.op('pool', lambda e, h=h: e.tensor_copy(Kst[64:96, h, :n], kpe[64:96, :n]), reads=[kpe], writes=[Kst])
        p.dma('pool', lambda e, t0=t0, n=n: e.dma_start(out=scr['KT'].rearrange("h r t -> r h t")[:, :, t0:t0 + n], in_=Kst[:, :, :n]), reads=[Kst])
        for h in range(8):
            pq = pa[h % 2]
            pqr = pb[h % 2]
            Q = Qst[h % 2]
            for kt in range(2):
                mm(p, pq, pq[:, :n], wuq, wuq[:, h, kt, :], cqn[kt], cqn[kt][:, :n], kt == 0, kt == 1)
            for kt in range(2):
                mm(p, pqr, pqr[:, :n], wuq, wuq[:, 8 + h, kt, :], cqn[kt], cqn[kt][:, :n], kt == 0, kt == 1)
            p.op('dve', lambda e, pq=pq: e.tensor_copy(qsb[0:64, :n], pq[0:64, :n]), reads=[pq], writes=[qsb])
            rope_rows(pq, pqr, tA, 64, 96, n, Q, Q[64:96, :n], MLA_SCALE)
            p.op('act', lambda e, Q=Q: e.activation(Q[0:64, :n], qsb[0:64, :n], AF.Copy, scale=MLA_SCALE), reads=[qsb], writes=[Q])
            p.dma('pool', lambda e, Q=Q, h=h, t0=t0, n=n: e.dma_start(out=scr['QT'][h, :, t0:t0 + n], in_=Q[:, :n]), reads=[Q])
        nb = n // 128
        for tb in range(nb):
            pv = pa[tb % 2]
            mm(p, pv, pv[:, 0:512], ckvn, ckvn[:, tb * 128:(tb + 1) * 128], wv, wv[:, :], True, True)
            p.op('dve', lambda e, pv=pv, tb=tb: e.tensor_copy(Vst[:, tb, :, 0:64], pv[:, 0:512].rearrange("p (h d) -> p h d", d=64)), reads=[pv], writes=[Vst])
        p.dma('pool', lambda e, t0=t0, nb=nb: e.dma_start(out=scr['V'][t0 // 128:t0 // 128 + nb].rearrange("b p h d -> p b h d"), in_=Vst[:, 0:nb]), reads=[Vst])
        for i in range(4):
            proj(pa[i % 2], 5 + i, n)
            proj(pb[i % 2], 9 + i, n)
            S = Sst[i % 2]
            rope_rows(pa[i % 2], pb[i % 2], tB, 0, 128, n, S, S[:, :n], SWA_SCALE)
            p.dma('pool', lambda e, S=S, i=i, t0=t0, n=n: e.dma_start(out=scr['QS'][i, :, t0:t0 + n], in_=S[:, :n]), reads=[S])
        for kv in range(2):
            proj(pa[kv], 13 + kv, n)
            proj(pb[kv], 15 + kv, n)
            S = Sst[kv]
            rope_rows(pa[kv], pb[kv], tB, 0, 128, n, S, S[:, :n], 1.0)
            p.dma('pool', lambda e, S=S, kv=kv, t0=t0, n=n: e.dma_start(out=scr['KS'][kv, :, t0:t0 + n], in_=S[:, :n]), reads=[S])
        for tb in range(nb):
            pv = pa[2]
            for kt in range(KT):
                mm(p, pv, pv[:, 0:128], aT[kt], aT[kt][:, tb * 128:(tb + 1) * 128], win, win[:, 17, kt, :], kt == 0, kt == KT - 1)
            p.op('dve', lambda e, pv=pv, tb=tb: e.tensor_copy(VSst[:, tb, :, 0:64], pv[:, 0:128].rearrange("p (h d) -> p h d", d=64)), reads=[pv], writes=[VSst])
        p.dma('pool', lambda e, t0=t0, nb=nb: e.dma_start(out=scr['VS'][t0 // 128:t0 // 128 + nb].rearrange("b p h d -> p b h d"), in_=VSst[:, 0:nb]), reads=[VSst])
    p.close_scope()


def normalize_store(p, C, pO, n, Osb, pR, On, dst_ap, extra_den=None, act_recip=False):
    p.op('dve', lambda e: e.tensor_copy(Osb[:, :n], pO[:, :n]), reads=[pO], writes=[Osb])
    if extra_den is not None:
        p.op('dve', lambda e: e.tensor_scalar(Osb[64:128, :n], Osb[64:128, :n], extra_den, None, ALU.add), reads=[Osb], writes=[Osb])
    if act_recip:
        p.op('act', lambda e: e.activation(Osb[64:128, :n], Osb[64:128, :n], AF.Ln), reads=[Osb], writes=[Osb])
        p.op('act', lambda e: e.activation(Osb[64:128, :n], Osb[64:128, :n], AF.Exp, scale=-1.0), reads=[Osb], writes=[Osb])
    else:
        p.op('dve', lambda e: e.reciprocal(Osb[64:128, :n], Osb[64:128, :n]), reads=[Osb], writes=[Osb])
    mm(p, pR, pR[0:64, :n], C['shiftI'], C['shiftI'][:, :], Osb, Osb[:, :n], True, True)
    p.op('dve', lambda e: e.tensor_tensor(On[0:64, :n], Osb[0:64, :n], pR[0:64, :n], ALU.mult), reads=[Osb, pR], writes=[On])
    p.dma('pool', lambda e: e.dma_start(out=dst_ap, in_=On[0:64, :n]), reads=[On])


def mla_phase(p, nc, C, chunks, T, scr):
    p.open_scope()
    NT = 512
    G3 = 3
    NKT = T // 128
    Kh = [p.sb("Kh", [96, T], BF16) for _ in range(2)]
    Vh = [p.sb("Vh", [128, NKT, 128], BF16) for _ in range(2)]
    Qc = [p.sb("Qc", [96, NT], BF16) for _ in range(2)]
    Pt = [p.sb("Pt", [128, G3, NT], BF16) for _ in range(2)]
    Osb = [p.sb("Osb", [128, NT], F32) for _ in range(2)]
    On = [p.sb("On", [64, NT], BF16) for _ in range(2)]
    pS = [p.ps("pS", (128, G3, NT)) for _ in range(2)]
    pO = p.ps("pO")
    pR = p.ps("pR")
    qi = 0
    for h in range(8):
        K = Kh[h % 2]
        V = Vh[h % 2]
        p.dma('sp', lambda e, K=K, h=h: e.dma_start(out=K[:], in_=scr['KT'][h]), writes=[K])
        p.dma('sp', lambda e, V=V, h=h: e.dma_start(out=V[:], in_=scr['V'][:, :, h, :].rearrange("b p d -> p b d")), writes=[V])
        for ci, (t0, n, kind) in enumerate(chunks):
            Q = Qc[qi % 2]
            on = On[qi % 2]
            osb = Osb[qi % 2]
            qi += 1
            p.dma('sp', lambda e, Q=Q, h=h, t0=t0, n=n: e.dma_start(out=Q[:, :n], in_=scr['QT'][h, :, t0:t0 + n]), writes=[Q])
            kts = list(range(2)) if kind == 1 else list(range(NKT))
            groups = [kts[i:i + G3] for i in range(0, len(kts), G3)]

            def smm(gi):
                ps = pS[gi % 2]
                for j, kt in enumerate(groups[gi]):
                    mm(p, ps, ps[:, j, :n], K, K[0:96, kt * 128:(kt + 1) * 128], Q, Q[0:96, :n], True, True)

            smm(0)
            nmm = len(kts)
            done = 0
            for gi, grp in enumerate(groups):
                if gi + 1 < len(groups):
                    smm(gi + 1)
                ps = pS[gi % 2]
                pt = Pt[gi % 2]
                ng = len(grp)
                p.op('act', lambda e, ps=ps, pt=pt, ng=ng, n=n: e.activation(pt[:, 0:ng, :n], ps[:, 0:ng, :n], AF.Exp), reads=[ps], writes=[pt])
                for j, kt in enumerate(grp):
                    mm(p, pO, pO[:, :n], V, V[:, kt, :], pt, pt[:, j, :n], done == 0, done == nmm - 1)
                    done += 1
            normalize_store(p, C, pO, n, osb, pR, on, scr['OT'][h * 64:(h + 1) * 64, t0:t0 + n])
    p.close_scope()


def swa_phase(p, nc, C, chunks, T, scr, sink_d):
    p.open_scope()
    NT = 512
    NKT = T // 128
    KS = p.sb("KS", [128, 2, T], BF16)
    p.dma('sp', lambda e: e.dma_start(out=KS[:], in_=scr['KS'].rearrange("k p t -> p k t")), writes=[KS])
    VS = p.sb("VS", [128, NKT, 2, 128], BF16)
    p.dma('sp', lambda e: e.dma_start(out=VS[:], in_=scr['VS'].rearrange("b p k d -> p b k d")), writes=[VS])
    esink = p.sb("esink", [128, 8], F32)
    p.dma('sp', lambda e: e.dma_start(out=esink[:], in_=sink_d), writes=[esink])
    p.op('act', lambda e: e.activation(esink[:], esink[:], AF.Exp), reads=[esink], writes=[esink])
    QS = [p.sb("QS", [128, 4, NT], BF16) for _ in range(2)]
    Pc4 = [p.sb("Pc", [128, NT], BF16) for _ in range(4)]
    Pl8 = [p.sb("Pl", [128, 384], BF16) for _ in range(8)]
    Osb2 = [p.sb("Osb", [128, NT], F32) for _ in range(2)]
    On = [p.sb("On", [64, NT], BF16) for _ in range(2)]
    pC = [p.ps("pC") for _ in range(2)]
    pL = [p.ps("pL") for _ in range(2)]
    pO = [p.ps("pO") for _ in range(2)]
    pR = p.ps("pR")
    mprev = C['mprev']
    mnext = C['mnext']
    oi = 0
    for ci, (t0, n, kind) in enumerate(chunks):
        Q = QS[ci % 2]
        p.dma('sp', lambda e, Q=Q, t0=t0, n=n: e.dma_start(out=Q[:, :, :n], in_=scr['QS'].rearrange("i p t -> p i t")[:, :, t0:t0 + n]), writes=[Q])
        nb = n // 128
        for hh in range(8):
            i, e2 = hh // 2, hh % 2
            kv = hh // 4
            lo, hi = 64 * e2, 64 * e2 + 64
            po = pO[oi % 2]
            on = On[oi % 2]
            Osb = Osb2[oi % 2]
            Pc = Pc4[2 * (oi % 2):2 * (oi % 2) + 2]
            Pl = Pl8[4 * (oi % 2):4 * (oi % 2) + 4]
            oi += 1
            for c2 in range(2):
                mm(p, pC[c2], pC[c2][:, :n], KS, KS[lo:hi, kv, c2 * 128:(c2 + 1) * 128], Q, Q[lo:hi, i, :n], True, True)
                p.op('act', lambda e, c2=c2, Pc=Pc, n=n: e.activation(Pc[c2][:, :n], pC[c2][:, :n], AF.Exp), reads=[pC[c2]], writes=[Pc[c2]])
            loc = []
            if kind == 0:
                for qb in range(nb):
                    kt_c = (t0 // 128) + qb
                    tiles = [(kt_c - 1, 0), (kt_c, 1), (kt_c + 1, 2)]
                    tiles = [(kt, s) for kt, s in tiles if 2 <= kt < NKT]
                    pl = pL[qb % 2]
                    P = Pl[qb]
                    for kt, s in tiles:
                        mm(p, pl, pl[:, s * 128:(s + 1) * 128], KS, KS[lo:hi, kv, kt * 128:(kt + 1) * 128], Q, Q[lo:hi, i, qb * 128:(qb + 1) * 128], True, True)
                    c0, c1 = tiles[0][1] * 128, tiles[-1][1] * 128 + 128
                    p.op('act', lambda e, pl=pl, P=P, c0=c0, c1=c1: e.activation(P[:, c0:c1], pl[:, c0:c1], AF.Exp), reads=[pl], writes=[P])
                    for kt, s in tiles:
                        if s == 0:
                            p.op('dve', lambda e, P=P: e.tensor_tensor(P[:, 0:128], P[:, 0:128], mprev[:], ALU.mult), reads=[P, mprev], writes=[P])
                        if s == 2:
                            p.op('dve', lambda e, P=P: e.tensor_tensor(P[:, 256:384], P[:, 256:384], mnext[:], ALU.mult), reads=[P, mnext], writes=[P])
                    loc.append(tiles)
            for qb in range(nb):
                items = [(0, Pc[0], qb * 128), (1, Pc[1], qb * 128)]
                if kind == 0:
                    for kt, s in loc[qb]:
                        items.append((kt, Pl[qb], s * 128))
                for k2, (kt, P, c0) in enumerate(items):
                    mm(p, po, po[:, qb * 128:(qb + 1) * 128], VS, VS[:, kt, kv, :], P, P[:, c0:c0 + 128], k2 == 0, k2 == len(items) - 1)
            normalize_store(p, C, po, n, Osb, pR, on, scr['OT'][(8 + hh) * 64:(9 + hh) * 64, t0:t0 + n], extra_den=esink[64:128, hh:hh + 1], act_recip=True)
    p.close_scope()


def mixout_phase(p, nc, C, y_src, wout_s, h_in, h_out, co, s, chunks, nkt_in):
    p.open_scope()
    NT = 512
    hTf = [p.sb("hT", [128, KT, NT], F32) for _ in range(2)]
    hTs = [p.views(b, KT) for b in hTf]
    yTf = [p.sb("yT", [128, nkt_in, NT], BF16) for _ in range(2)]
    sq = p.views(p.sb("sq", [128, KT, NT], BF16), KT)
    rstd2 = p.sb("rstd2", [128, NT], F32)
    tmps = [p.sb("tmp", [128, NT], F32) for _ in range(3)]
    ysb = p.views(p.sb("ysb", [128, KT, NT], F32), KT)
    wo = p.sb("wo", [128, KT, nkt_in, 128], BF16)
    p.dma('sp', lambda e: e.dma_start(out=wo[:].rearrange("p i a b -> p i (a b)"), in_=wout_s.rearrange("i p x -> p i x")), writes=[wo])
    pstat = p.ps("pstat")
    py = [p.ps("py") for _ in range(2)]
    hin_v = h_in.rearrange("(kt p) t -> p kt t", p=128)
    hout_v = h_out.rearrange("(kt p) t -> p kt t", p=128)
    ysrc_v = y_src.rearrange("(kt p) t -> p kt t", p=128)
    for ci, (t0, n, kind) in enumerate(chunks):
        hT = hTs[ci % 2]
        hf = hTf[ci % 2]
        yT = yTf[ci % 2]
        p.dma('sp', lambda e, hf=hf, t0=t0, n=n: e.dma_start(out=hf[:, :, :n], in_=hin_v[:, :, t0:t0 + n]), writes=hT)
        p.dma('sp', lambda e, yT=yT, t0=t0, n=n: e.dma_start(out=yT[:, :, :n], in_=ysrc_v[:, :, t0:t0 + n]), writes=[yT])
        for i in range(KT):
            y = py[i % 2]
            for j in range(nkt_in):
                mm(p, y, y[:, :n], wo, wo[:, i, j, :], yT, yT[:, j, :n], j == 0, j == nkt_in - 1)
            p.op('dve', lambda e, y=y, i=i: e.tensor_copy(ysb[i][:, :n], y[:, :n]), reads=[y], writes=[ysb[i]])
            p.op('act', lambda e, i=i: e.activation(sq[i][:, :n], ysb[i][:, :n], AF.Square), reads=[ysb[i]], writes=[sq[i]])
        post_residual(p, C, hT, ysb, n, co, s, kind, sq, pstat, rstd2, tmps)
        p.dma('pool', lambda e, hf=hf, t0=t0, n=n: e.dma_start(out=hout_v[:, :, t0:t0 + n], in_=hf[:, :, :n]), reads=hT)
    p.close_scope()


S5T = 8
MAGIC = 12582912.0
TWO_PI = 6.283185307179586
PI = 3.141592653589793


def s5_in_phase(p, nc, C, h_in, co, chunks, w5_s, dsk_d, uT_d, y0_d):
    p.open_scope()
    NT = 512
    ident = C['ident']
    hTf = [p.sb("hT", [128, KT, NT], F32) for _ in range(2)]
    hTs = [p.views(b, KT) for b in hTf]
    aT = p.views(p.sb("aT", [128, KT, NT], BF16), KT)
    sq = p.views(p.sb("sq", [128, KT, NT], BF16), KT)
    rstd = p.sb("rstd", [128, NT], F32)
    tmps = [p.sb("tmp", [128, NT], F32) for _ in range(3)]
    pstat = p.ps("pstat")
    pu = [p.ps("pu") for _ in range(2)]
    w5 = p.sb("w5", [128, 8, KT, 128], BF16)
    p.dma('sp', lambda e: e.dma_start(out=w5[:].rearrange("p g a b -> p g (a b)"), in_=w5_s.rearrange("g p x -> p g x")), writes=[w5])
    stage = p.sb("stage", [128, 128], F32)
    dsk = p.sb("dsk", [128, 8], F32)
    load_cols(p, dsk, dsk[:], dsk_d, 8, ident, pstat, stage)
    ub = [p.sb("ub", [128, NT], BF16) for _ in range(2)]
    y0 = [p.sb("y0", [128, NT], F32) for _ in range(2)]
    hin_v = h_in.rearrange("(kt p) t -> p kt t", p=128)
    for ci, (t0, n, kind) in enumerate(chunks):
        hT = hTs[ci % 2]
        hf = hTf[ci % 2]
        p.dma('sp', lambda e, hf=hf, t0=t0, n=n: e.dma_start(out=hf[:, :, :n], in_=hin_v[:, :, t0:t0 + n]), writes=hT)
        modulate(p, C, hT, n, co, 1, kind, sq, pstat, rstd, tmps, aT)
        for ct in range(8):
            ps = pu[ct % 2]
            for kt in range(KT):
                mm(p, ps, ps[:, :n], w5, w5[:, ct, kt, :], aT[kt], aT[kt][:, :n], kt == 0, kt == KT - 1)
            u = ub[ct % 2]
            p.op('dve', lambda e, ps=ps, u=u: e.tensor_copy(u[:, :n], ps[:, :n]), reads=[ps], writes=[u])
            p.dma('pool', lambda e, u=u, ct=ct, t0=t0, n=n: e.dma_start(out=uT_d[ct * 128:(ct + 1) * 128, t0:t0 + n], in_=u[:, :n]), reads=[u])
            if kind == 0:
                y = y0[ct % 2]
                p.op('dve', lambda e, ps=ps, y=y, ct=ct: e.tensor_scalar(y[:, :n], ps[:, :n], dsk[:, ct:ct + 1], None, ALU.mult), reads=[ps, dsk], writes=[y])
                p.dma('pool', lambda e, y=y, ct=ct, t0=t0, n=n: e.dma_start(out=y0_d[ct * 128:(ct + 1) * 128, t0 - NCTX:t0 - NCTX + n], in_=y[:, :n]), reads=[y])
    p.close_scope()


def s5_scan_phase(p, nc, C, T, SEQ, uT_d, y0_d, gT_d, lam_d, B12_d, C1_d, J_d, rmask_d):
    p.open_scope()
    ident = C['ident']
    NSC = T // S5T
    NCC = NCTX // S5T
    NLC = SEQ // S5T
    XW = NSC + 2
    NB = NSC // 8
    assert NB * 8 == NSC
    NRB = 0
    while (1 << NRB) < NB:
        NRB += 1
    NG = 8
    KLIST = list(range(1, 9)) + [8 * m for m in range(2, 8)] + [64 << r for r in range(NRB)]
    KIDX = {k: i for i, k in enumerate(KLIST)}
    NK = len(KLIST)
    NUL = 56 + NRB * 8
    Jt = p.sb("Jt", [128, 128], F32)
    p.dma('sp', lambda e: e.dma_start(out=Jt[:], in_=J_d), writes=[Jt])
    rmask = p.sb("rmask", [128, 8], F32)
    p.dma('sp', lambda e: e.dma_start(out=rmask[:], in_=rmask_d), writes=[rmask])
    pb = [p.ps("pb%d" % i) for i in range(8)]
    u_de = p.sb("u_de", [128, S5T, NSC], BF16)
    Y = p.sb("Y", [128, SEQ], F32)
    Xb = p.sb("Xb", [128, 8, NSC], BF16)
    Xbg = p.views(Xb, 8)
    Hin = p.sb("Hin", [128, 8, NSC], BF16)
    Hing = p.views(Hin, 8)
    Wf = p.sb("Wf", [128, 8, NB], F32)
    Wfg = p.views(Wf, 8)
    Wb = p.sb("Wb", [128, 8, NB + 2], BF16)
    Wbg = p.views(Wb, 8)
    p.op('pool', lambda e: e.memset(Wb[:], 0.0), writes=[Wb] + Wbg)
    identb = p.sb("identb", [128, 128], BF16)
    p.op('dve', lambda e: e.tensor_copy(identb[:], ident[:]), reads=[ident], writes=[identb])
    CtL = p.sb("CtL", [128, 8, 8, 128], BF16)
    p.op('pool', lambda e: e.memset(CtL[:], 0.0), writes=[CtL])
    CtLv = [[Buf("CtLv", CtL.t[:, t, g, :]) for g in range(NG)] for t in range(8)]
    UL = p.sb("UL", [128, max(NUL, 64), 128], BF16)
    ULv = [Buf("ULv", UL.t[:, i, :]) for i in range(max(NUL, 64))]
    KdL = p.sb("KdL", [128, 8, 128], BF16)
    MBm = p.sb("MBm", [128, 8, 128], F32)
    p.op('pool', lambda e: e.memset(MBm[:], 0.0), writes=[MBm])
    MBv = p.views(MBm, 8)
    ctf = p.sb("ctf", [128, 9, 8, 16], F32)
    ctfv = [[Buf("ctfv", ctf.t[:, k, g, :]) for g in range(NG)] for k in range(9)]
    Qk = [p.sb("Qk", [128, 128], F32) for _ in range(8)]
    Bs1 = p.sb("Bs1", [128, 8, 16], F32)
    Bs2 = p.sb("Bs2", [128, 8, 16], F32)
    Cs1 = p.sb("Cs1", [128, 8, 16], F32)
    Cs2 = p.sb("Cs2", [128, 8, 16], F32)
    tb = {nm: p.sb("tb_" + nm, [128, 64], F32) for nm in
          ['lre', 'lim', 'lst', 'dt', 'lr', 'lrdt', 'mag', 'ang', 'angk', 'red', 'sin', 'cos', 'ar', 'ai', 'nai', 'den', 'am1', 'fr', 'fi', 'nfi', 't1', 't2']}
    tbd = [{nm: p.sb("tbd_" + nm, [128, 64], F32) for nm in ['fr', 'S2']} for _ in range(2)]
    pw = [[{nm: p.sb("pw_" + nm, [128, 64], F32) for nm in (['AR', 'T5', 'T3', 'T4'] if KLIST[ki] <= 8 else ['T3', 'T4'])}
           for ki in range(NK)] for _ in range(2)]

    def ew(eng, fn, reads, writes):
        p.op(eng, fn, reads=reads, writes=writes)

    def reduce_sin(dst, src, shift):
        r = tb['red']
        ew('dve', lambda e: e.tensor_scalar(r[:], src[:], float(shift), None, ALU.add), [src], [r])
        ew('dve', lambda e: e.tensor_scalar(tb['t2'][:], r[:], float(1.0 / TWO_PI), MAGIC, ALU.mult, ALU.add), [r], [tb['t2']])
        ew('dve', lambda e: e.tensor_scalar(tb['t2'][:], tb['t2'][:], -MAGIC, None, ALU.add), [tb['t2']], [tb['t2']])
        ew('dve', lambda e: e.scalar_tensor_tensor(r[:], tb['t2'][:], float(-TWO_PI), r[:], ALU.mult, ALU.add), [tb['t2'], r], [r])
        ew('dve', lambda e: e.tensor_scalar(r[:], r[:], float(PI), float(-PI), ALU.min, ALU.max), [r], [r])
        ew('act', lambda e: e.activation(dst[:], r[:], AF.Sin), [r], [dst])

    def tt(dst, a, b, op, eng='dve'):
        ew(eng, lambda e: e.tensor_tensor(dst[:], a[:], b[:], op), [a, b], [dst])

    def half_copy(dst, top, bot):
        ew('dve', lambda e: e.tensor_copy(dst[0:64, :], top[0:64, :]), [top], [dst])
        ew('dve', lambda e: e.tensor_copy(dst[64:128, :], bot[64:128, :]), [bot], [dst])

    def power(k, dst):
        ew('dve', lambda e: e.tensor_scalar(tb['t1'][:], tb['lrdt'][:], float(k), None, ALU.mult), [tb['lrdt']], [tb['t1']])
        ew('act', lambda e: e.activation(tb['mag'][:], tb['t1'][:], AF.Exp), [tb['t1']], [tb['mag']])
        ew('dve', lambda e: e.tensor_scalar(tb['angk'][:], tb['ang'][:], float(k), None, ALU.mult), [tb['ang']], [tb['angk']])
        reduce_sin(tb['sin'], tb['angk'], 0.0)
        reduce_sin(tb['cos'], tb['angk'], PI / 2)
        tt(tb['ar'], tb['mag'], tb['cos'], ALU.mult)
        tt(tb['ai'], tb['mag'], tb['sin'], ALU.mult)
        ew('dve', lambda e: e.tensor_scalar(tb['nai'][:], tb['ai'][:], -1.0, None, ALU.mult), [tb['ai']], [tb['nai']])

    for d in range(2):
        for i, nm in enumerate(['lre', 'lim', 'lst']):
            p.dma('sp', lambda e, i=i, nm=nm, d=d: e.dma_start(out=tb[nm][:], in_=lam_d[d, i]), writes=[tb[nm]])
        ew('act', lambda e: e.activation(tb['dt'][:], tb['lst'][:], AF.Exp), [tb['lst']], [tb['dt']])
        ew('dve', lambda e: e.tensor_scalar(tb['lr'][:], tb['lre'][:], -1e-4, None, ALU.min), [tb['lre']], [tb['lr']])
        tt(tb['lrdt'], tb['lr'], tb['dt'], ALU.mult)
        tt(tb['ang'], tb['lim'], tb['dt'], ALU.mult)
        for ki, k in enumerate(KLIST):
            power(k, None)
            t = pw[d][ki]
            if 'AR' in t:
                ew('pool', lambda e, t=t: e.tensor_copy(t['AR'][:], tb['ar'][:]), [tb['ar']], [t['AR']])
                half_copy(t['T5'], tb['ai'], tb['nai'])
            half_copy(t['T3'], tb['ar'], tb['nai'])
            half_copy(t['T4'], tb['ai'], tb['ar'])
            if k == 1:
                tt(tb['den'], tb['lr'], tb['lr'], ALU.mult)
                tt(tb['t1'], tb['lim'], tb['lim'], ALU.mult)
                tt(tb['den'], tb['den'], tb['t1'], ALU.add)
                ew('dve', lambda e: e.reciprocal(tb['den'][:], tb['den'][:]), [tb['den']], [tb['den']])
                ew('dve', lambda e: e.tensor_scalar(tb['am1'][:], tb['ar'][:], -1.0, None, ALU.add), [tb['ar']], [tb['am1']])
                tt(tb['t1'], tb['am1'], tb['lr'], ALU.mult)
                tt(tb['t2'], tb['ai'], tb['lim'], ALU.mult)
                tt(tb['t1'], tb['t1'], tb['t2'], ALU.add)
                tt(tbd[d]['fr'], tb['t1'], tb['den'], ALU.mult)
                tt(tb['t1'], tb['ai'], tb['lr'], ALU.mult)
                tt(tb['t2'], tb['am1'], tb['lim'], ALU.mult)
                tt(tb['t1'], tb['t1'], tb['t2'], ALU.subtract)
                tt(tb['fi'], tb['t1'], tb['den'], ALU.mult)
                ew('dve', lambda e: e.tensor_scalar(tb['nfi'][:], tb['fi'][:], -1.0, None, ALU.mult), [tb['fi']], [tb['nfi']])
                half_copy(tbd[d]['S2'], tb['nfi'], tb['fi'])

    Xbflat = Xb[:].rearrange("p a b -> p (a b)")
    for ct in range(8):
        for d in range(2):
            fwd = (d == 0)
            if fwd:
                pieces = [(0, NCC, 0)]
                c = 0
                while c < NLC:
                    w = min(512, NLC - c)
                    pieces.append((NCC + c, w, NCTX + c * S5T))
                    c += w
            else:
                pieces = []
                c = 0
                while c < NLC:
                    w = min(512, NLC - c)
                    pieces.append((c, w, NCTX + c * S5T))
                    c += w
                pieces.append((NLC, NCC, 0))
            if d == 0:
                p.sync_all()
                p.dma('sp', lambda e, ct=ct: e.dma_start(out=Xbflat[:, 0:T], in_=uT_d[ct * 128:(ct + 1) * 128, :]), writes=[Xb])
                uv = Xbflat[:, 0:T].rearrange("p (c s) -> p c s", s=S5T)
                for s in range(S5T):
                    eng = 'dve' if s % 2 == 0 else 'act'
                    if eng == 'dve':
                        ew('dve', lambda e, s=s, uv=uv: e.tensor_copy(u_de[:, s, :], uv[:, :, s]), [Xb], [u_de])
                    else:
                        ew('act', lambda e, s=s, uv=uv: e.copy(u_de[:, s, :], uv[:, :, s]), [Xb], [u_de])
                p.dma('sp', lambda e, ct=ct: e.dma_start(out=Y[:], in_=y0_d[ct * 128:(ct + 1) * 128, :]), writes=[Y])
                p.sync_all()
            p.dma('sp', lambda e, ct=ct, d=d: e.dma_start(out=Bs1[:], in_=B12_d[d, 0, :, ct * 8:(ct + 1) * 8, :]), writes=[Bs1])
            p.dma('sp', lambda e, ct=ct, d=d: e.dma_start(out=Bs2[:], in_=B12_d[d, 1, :, ct * 8:(ct + 1) * 8, :]), writes=[Bs2])
            p.dma('sp', lambda e, ct=ct, d=d: e.dma_start(out=Cs1[:], in_=C1_d[d, 0, :, ct * 8:(ct + 1) * 8, :]), writes=[Cs1])
            p.dma('sp', lambda e, ct=ct, d=d: e.dma_start(out=Cs2[:], in_=C1_d[d, 1, :, ct * 8:(ct + 1) * 8, :]), writes=[Cs2])
            ew('pool', lambda e: e.tensor_scalar(Cs1[64:128], Cs1[64:128], -1.0, None, ALU.mult), [Cs1], [Cs1])
            ew('pool', lambda e: e.tensor_scalar(Cs2[0:64], Cs2[0:64], -1.0, None, ALU.mult), [Cs2], [Cs2])
            t_fr, t_S2 = tbd[d]['fr'], tbd[d]['S2']
            for g in range(NG):
                gg = ct * 8 + g
                blk = slice(16 * g, 16 * g + 16)
                ew('dve', lambda e, g=g, gg=gg, blk=blk, t_fr=t_fr: e.tensor_scalar(MBm[:, g, blk], Bs1[:, g, :], t_fr[:, gg:gg + 1], None, ALU.mult), [Bs1, t_fr], [MBv[g]])
                ew('dve', lambda e, g=g, gg=gg, blk=blk, t_S2=t_S2: e.scalar_tensor_tensor(MBm[:, g, blk], Bs2[:, g, :], t_S2[:, gg:gg + 1], MBm[:, g, blk], ALU.mult, ALU.add), [Bs2, t_S2, MBv[g]], [MBv[g]])
                ew('pool', lambda e, g=g: e.tensor_copy(ctf[:, 0, g, :], Cs1[:, g, :]), [Cs1], [ctfv[0][g]])
            BsT = [p.view(pb[3 + k // 4], pb[3 + k // 4].t[:, (k % 4) * 128:(k % 4 + 1) * 128]) for k in range(8)]
            KdA = [p.view(pb[5 + k // 4], pb[5 + k // 4].t[:, (k % 4) * 128:(k % 4 + 1) * 128]) for k in range(8)]
            qi = 0
            for k in range(0, 9):
                tk = pw[d][k - 1] if k >= 1 else None
                for g in range(NG):
                    gg = ct * 8 + g
                    blk = slice(16 * g, 16 * g + 16)
                    if k >= 1:
                        cv = ctfv[k][g]
                        ew('dve', lambda e, k=k, g=g, gg=gg, tk=tk: e.tensor_scalar(ctf[:, k, g, :], Cs1[:, g, :], tk['AR'][:, gg:gg + 1], None, ALU.mult), [Cs1, tk['AR']], [cv])
                        ew('dve', lambda e, k=k, g=g, gg=gg, tk=tk: e.scalar_tensor_tensor(ctf[:, k, g, :], Cs2[:, g, :], tk['T5'][:, gg:gg + 1], ctf[:, k, g, :], ALU.mult, ALU.add), [Cs2, tk['T5'], cv], [cv])
                        t_idx = (k - 1) if fwd else (8 - k)
                        ew('act', lambda e, k=k, g=g, t_idx=t_idx, blk=blk: e.copy(CtL[:, t_idx, g, blk], ctf[:, k, g, :]), [cv], [CtLv[t_idx][g]])
                    if k <= 7:
                        if k == 0:
                            Qb_, Qap = ident, ident[:, :]
                        else:
                            Qb_ = Qk[qi % 8]
                            qi += 1
                            ew('pool', lambda e, Qb_=Qb_, gg=gg, tk=tk: e.tensor_scalar(Qb_[:, 0:64], Jt[:, 0:64], tk['T3'][:, gg:gg + 1], None, ALU.mult), [Jt, tk['T3']], [Qb_])
                            ew('pool', lambda e, Qb_=Qb_, gg=gg, tk=tk: e.tensor_scalar(Qb_[:, 64:128], Jt[:, 64:128], tk['T4'][:, gg:gg + 1], None, ALU.mult), [Jt, tk['T4']], [Qb_])
                            Qap = Qb_[:, :]
                        mm(p, BsT[k], BsT[k][:, :], MBv[g], MBm[:, g, :], Qb_, Qap, g == 0, g == NG - 1)
                        mm(p, KdA[k], KdA[k][:, blk], MBv[g], MBm[:, g, :], ctfv[k][g], ctf[:, k, g, :], True, True)
            p.sync_all()
            for k in range(8):
                s_idx = (7 - k) if fwd else k
                for g in range(NG):
                    if k < 4:
                        ew('dve', lambda e, k=k, g=g, s_idx=s_idx: e.tensor_scalar(UL[:, s_idx * 8 + g, :], BsT[k][:, :], rmask[:, g:g + 1], None, ALU.mult), [BsT[k], rmask], [ULv[s_idx * 8 + g]])
                    else:
                        ew('act', lambda e, k=k, g=g, s_idx=s_idx: e.activation(UL[:, s_idx * 8 + g, :], BsT[k][:, :], AF.Copy, scale=rmask[:, g:g + 1]), [BsT[k], rmask], [ULv[s_idx * 8 + g]])
                if k < 4:
                    ew('dve', lambda e, k=k: e.tensor_copy(KdL[:, k, :], KdA[k][:, :]), [KdA[k]], [KdL])
                else:
                    ew('act', lambda e, k=k: e.copy(KdL[:, k, :], KdA[k][:, :]), [KdA[k]], [KdL])
            p.sync_all()
            bi = 0
            for g in range(NG):
                for (c0, ncol, tok0) in pieces:
                    ps = pb[bi % 4]
                    bi += 1
                    j0 = tok0 // S5T
                    for s in range(S5T):
                        mm(p, ps, ps[:, :ncol], ULv[s * 8 + g], UL[:, s * 8 + g, :], u_de, u_de[:, s, j0:j0 + ncol], s == 0, s == S5T - 1)
                    ew('dve', lambda e, ps=ps, g=g, c0=c0, ncol=ncol: e.tensor_copy(Xb[:, g, c0:c0 + ncol], ps[:, :ncol]), [ps], [Xbg[g]])
            p.sync_all()
            def build_tile(idx, k, g, gg, par):
                tk = pw[d][KIDX[k]]
                uv_ = ULv[idx]
                if par == 0:
                    ew('pool', lambda e: e.tensor_scalar(UL[:, idx, 0:64], Jt[:, 0:64], tk['T3'][:, gg:gg + 1], None, ALU.mult), [Jt, tk['T3']], [uv_])
                    ew('pool', lambda e: e.tensor_scalar(UL[:, idx, 64:128], Jt[:, 64:128], tk['T4'][:, gg:gg + 1], None, ALU.mult), [Jt, tk['T4']], [uv_])
                else:
                    ew('act', lambda e: e.activation(UL[:, idx, 0:64], Jt[:, 0:64], AF.Copy, scale=tk['T3'][:, gg:gg + 1]), [Jt, tk['T3']], [uv_])
                    ew('act', lambda e: e.activation(UL[:, idx, 64:128], Jt[:, 64:128], AF.Copy, scale=tk['T4'][:, gg:gg + 1]), [Jt, tk['T4']], [uv_])

            cntb = 0
            for m in range(1, 8):
                for g in range(NG):
                    build_tile((m - 1) * 8 + g, 8 * m, g, ct * 8 + g, cntb % 2)
                    cntb += 1
            for r in range(NRB):
                for g in range(NG):
                    build_tile(56 + r * 8 + g, 64 << r, g, ct * 8 + g, cntb % 2)
                    cntb += 1
            p.sync_all()

            def LA(m, g):
                if m == 0:
                    return identb, identb[:, :]
                return ULv[(m - 1) * 8 + g], UL[:, (m - 1) * 8 + g, :]

            xv = [Xb[:, g, :].rearrange("p (c j) -> p c j", j=8) for g in range(NG)]
            hv = [Hin[:, g, :].rearrange("p (c j) -> p c j", j=8) for g in range(NG)]
            bi = 0
            for g in range(NG):
                ps = pb[bi % 4]
                bi += 1
                for j in range(8):
                    lb, lap = LA((7 - j) if fwd else j, g)
                    mm(p, ps, ps[:, :NB], lb, lap, Xbg[g], xv[g][:, :, j], j == 0, j == 7)
                ew('dve', lambda e, ps=ps, g=g: e.tensor_copy(Wf[:, g, :], ps[:, :NB]), [ps], [Wfg[g]])
                ew('act', lambda e, g=g: e.copy(Wb[:, g, 1:1 + NB], Wf[:, g, :]), [Wfg[g]], [Wbg[g]])
            for r in range(NRB):
                sh = 1 << r
                L = NB - sh
                if L <= 0:
                    continue
                if fwd:
                    dst0, src0 = sh, 0
                else:
                    dst0, src0 = 0, sh
                for g in range(NG):
                    ps = pb[bi % 4]
                    bi += 1
                    mm(p, ps, ps[:, :L], ULv[56 + r * 8 + g], UL[:, 56 + r * 8 + g, :], Wbg[g], Wb[:, g, 1 + src0:1 + src0 + L], True, True)
                    ew('dve', lambda e, ps=ps, g=g, L=L, dst0=dst0: e.tensor_tensor(Wf[:, g, dst0:dst0 + L], Wf[:, g, dst0:dst0 + L], ps[:, :L], ALU.add), [Wfg[g], ps], [Wfg[g]])
                    ew('act', lambda e, g=g, L=L, dst0=dst0: e.copy(Wb[:, g, 1 + dst0:1 + dst0 + L], Wf[:, g, dst0:dst0 + L]), [Wfg[g]], [Wbg[g]])
            ge0 = 0 if fwd else 2
            for g in range(NG):
                for j in range(8):
                    mG = j if fwd else (7 - j)
                    others = list(range(0, j)) if fwd else list(range(j + 1, 8))
                    if mG == 0 and not others:
                        ew('act', lambda e, g=g, j=j, ge0=ge0, hvg=hv[g]: e.copy(hvg[:, :, j], Wb[:, g, ge0:ge0 + NB]), [Wbg[g]], [Hing[g]])
                        continue
                    ps = pb[bi % 4]
                    bi += 1
                    terms = [(mG, Wbg[g], Wb[:, g, ge0:ge0 + NB])]
                    for i in others:
                        mi = (j - 1 - i) if fwd else (i - j - 1)
                        terms.append((mi, Xbg[g], xv[g][:, :, i]))
                    for ti, (m_, rb, rap) in enumerate(terms):
                        lb, lap = LA(m_, g)
                        mm(p, ps, ps[:, :NB], lb, lap, rb, rap, ti == 0, ti == len(terms) - 1)
                    if (g + j) % 2 == 0:
                        ew('dve', lambda e, ps=ps, g=g, j=j, hvg=hv[g]: e.tensor_copy(hvg[:, :, j], ps[:, :NB]), [ps], [Hing[g]])
                    else:
                        ew('act', lambda e, ps=ps, g=g, j=j, hvg=hv[g]: e.copy(hvg[:, :, j], ps[:, :NB]), [ps], [Hing[g]])
            p.sync_all()
            bi = 0
            for (c0, ncol, tok0) in pieces:
                if tok0 < NCTX:
                    continue
                j0 = tok0 // S5T
                yv = Y[:, tok0 - NCTX:tok0 - NCTX + S5T * ncol].rearrange("p (c s) -> p c s", s=S5T)
                for t in range(S5T):
                    ps = pb[bi % 4]
                    bi += 1
                    ss = list(range(0, t + 1)) if fwd else list(range(t, S5T))
                    nmm = NG + len(ss)
                    k2 = 0
                    for g in range(NG):
                        mm(p, ps, ps[:, :ncol], CtLv[t][g], CtL[:, t, g, :], Hing[g], Hin[:, g, c0:c0 + ncol], k2 == 0, k2 == nmm - 1)
                        k2 += 1
                    for s in ss:
                        mm(p, ps, ps[:, :ncol], KdL, KdL[:, abs(t - s), :], u_de, u_de[:, s, j0:j0 + ncol], k2 == 0, k2 == nmm - 1)
                        k2 += 1
                    ew('dve', lambda e, ps=ps, t=t, yv=yv, ncol=ncol: e.tensor_tensor(yv[:, :, t], yv[:, :, t], ps[:, :ncol], ALU.add), [Y, ps], [Y])
            p.sync_all()
        p.op('act', lambda e: e.activation(Xbflat[:, 0:SEQ], Y[:], AF.Gelu), reads=[Y], writes=[Xb])
        p.dma('pool', lambda e, ct=ct: e.dma_start(out=gT_d[ct * 128:(ct + 1) * 128, :], in_=Xbflat[:, 0:SEQ]), reads=[Xb])
    p.close_scope()


def glu_phase(p, nc, C, gT_d, wg_s, h_in, h_out, co, chunks, out_off):
    p.open_scope()
    NT = 512
    hTf = [p.sb("hT", [128, KT, NT], F32) for _ in range(2)]
    hTs = [p.views(b, KT) for b in hTf]
    gTf = [p.sb("gT", [128, KT, NT], BF16) for _ in range(2)]
    sq = p.views(p.sb("sq", [128, KT, NT], BF16), KT)
    rstd2 = p.sb("rstd2", [128, NT], F32)
    tmps = [p.sb("tmp", [128, NT], F32) for _ in range(3)]
    sgs = [p.sb("sg", [128, NT], F32) for _ in range(2)]
    ysb = p.views(p.sb("ysb", [128, KT, NT], F32), KT)
    wg = p.sb("wg", [128, KT, KT, 256], BF16)
    p.dma('sp', lambda e: e.dma_start(out=wg[:].rearrange("p i a b -> p i (a b)"), in_=wg_s.rearrange("i p x -> p i x")), writes=[wg])
    pstat = p.ps("pstat")
    pa = [p.ps("pa") for _ in range(2)]
    pg = [p.ps("pg") for _ in range(2)]
    hin_v = h_in.rearrange("(kt p) t -> p kt t", p=128)
    hout_v = h_out.rearrange("(kt p) t -> p kt t", p=128)
    g_v = gT_d.rearrange("(kt p) t -> p kt t", p=128)
    for ci, (t0, n, kind) in enumerate(chunks):
        hT = hTs[ci % 2]
        hf = hTf[ci % 2]
        gT = gTf[ci % 2]
        p.dma('sp', lambda e, hf=hf, t0=t0, n=n: e.dma_start(out=hf[:, :, :n], in_=hin_v[:, :, t0:t0 + n]), writes=hT)
        p.dma('sp', lambda e, gT=gT, t0=t0, n=n: e.dma_start(out=gT[:, :, :n], in_=g_v[:, :, t0 - NCTX:t0 - NCTX + n]), writes=[gT])
        for i in range(KT):
            a_, g_ = pa[i % 2], pg[i % 2]
            for kt in range(KT):
                mm(p, a_, a_[:, :n], wg, wg[:, i, kt, 0:128], gT, gT[:, kt, :n], kt == 0, kt == KT - 1)
            for kt in range(KT):
                mm(p, g_, g_[:, :n], wg, wg[:, i, kt, 128:256], gT, gT[:, kt, :n], kt == 0, kt == KT - 1)
            sg = sgs[i % 2]
            p.op('act', lambda e, g_=g_, sg=sg: e.activation(sg[:, :n], g_[:, :n], AF.Sigmoid), reads=[g_], writes=[sg])
            p.op('dve', lambda e, a_=a_, sg=sg, i=i: e.tensor_tensor(ysb[i][:, :n], sg[:, :n], a_[:, :n], ALU.mult), reads=[sg, a_], writes=[ysb[i]])
            p.op('act', lambda e, i=i: e.activation(sq[i][:, :n], ysb[i][:, :n], AF.Square), reads=[ysb[i]], writes=[sq[i]])
        post_residual(p, C, hT, ysb, n, co, 1, kind, sq, pstat, rstd2, tmps)
        p.dma('pool', lambda e, hf=hf, t0=t0, n=n: e.dma_start(out=hout_v[:, :, t0 - out_off:t0 - out_off + n], in_=hf[:, :, :n]), reads=hT)
    p.close_scope()


import numpy as np
GRID_W = 64
ROPE_BASE = 10000.0

def rope_tables(seq, nctx):
    t = np.arange(seq)
    rows = (t // GRID_W).astype(np.float32)
    cols = (t % GRID_W).astype(np.float32)
    def tab(d_rot):
        d_axis = d_rot // 2
        inv = (ROPE_BASE ** (-np.arange(0, d_axis, 2, dtype=np.float32) / d_axis)).astype(np.float32)
        ang = np.concatenate([rows[:, None] * inv, cols[:, None] * inv], axis=-1).astype(np.float32)
        return np.cos(ang).astype(np.float32), np.sin(ang).astype(np.float32)
    T = nctx + seq
    ca, sa = tab(32)
    cb, sb_ = tab(64)
    A = np.zeros((32, 2, T), np.float32); A[:, 0, :] = 1.0
    A[:, 0, nctx:] = np.concatenate([ca.T, ca.T], 0)
    A[:, 1, nctx:] = np.concatenate([sa.T, sa.T], 0)
    B = np.zeros((128, 2, T), np.float32); B[:, 0, :] = 1.0
    B[:, 0, nctx:] = np.concatenate([cb.T, cb.T, cb.T, cb.T], 0)
    B[:, 1, nctx:] = np.concatenate([sb_.T, sb_.T, sb_.T, sb_.T], 0)
    return A, B

def const_inputs():
    ident = np.eye(128, dtype=np.float32)
    shiftI = np.zeros((128, 64), np.float32); shiftI[64 + np.arange(64), np.arange(64)] = 1.0
    j = np.arange(128)[:, None]; q = np.arange(128)[None, :]
    mprev = (j >= q).astype(np.float32)
    mnext = (j <= q).astype(np.float32)
    return dict(ident=ident, shiftI=shiftI, mprev=mprev, mnext=mnext)

def s5_layouts(inp):
    lre = inp['s5_lambda_re'][0]; lim = inp['s5_lambda_im'][0]; lst = inp['s5_log_step'][0]
    lam = np.zeros((2, 3, 128, 64), np.float32)
    for d in range(2):
        lam[d, 0] = np.concatenate([lre[d].T, lre[d].T], 0)
        lam[d, 1] = np.concatenate([lim[d].T, lim[d].T], 0)
        lam[d, 2] = np.broadcast_to(lst[d][None, :], (128, 64))
    br = inp['s5_b_re'][0].transpose(0, 2, 1, 3)
    bi = inp['s5_b_im'][0].transpose(0, 2, 1, 3)
    B12 = np.stack([np.concatenate([br, bi], 1), np.concatenate([bi, br], 1)], 1)
    cr = inp['s5_c_re'][0].transpose(0, 3, 1, 2)
    ci = inp['s5_c_im'][0].transpose(0, 3, 1, 2)
    C1 = np.stack([np.concatenate([cr, ci], 1), np.concatenate([ci, cr], 1)], 1)
    r = np.arange(128)
    J = (r[:, None] % 64 == r[None, :] % 64).astype(np.float32)
    rmask = (r[:, None] // 16 == np.arange(8)[None, :]).astype(np.float32)
    return dict(s5_lam=np.ascontiguousarray(lam), s5_B12=np.ascontiguousarray(B12), s5_C1=np.ascontiguousarray(C1), J128=J, rmask=rmask)


SEQ_FULL = 8192
N_CORES = 8


def build_program(SEQ=SEQ_FULL):
    T = NCTX + SEQ
    nc = bass.Bass("TRN2", target_bir_lowering=False)

    def inp(name, shape, dt=F32):
        return nc.dram_tensor(name, list(shape), dt, kind="ExternalInput").ap()

    h0 = inp("h0", [D, T]); cvec = inp("cvec", [16, 128])
    ident = inp("ident", [128, 128]); shiftI = inp("shiftI", [128, 64]); mprev = inp("mprev", [128, 128]); mnext = inp("mnext", [128, 128])
    modw = inp("mod_w", [2, D, 9 * D]); modb = inp("mod_b", [2, 72, 128]); npre = inp("norm_pre", [2, 24, 128]); npost = inp("norm_post", [2, 24, 128])
    w13 = inp("ffn_w13", [2, 2, D, 2 * DFF]); w2 = inp("ffn_w2", [2, 2, DFF, D])
    w_in = inp("attn_w_in", [1, D, 1184]); qn = inp("mla_q_norm", [2, 128]); wuq = inp("mla_w_uq", [1, 256, 768])
    kvn = inp("mla_kv_norm", [1, 128]); wukv = inp("mla_w_ukv", [1, 128, 1024]); sink = inp("sink128", [128, 8]); wout = inp("attn_w_out", [1, D, D])
    ropeA = inp("ropeA", [32, 2, T]); ropeB = inp("ropeB", [128, 2, T])
    w5in = inp("s5_w_in", [1, D, D]); dsk = inp("s5_d", [8, 128]); wglu = inp("s5_w_glu", [1, D, 2 * D])
    lam = inp("s5_lam", [2, 3, 128, 64]); B12 = inp("s5_B12", [2, 2, 128, 64, 16]); C1 = inp("s5_C1", [2, 2, 128, 64, 16])
    J = inp("J128", [128, 128]); rmask = inp("rmask", [128, 8])
    out = nc.dram_tensor("outT", [D, SEQ], F32, kind="ExternalOutput").ap()

    hA = nc.dram_tensor("hA", [D, T], F32).ap()
    hB = nc.dram_tensor("hB", [D, T], F32).ap()
    p = Prog(nc)
    C = build_consts(p, nc, ident, shiftI, mprev, mnext)
    CO = mod_phase(p, nc, C, cvec, modw, modb, npre, npost, 2)
    chunks = [(0, NCTX, 1)] + [(NCTX + 512 * i, 512, 0) for i in range(SEQ // 512)]
    lat_chunks = chunks[1:]
    g13 = [[(j * 128, 128), (DFF + j * 128, 128)] for j in range(JT)]
    g2 = [[(i * 128, 128)] for i in range(KT)]

    gattn_k = [[(128 * h, 64) for h in range(8)]]
    gattn_v = [[(128 * h + 64, 64) for h in range(8)]]
    gglu = [[(i * 128, 128), (D + i * 128, 128)] for i in range(KT)]
    W = {}

    def mk(names):
        def fn(stg):
            gens = []
            for (name, src, K, groups) in names:
                scr_, g_ = prep_weight_gen(p, nc, name, src, K, groups, stg)
                W[name] = scr_
                gens.append(g_)
            return gens
        return fn

    def ffn_w(l, f):
        return [("w13s_%d%d" % (l, f), w13[l, f], D, g13), ("w2s_%d%d" % (l, f), w2[l, f], DFF, g2)]

    attn_w = [("win_s", w_in[0], D, attn_weight_groups()), ("wuq_s", wuq[0], 256, wuq_groups()),
              ("wk_s", wukv[0], 128, gattn_k), ("wv_s", wukv[0], 128, gattn_v), ("wout_s", wout[0], D, g2)]
    s5_w = [("w5_s", w5in[0], D, g2), ("wg_s", wglu[0], D, gglu)]

    for (name, src, K, groups) in ffn_w(0, 0):
        W[name] = prep_weight(p, nc, name, src, K, groups)
    ffn_phase(p, nc, C, h0, hA, W["w13s_00"], W["w2s_00"], CO[0], 0, chunks, 0, mk(attn_w + ffn_w(0, 1)))
    scr = dict(QT=nc.dram_tensor("QT", [8, 96, T], BF16).ap(), KT=nc.dram_tensor("KTs", [8, 96, T], BF16).ap(),
               V=nc.dram_tensor("Vs", [T // 128, 128, 8, 128], BF16).ap(), QS=nc.dram_tensor("QS", [4, 128, T], BF16).ap(),
               KS=nc.dram_tensor("KS", [2, 128, T], BF16).ap(), VS=nc.dram_tensor("VS", [T // 128, 128, 2, 128], BF16).ap(),
               OT=nc.dram_tensor("OT", [D, T], BF16).ap())
    attn_proj_phase(p, nc, C, hA, CO[0], chunks, T, W["win_s"], W["wuq_s"], W["wk_s"], W["wv_s"], qn, kvn, ropeA, ropeB, scr)
    mla_phase(p, nc, C, chunks, T, scr)
    swa_phase(p, nc, C, chunks, T, scr, sink)
    mixout_phase(p, nc, C, scr['OT'], W["wout_s"], hA, hB, CO[0], 1, chunks, 8)
    ffn_phase(p, nc, C, hB, hA, W["w13s_01"], W["w2s_01"], CO[0], 2, chunks, 0, mk(ffn_w(1, 0) + s5_w))
    ffn_phase(p, nc, C, hA, hB, W["w13s_10"], W["w2s_10"], CO[1], 0, chunks, 0, mk(ffn_w(1, 1)))
    uT = nc.dram_tensor("uT", [D, T], BF16).ap()
    y0 = nc.dram_tensor("y0T", [D, SEQ], F32).ap()
    gT = nc.dram_tensor("gT", [D, SEQ], BF16).ap()
    s5_in_phase(p, nc, C, hB, CO[1], chunks, W["w5_s"], dsk, uT, y0)
    s5_scan_phase(p, nc, C, T, SEQ, uT, y0, gT, lam, B12, C1, J, rmask)
    glu_phase(p, nc, C, gT, W["wg_s"], hB, hA, CO[1], lat_chunks, 0)
    ffn_phase(p, nc, C, hA, out, W["w13s_11"], W["w2s_11"], CO[1], 2, lat_chunks, NCTX)
    p.finish()
    return nc


def make_in_maps(inputs, SEQ=SEQ_FULL, n_cores=N_CORES):
    inp = {k: np.asarray(v) for k, v in inputs.items()}
    rA, rB = rope_tables(SEQ, NCTX)
    shared = {"mod_w": inp['mod_w'], "mod_b": inp['mod_b'].reshape(2, 72, 128),
              "norm_pre": inp['norm_pre'].reshape(2, 24, 128), "norm_post": inp['norm_post'].reshape(2, 24, 128),
              "ffn_w13": inp['ffn_w13'], "ffn_w2": inp['ffn_w2'],
              "attn_w_in": inp['attn_w_in'], "mla_q_norm": inp['mla_q_norm'].reshape(2, 128), "mla_w_uq": inp['mla_w_uq'],
              "mla_kv_norm": inp['mla_kv_norm'].reshape(1, 128), "mla_w_ukv": inp['mla_w_ukv'],
              "sink128": np.ascontiguousarray(np.tile(inp['swa_sink'][0][None], (128, 1))), "attn_w_out": inp['attn_w_out'],
              "ropeA": rA, "ropeB": rB,
              "s5_w_in": inp['s5_w_in'], "s5_d": inp['s5_d'].reshape(8, 128), "s5_w_glu": inp['s5_w_glu']}
    shared.update(s5_layouts(inp))
    shared.update(const_inputs())
    shared = {k: np.ascontiguousarray(v, dtype=np.float32) for k, v in shared.items()}
    maps = []
    for b in range(n_cores):
        m = dict(shared)
        m["h0"] = np.ascontiguousarray(np.concatenate([inp['ctx'][b].T, inp['x'][b, :SEQ].T], axis=1), dtype=np.float32)
        m["cvec"] = np.ascontiguousarray(np.concatenate([inp['c'][b].reshape(8, 128), inp['c_ctx'].reshape(8, 128)], 0), dtype=np.float32)
        maps.append(m)
    return maps


def kernel(**inputs):
    nc = build_program(SEQ_FULL)
    maps = make_in_maps(inputs, SEQ_FULL, N_CORES)
    res = run_bass_kernel_spmd(nc, maps, core_ids=list(range(N_CORES)))
    out = np.stack([np.ascontiguousarray(r["outT"].T) for r in res.results], axis=0)
    return out.astype(np.float32)
```

```python
import numpy as np
import concourse.bass as bass
import concourse.mybir as mybir
from concourse.bass_utils import run_bass_kernel_spmd
from contextlib import ExitStack

F32 = mybir.dt.float32
BF16 = mybir.dt.bfloat16
AF = mybir.ActivationFunctionType
ALU = mybir.AluOpType
AX = mybir.AxisListType
ENGS = ['pe', 'act', 'dve', 'pool', 'sp']

D = 1024
KT = 8
DFF = 2816
JT = 22
NCTX = 256
EPS = 1e-6


class Buf:
    _serial = [0]

    def __init__(self, name, t=None):
        Buf._serial[0] += 1
        self.uid = Buf._serial[0]
        self.name = name
        self.t = t
        self.w = None
        self.r = []
        self.dsem = None
        self.dcnt = 0

    def __getitem__(self, idx):
        return self.t[idx]


class Prog:
    def __init__(self, nc):
        self.nc = nc
        self.es = ExitStack()
        self.sem = {e: self.es.enter_context(nc.semaphore("s_" + e)) for e in ENGS}
        self.cnt = {e: 0 for e in ENGS}
        self.ops = {e: [] for e in ENGS}
        self.waited = {e: {} for e in ENGS}
        self.dbufs = []
        self.scopes = []
        self.nm = 0
        self.dsem_pool = []

    def _stack(self):
        return self.scopes[-1] if self.scopes else self.es

    def sb(self, name, shape, dt):
        self.nm += 1
        t = self._stack().enter_context(self.nc.sbuf_tensor("%s_%d" % (name, self.nm), list(shape), dt))
        return Buf(name, t)

    def ps(self, name, shape=(128, 512), dt=F32):
        self.nm += 1
        t = self._stack().enter_context(self.nc.psum_tensor("%s_%d" % (name, self.nm), list(shape), dt))
        return Buf(name, t)

    def view(self, buf, t):
        return Buf(buf.name + "_v", t)

    def views(self, buf, k):
        return [Buf("%s_%d" % (buf.name, i), buf.t[:, i, :]) for i in range(k)]

    def open_scope(self):
        self.scopes.append(ExitStack())

    def close_scope(self):
        self.flush()
        depth = len(self.scopes)
        keep = []
        for b in self.dbufs:
            if getattr(b, 'scope_depth', 0) >= depth:
                self.dsem_pool.append((b.dsem, b.dcnt))
                b.dsem = None
            else:
                keep.append(b)
        self.dbufs = keep
        self.scopes.pop().close()

    def _need(self, eng, dep, waits):
        if dep is None:
            return
        kind, key, val = dep
        if kind == 'e' and key == 'pe' and eng == 'pe':
            return
        k = (kind, key if kind == 'e' else key.uid)
        if self.waited[eng].get(k, 0) >= val:
            return
        self.waited[eng][k] = val
        sem = self.sem[key] if kind == 'e' else key.dsem
        waits.append((sem, val))

    def _deps(self, eng, reads, writes):
        waits = []
        for b in reads:
            self._need(eng, b.w, waits)
        for b in writes:
            self._need(eng, b.w, waits)
            for r in b.r:
                self._need(eng, r, waits)
        return waits

    def op(self, eng, fn, reads=(), writes=(), inc=True):
        waits = self._deps(eng, reads, writes)
        if inc:
            self.cnt[eng] += 1
            tick = ('e', eng, self.cnt[eng])
        else:
            tick = ('e', eng, self.cnt[eng] + 1)
        self.ops[eng].append((waits, fn, (self.sem[eng], 1) if inc else None))
        for b in reads:
            b.r.append(tick)
        for b in writes:
            b.w = tick
            b.r = []
        return tick

    def dma(self, eng, fn, reads=(), writes=(), owner=None):
        waits = self._deps(eng, reads, writes)
        if owner is None:
            owner = (list(writes) + list(reads))[0]
        if owner.dsem is None:
            if self.dsem_pool:
                owner.dsem, owner.dcnt = self.dsem_pool.pop()
            else:
                self.nsem = getattr(self, 'nsem', 0) + 1
                owner.dsem = self.es.enter_context(self.nc.semaphore("d%d" % self.nsem))
            self.dbufs.append(owner)
            owner.scope_depth = len(self.scopes)
        owner.dcnt += 16
        tick = ('d', owner, owner.dcnt)
        self.ops[eng].append((waits, fn, (owner.dsem, 16)))
        for b in reads:
            b.r.append(tick)
        for b in writes:
            b.w = tick
            b.r = []
        return tick

    def barrier(self):
        for e in ENGS:
            waits = []
            for e2 in ENGS:
                if e2 != e and self.cnt[e2] > 0:
                    self._need(e, ('e', e2, self.cnt[e2]), waits)
            for b in self.dbufs:
                self._need(e, ('d', b, b.dcnt), waits)
            if waits:
                self.ops[e].append((waits, None, None))

    def sync_all(self):
        self.barrier()

    def flush(self):
        nc = self.nc
        self.barrier()
        ops = self.ops
        self.ops = {e: [] for e in ENGS}
        self.simulate(ops)

        def run(lst, e):
            for waits, fn, inc in lst:
                for sem, val in waits:
                    e.wait_ge(sem, val)
                if fn is not None:
                    ins = fn(e)
                    if inc is not None:
                        ins.then_inc(inc[0], inc[1])

        with nc.Block() as block:
            @block.tensor
            def _(e):
                run(ops['pe'], e)

            @block.scalar
            def _(e):
                run(ops['act'], e)

            @block.vector
            def _(e):
                run(ops['dve'], e)

            @block.gpsimd
            def _(e):
                run(ops['pool'], e)

            @block.sync
            def _(e):
                run(ops['sp'], e)

    def simulate(self, ops):
        if not hasattr(self, 'simv'):
            self.simv = {}
        ptr = {e: 0 for e in ENGS}
        while True:
            prog = False
            for e in ENGS:
                while ptr[e] < len(ops[e]):
                    waits, fn, inc = ops[e][ptr[e]]
                    if all(self.simv.get(id(s), 0) >= v for s, v in waits):
                        if inc is not None:
                            self.simv[id(inc[0])] = self.simv.get(id(inc[0]), 0) + inc[1]
                        ptr[e] += 1
                        prog = True
                    else:
                        break
            if all(ptr[e] == len(ops[e]) for e in ENGS):
                return
            if not prog:
                for e in ENGS:
                    if ptr[e] < len(ops[e]):
                        waits, fn, inc = ops[e][ptr[e]]
                        print("DEADLOCK", e, ptr[e], [(s, v, self.simv.get(id(s), 0)) for s, v in waits])
                raise RuntimeError("deadlock in sync plan")

    def finish(self):
        self.flush()
        self.es.close()


def mm(p, out, out_ap, lhs, lhs_ap, rhs, rhs_ap, start, stop):
    p.op('pe', lambda e: e.matmul(out_ap, lhs_ap, rhs_ap, start=start, stop=stop),
         reads=[lhs, rhs], writes=[out], inc=stop)


_rr = [0]


def cast_any(p, out, out_ap, in_, in_ap):
    i = _rr[0] % 3
    _rr[0] += 1
    if i == 0:
        p.op('dve', lambda e: e.tensor_copy(out_ap, in_ap), reads=[in_], writes=[out])
    elif i == 1:
        p.op('pool', lambda e: e.tensor_copy(out_ap, in_ap), reads=[in_], writes=[out])
    else:
        p.op('act', lambda e: e.copy(out_ap, in_ap), reads=[in_], writes=[out])


PREP_MAX = 2816


def prep_weight_gen(p, nc, name, W, K, groups, stg):
    ktin = K // 128
    wtot = sum(g[1] for g in groups[0])
    G = len(groups)
    scr = nc.dram_tensor(name, [G, 128, ktin * wtot], BF16).ap()
    Wv = W.rearrange("(kt p) n -> p kt n", p=128)
    sz = ktin * wtot
    assert sz <= PREP_MAX

    def gen():
        for g, grp in enumerate(groups):
            i = stg['cnt'][0] % 2
            stg['cnt'][0] += 1
            fb, bb = stg['f'][i], stg['b'][i]
            ft = fb.t[:, 0:sz].rearrange("p (a b) -> p a b", a=ktin)
            bt = bb.t[:, 0:sz]
            off = 0
            negs = []
            for it in grp:
                c0, w = it[0], it[1]
                sgn = it[2] if len(it) > 2 else 1
                p.dma('sp', lambda e, ft=ft, off=off, c0=c0, w=w: e.dma_start(out=ft[:, :, off:off + w], in_=Wv[:, :, c0:c0 + w]),
                      writes=[fb])
                if sgn < 0:
                    negs.append((off, w))
                off += w
            for (o2, w2) in negs:
                p.op('dve', lambda e, ft=ft, o2=o2, w2=w2: e.tensor_scalar(ft[:, :, o2:o2 + w2], ft[:, :, o2:o2 + w2], -1.0, None, ALU.mult),
                     reads=[fb], writes=[fb])
            cast_any(p, bb, bt, fb, fb.t[:, 0:sz])
            p.dma('pool', lambda e, bt=bt, g=g: e.dma_start(out=scr[g], in_=bt), reads=[bb])
            yield

    return scr, gen()


def prep_staging(p):
    return dict(f=[p.sb("pwf", [128, PREP_MAX], F32) for _ in range(2)],
                b=[p.sb("pwb", [128, PREP_MAX], BF16) for _ in range(2)], cnt=[0])


def prep_weight(p, nc, name, W, K, groups):
    p.open_scope()
    stg = prep_staging(p)
    scr, gen = prep_weight_gen(p, nc, name, W, K, groups, stg)
    for _ in gen:
        pass
    p.close_scope()
    return scr


def advance(bg):
    while bg:
        try:
            next(bg[0])
            return
        except StopIteration:
            bg.pop(0)


def load_cols(p, dst, dst_ap, src_ap, R, ident, ps, stage):
    p.dma('sp', lambda e: e.dma_start(out=stage[0:R, :], in_=src_ap), writes=[stage])
    mm(p, ps, ps[:, 0:R], stage, stage[0:R, :], ident, ident[0:R, 0:R], True, True)
    p.op('dve', lambda e: e.tensor_copy(dst_ap, ps[:, 0:R]), reads=[ps], writes=[dst])


def build_consts(p, nc, ident_d, shift_d=None, mprev_d=None, mnext_d=None):
    c = {}
    if shift_d is not None:
        c['shiftI'] = p.sb("shiftI", [128, 64], F32)
        p.dma('sp', lambda e: e.dma_start(out=c['shiftI'][:], in_=shift_d), writes=[c['shiftI']])
        c['mprev'] = p.sb("mprev", [128, 128], F32)
        p.dma('sp', lambda e: e.dma_start(out=c['mprev'][:], in_=mprev_d), writes=[c['mprev']])
        c['mnext'] = p.sb("mnext", [128, 128], F32)
        p.dma('sp', lambda e: e.dma_start(out=c['mnext'][:], in_=mnext_d), writes=[c['mnext']])
    c['ident'] = p.sb("ident", [128, 128], F32)
    p.dma('sp', lambda e: e.dma_start(out=c['ident'][:], in_=ident_d), writes=[c['ident']])
    c['ones_bf'] = p.sb("ones_bf", [128, 128], BF16)
    p.op('dve', lambda e: e.memset(c['ones_bf'][:], 1.0), writes=[c['ones_bf']])
    c['eps'] = p.sb("epsc", [128, 1], F32)
    p.op('dve', lambda e: e.memset(c['eps'][:], EPS), writes=[c['eps']])
    return c


def mod_phase(p, nc, C, cvec_d, modw_d, modb_d, npre_d, npost_d, L):
    CO = [p.sb("CO%d" % l, [128, 3, 3, 2, 8], F32) for l in range(L)]
    p.open_scope()
    ident = C['ident']
    stage = p.sb("stage", [128, 128], F32)
    pst = p.ps("pst")
    cT = p.sb("cT", [128, 16], F32)
    load_cols(p, cT, cT[:], cvec_d, 16, ident, pst, stage)
    sc = p.sb("sc", [128, 8, 2], F32)
    for kind in range(2):
        p.op('act', lambda e, kind=kind: e.activation(sc[:, :, kind], cT[:, kind * 8:(kind + 1) * 8], AF.Silu),
             reads=[cT], writes=[sc])
    wbuf = [p.sb("mwb", [128, 8, 1024], F32) for _ in range(2)]
    psM = p.ps("psM")
    M = p.sb("M", [128, 72, 2], F32)
    mb = p.sb("mb", [128, 72], F32)
    gpre = p.sb("gpre", [128, 24], F32)
    gpost = p.sb("gpost", [128, 24], F32)
    t8 = p.sb("t8", [128, 8], F32)
    for l in range(L):
        mwv = modw_d[l].rearrange("(kt p) n -> p kt n", p=128)
        for j in range(9):
            wb = wbuf[(l * 9 + j) % 2]
            p.dma('sp', lambda e, wb=wb, j=j, mwv=mwv: e.dma_start(out=wb[:], in_=mwv[:, :, j * 1024:(j + 1) * 1024]), writes=[wb])
            for q in range(8):
                col = (j * 8 + q) * 2
                for kt in range(8):
                    mm(p, psM, psM[:, col:col + 2], wb, wb[:, kt, q * 128:(q + 1) * 128], sc, sc[:, kt, :], kt == 0, kt == 7)
        load_cols(p, mb, mb[:], modb_d[l], 72, ident, pst, stage)
        load_cols(p, gpre, gpre[:], npre_d[l], 24, ident, pst, stage)
        load_cols(p, gpost, gpost[:], npost_d[l], 24, ident, pst, stage)
        for kind in range(2):
            p.op('dve', lambda e, kind=kind: e.tensor_tensor(M[:, :, kind], psM[:, 0:144].rearrange("p (q k) -> p q k", k=2)[:, :, kind], mb[:], ALU.add),
                 reads=[psM, mb], writes=[M])
        co = CO[l]
        for s in range(3):
            coef = 1.0 if s == 1 else 0.5
            for kind in range(2):
                p.op('dve', lambda e, s=s, kind=kind: e.tensor_scalar(t8[:], M[:, (3 * s + 1) * 8:(3 * s + 2) * 8, kind], 1.0, None, ALU.add),
                     reads=[M], writes=[t8])
                p.op('dve', lambda e, s=s, kind=kind, co=co: e.tensor_tensor(co[:, s, 0, kind, :], t8[:], gpre[:, s * 8:(s + 1) * 8], ALU.mult),
                     reads=[t8, gpre], writes=[co])
                p.op('dve', lambda e, s=s, kind=kind, co=co: e.tensor_copy(co[:, s, 1, kind, :], M[:, (3 * s) * 8:(3 * s + 1) * 8, kind]),
                     reads=[M], writes=[co])
                p.op('dve', lambda e, s=s, kind=kind, coef=coef: e.tensor_scalar(t8[:], M[:, (3 * s + 2) * 8:(3 * s + 3) * 8, kind], coef, None, ALU.mult),
                     reads=[M], writes=[t8])
                p.op('dve', lambda e, s=s, kind=kind, co=co: e.tensor_tensor(co[:, s, 2, kind, :], t8[:], gpost[:, s * 8:(s + 1) * 8], ALU.mult),
                     reads=[t8, gpost], writes=[co])
    p.close_scope()
    return CO


def norm_stats(p, C, sq, nk, n, pstat, rstd, inv_dim):
    for kt in range(nk):
        mm(p, pstat, pstat[:, :n], C['ones_bf'], C['ones_bf'][:], sq[kt], sq[kt][:, :n], kt == 0, kt == nk - 1)
    p.op('act', lambda e: e.activation(rstd[:, :n], pstat[:, :n], AF.Sqrt, bias=C['eps'][:, 0:1], scale=inv_dim),
         reads=[pstat, C['eps']], writes=[rstd])
    p.op('dve', lambda e: e.reciprocal(rstd[:, :n], rstd[:, :n]), reads=[rstd], writes=[rstd])


def modulate(p, C, hT, n, co, s, kind, sq, pstat, rstd, tmps, aT):
    for kt in range(KT):
        p.op('act', lambda e, kt=kt: e.activation(sq[kt][:, :n], hT[kt][:, :n], AF.Square), reads=[hT[kt]], writes=[sq[kt]])
    norm_stats(p, C, sq, KT, n, pstat, rstd, 1.0 / D)
    for kt in range(KT):
        tmp = tmps[kt % len(tmps)]
        p.op('dve', lambda e, kt=kt, tmp=tmp: e.tensor_tensor(tmp[:, :n], hT[kt][:, :n], rstd[:, :n], ALU.mult),
             reads=[hT[kt], rstd], writes=[tmp])
        p.op('pool', lambda e, kt=kt, tmp=tmp: e.tensor_scalar(aT[kt][:, :n], tmp[:, :n], co[:, s, 0, kind, kt:kt + 1], co[:, s, 1, kind, kt:kt + 1], ALU.mult, ALU.add),
             reads=[tmp, co], writes=[aT[kt]])


def post_residual(p, C, hT, ysb, n, co, s, kind, sq, pstat, rstd, tmps):
    norm_stats(p, C, sq, KT, n, pstat, rstd, 1.0 / D)
    for kt in range(KT):
        tmp = tmps[kt % len(tmps)]
        p.op('pool', lambda e, kt=kt, tmp=tmp: e.tensor_tensor(tmp[:, :n], ysb[kt][:, :n], rstd[:, :n], ALU.mult),
             reads=[ysb[kt], rstd], writes=[tmp])
        p.op('dve', lambda e, kt=kt, tmp=tmp: e.scalar_tensor_tensor(hT[kt][:, :n], tmp[:, :n], co[:, s, 2, kind, kt:kt + 1], hT[kt][:, :n], ALU.mult, ALU.add),
             reads=[tmp, co, hT[kt]], writes=[hT[kt]])


def ffn_phase(p, nc, C, h_in, h_out, w13s, w2s, co, s, chunks, out_off=0, bg_fn=None):
    p.open_scope()
    NT = 512
    hTf = [p.sb("hT", [128, KT, NT], F32) for _ in range(2)]
    hTs = [p.views(b, KT) for b in hTf]
    aTs = [p.views(p.sb("aT", [128, KT, NT], BF16), KT) for _ in range(2)]
    sq = p.views(p.sb("sq", [128, KT, NT], BF16), KT)
    rstd = p.sb("rstd", [128, NT], F32)
    rstd2 = p.sb("rstd2", [128, NT], F32)
    tmps = [p.sb("tmp", [128, NT], F32) for _ in range(3)]
    sgs = [p.sb("sg", [128, NT], F32) for _ in range(2)]
    HT = p.sb("HT", [128, JT, NT], BF16)
    HTj = [p.view(HT, HT.t[:, j, :]) for j in range(JT)]
    ysb = p.views(p.sb("ysb", [128, KT, NT], F32), KT)
    w13b = [p.sb("w13b", [128, KT, 256], BF16) for _ in range(3)]
    w2b = [p.sb("w2b", [128, JT, 128], BF16) for _ in range(2)]
    pstat = p.ps("pstat")
    pg = [p.ps("pg") for _ in range(2)]
    pu = [p.ps("pu") for _ in range(2)]
    py = [p.ps("py") for _ in range(2)]
    hin_v = h_in.rearrange("(kt p) t -> p kt t", p=128)
    hout_v = h_out.rearrange("(kt p) t -> p kt t", p=128)
    bg = bg_fn(prep_staging(p)) if bg_fn is not None else []

    def pre(ci):
        t0, n, kind = chunks[ci]
        hT = hTs[ci % 2]
        hf = hTf[ci % 2]
        p.dma('sp', lambda e: e.dma_start(out=hf[:, :, :n], in_=hin_v[:, :, t0:t0 + n]), writes=hT)
        modulate(p, C, hT, n, co, s, kind, sq, pstat, rstd, tmps, aTs[ci % 2])

    wcnt = [0, 0]
    pre(0)
    for ci, (t0, n, kind) in enumerate(chunks):
        hT = hTs[ci % 2]
        aT = aTs[ci % 2]
        for j in range(JT):
            wb = w13b[wcnt[0] % 3]
            wcnt[0] += 1
            p.dma('sp', lambda e, wb=wb, j=j: e.dma_start(out=wb[:].rearrange("p a b -> p (a b)"), in_=w13s[j]), writes=[wb])
            g = pg[j % 2]
            u = pu[j % 2]
            if j % 2 == 0:
                advance(bg)
            for kt in range(KT):
                mm(p, g, g[:, :n], wb, wb[:, kt, 0:128], aT[kt], aT[kt][:, :n], kt == 0, kt == KT - 1)
            for kt in range(KT):
                mm(p, u, u[:, :n], wb, wb[:, kt, 128:256], aT[kt], aT[kt][:, :n], kt == 0, kt == KT - 1)
            sg = sgs[j % 2]
            p.op('act', lambda e, g=g, sg=sg: e.activation(sg[:, :n], g[:, :n], AF.Silu), reads=[g], writes=[sg])
            p.op('dve', lambda e, u=u, sg=sg, j=j: e.tensor_tensor(HT[:, j, :n], sg[:, :n], u[:, :n], ALU.mult),
                 reads=[sg, u], writes=[HTj[j]])
        if ci + 1 < len(chunks):
            pre(ci + 1)
        for i in range(KT):
            wb = w2b[wcnt[1] % 2]
            wcnt[1] += 1
            p.dma('sp', lambda e, wb=wb, i=i: e.dma_start(out=wb[:].rearrange("p a b -> p (a b)"), in_=w2s[i]), writes=[wb])
            y = py[i % 2]
            for j in range(JT):
                mm(p, y, y[:, :n], wb, wb[:, j, :], HTj[j], HT[:, j, :n], j == 0, j == JT - 1)
            p.op('dve', lambda e, y=y, i=i: e.tensor_copy(ysb[i][:, :n], y[:, :n]), reads=[y], writes=[ysb[i]])
            p.op('act', lambda e, i=i: e.activation(sq[i][:, :n], ysb[i][:, :n], AF.Square), reads=[ysb[i]], writes=[sq[i]])
        post_residual(p, C, hT, ysb, n, co, s, kind, sq, pstat, rstd2, tmps)
        hf = hTf[ci % 2]
        p.dma('pool', lambda e, hf=hf, t0=t0, n=n: e.dma_start(out=hout_v[:, :, t0 - out_off:t0 - out_off + n], in_=hf[:, :, :n]), reads=hT)
    while bg:
        advance(bg)
    p.close_scope()


MLA_SCALE = 96 ** -0.5
SWA_SCALE = 64 ** -0.5
W_CQ, W_CKV, W_KPE, W_QS, W_KS, W_VS = 0, 256, 384, 416, 928, 1056


def attn_weight_groups():
    g = []
    g.append([(0, 128)])
    g.append([(128, 128)])
    g.append([(256, 128)])
    g.append([(320, 96), (0, 32)])
    g.append([(320, 64), (400, 16, -1), (384, 16), (0, 32)])
    for i in range(4):
        g.append([(W_QS + 128 * i, 128)])
    for i in range(4):
        b = W_QS + 128 * i
        g.append([(b + 32, 32, -1), (b, 32), (b + 96, 32, -1), (b + 64, 32)])
    for kv in range(2):
        b = W_KS + 64 * kv
        g.append([(b, 64), (b, 64)])
    for kv in range(2):
        b = W_KS + 64 * kv
        g.append([(b + 32, 32, -1), (b, 32), (b + 32, 32, -1), (b, 32)])
    g.append([(W_VS, 128)])
    return g


def wuq_groups():
    g = []
    for h in range(8):
        g.append([(96 * h, 96), (0, 32)])
    for h in range(8):
        g.append([(96 * h, 64), (96 * h + 80, 16, -1), (96 * h + 64, 16), (0, 32)])
    return g


def attn_proj_phase(p, nc, C, h_in, co, chunks, T, win_s, wuq_s, wk_s, wv_s, qn_d, kvn_d, ropeA, ropeB, scr):
    p.open_scope()
    NT = 512
    ident = C['ident']
    hTf = [p.sb("hT", [128, KT, NT], F32) for _ in range(2)]
    hTs = [p.views(b, KT) for b in hTf]
    aT = p.views(p.sb("aT", [128, KT, NT], BF16), KT)
    sq = p.views(p.sb("sq", [128, KT, NT], BF16), KT)
    rstd = p.sb("rstd", [128, NT], F32)
    rq = p.sb("rq", [128, NT], F32)
    rkv = p.sb("rkv", [128, NT], F32)
    tmps = [p.sb("tmp", [128, NT], F32) for _ in range(3)]
    pstat = p.ps("pstat")
    pa = [p.ps("pa") for _ in range(3)]
    pb = [p.ps("pb") for _ in range(3)]
    win = p.sb("win", [128, 18, KT, 128], BF16)
    p.dma('sp', lambda e: e.dma_start(out=win[:].rearrange("p g a b -> p g (a b)"), in_=win_s.rearrange("g p x -> p g x")), writes=[win])
    wuq = p.sb("wuq", [128, 16, 2, 128], BF16)
    p.dma('sp', lambda e: e.dma_start(out=wuq[:].rearrange("p g a b -> p g (a b)"), in_=wuq_s.rearrange("g p x -> p g x")), writes=[wuq])
    wk = p.sb("wk", [128, 512], BF16)
    p.dma('sp', lambda e: e.dma_start(out=wk[:], in_=wk_s[0]), writes=[wk])
    wv = p.sb("wv", [128, 512], BF16)
    p.dma('sp', lambda e: e.dma_start(out=wv[:], in_=wv_s[0]), writes=[wv])
    stage = p.sb("stage", [128, 128], F32)
    qn = p.sb("qn", [128, 2], F32)
    kvn = p.sb("kvn", [128, 1], F32)
    load_cols(p, qn, qn[:], qn_d, 2, ident, pstat, stage)
    load_cols(p, kvn, kvn[:], kvn_d, 1, ident, pstat, stage)
    cqn = p.views(p.sb("cqn", [128, 2, NT], BF16), 2)
    ckvn = p.sb("ckvn", [128, NT], BF16)
    tA = p.sb("tA", [128, 2, NT], F32)
    tB = p.sb("tB", [128, 2, NT], F32)
    qsb = p.sb("qsb", [128, NT], F32)
    r1 = p.sb("r1", [128, NT], F32)
    r2 = p.sb("r2", [128, NT], F32)
    r3 = p.sb("r3", [128, NT], F32)
    Qst = [p.sb("Qst", [96, NT], BF16) for _ in range(2)]
    Kst = p.sb("Kst", [96, 8, NT], BF16)
    kpe = p.sb("kpe", [96, NT], BF16)
    Vst = p.sb("Vst", [128, 4, 8, 128], BF16)
    VSst = p.sb("VSst", [128, 4, 2, 128], BF16)
    p.op('pool', lambda e: e.memset(Vst[:], 1.0), writes=[Vst])
    p.op('pool', lambda e: e.memset(VSst[:], 1.0), writes=[VSst])
    Sst = [p.sb("Sst", [128, NT], BF16) for _ in range(2)]
    hin_v = h_in.rearrange("(kt p) t -> p kt t", p=128)

    def proj(dst_ps, gidx, n):
        for kt in range(KT):
            mm(p, dst_ps, dst_ps[:, :n], win, win[:, gidx, kt, :], aT[kt], aT[kt][:, :n], kt == 0, kt == KT - 1)

    def rope_rows(ps_x, ps_r, tab, lo, hi, n, out_buf, out_ap, scale):
        p.op('dve', lambda e: e.tensor_tensor(r1[lo:hi, :n], ps_x[lo:hi, :n], tab[lo:hi, 0, :n], ALU.mult), reads=[ps_x, tab], writes=[r1])
        p.op('dve', lambda e: e.tensor_tensor(r2[lo:hi, :n], ps_r[lo:hi, :n], tab[lo:hi, 1, :n], ALU.mult), reads=[ps_r, tab], writes=[r2])
        p.op('dve', lambda e: e.tensor_tensor(r3[lo:hi, :n], r1[lo:hi, :n], r2[lo:hi, :n], ALU.add), reads=[r1, r2], writes=[r3])
        p.op('act', lambda e: e.activation(out_ap, r3[lo:hi, :n], AF.Copy, scale=scale), reads=[r3], writes=[out_buf])

    for ci, (t0, n, kind) in enumerate(chunks):
        hT = hTs[ci % 2]
        hf = hTf[ci % 2]
        p.dma('sp', lambda e, hf=hf, t0=t0, n=n: e.dma_start(out=hf[:, :, :n], in_=hin_v[:, :, t0:t0 + n]), writes=hT)
        p.dma('sp', lambda e, t0=t0, n=n: e.dma_start(out=tA[64:96, :, :n], in_=ropeA[:, :, t0:t0 + n]), writes=[tA])
        p.dma('sp', lambda e, t0=t0, n=n: e.dma_start(out=tB[:, :, :n], in_=ropeB[:, :, t0:t0 + n]), writes=[tB])
        modulate(p, C, hT, n, co, 1, kind, sq, pstat, rstd, tmps, aT)
        for i in range(2):
            proj(pa[i], i, n)
            p.op('dve', lambda e, i=i: e.tensor_copy(tmps[i][:, :n], pa[i][:, :n]), reads=[pa[i]], writes=[tmps[i]])
            p.op('act', lambda e, i=i: e.activation(sq[i][:, :n], tmps[i][:, :n], AF.Square), reads=[tmps[i]], writes=[sq[i]])
        norm_stats(p, C, sq, 2, n, pstat, rq, 1.0 / 256)
        for i in range(2):
            p.op('dve', lambda e, i=i: e.tensor_tensor(tmps[i][:, :n], tmps[i][:, :n], rq[:, :n], ALU.mult), reads=[tmps[i], rq], writes=[tmps[i]])
            p.op('pool', lambda e, i=i: e.tensor_scalar(cqn[i][:, :n], tmps[i][:, :n], qn[:, i:i + 1], None, ALU.mult), reads=[tmps[i], qn], writes=[cqn[i]])
        proj(pa[2], 2, n)
        p.op('dve', lambda e: e.tensor_copy(tmps[2][:, :n], pa[2][:, :n]), reads=[pa[2]], writes=[tmps[2]])
        p.op('act', lambda e: e.activation(sq[2][:, :n], tmps[2][:, :n], AF.Square), reads=[tmps[2]], writes=[sq[2]])
        norm_stats(p, C, sq[2:3], 1, n, pstat, rkv, 1.0 / 128)
        p.op('dve', lambda e: e.tensor_tensor(tmps[2][:, :n], tmps[2][:, :n], rkv[:, :n], ALU.mult), reads=[tmps[2], rkv], writes=[tmps[2]])
        p.op('pool', lambda e: e.tensor_scalar(ckvn[:, :n], tmps[2][:, :n], kvn[:, 0:1], None, ALU.mult), reads=[tmps[2], kvn], writes=[ckvn])
        proj(pa[0], 3, n)
        proj(pb[0], 4, n)
        rope_rows(pa[0], pb[0], tA, 64, 96, n, kpe, kpe[64:96, :n], 1.0)
        for h in range(8):
            pk = pa[1 + h % 2]
            mm(p, pk, pk[0:64, :n], wk, wk[:, 64 * h:64 * h + 64], ckvn, ckvn[:, :n], True, True)
            p.op('dve', lambda e, pk=pk, h=h: e.tensor_copy(Kst[0:64, h, :n], pk[0:64, :n]), reads=[pk], writes=[Kst])
            p.op('act', lambda e, h=h, n=n: e.copy(Kst[64:96, h, :n], kpe[64:96, :n]), reads=[kpe], writes=[Kst])
        p.dma('pool', lambda e, t0=t0, n=n: e.dma_start(out=scr['KT'].rearrange("h r t -> r h t")[:, :, t0:t0 + n], in_=Kst[:, :, :n]), reads=[Kst])
        for h in range(8):
            pq = pa[h % 2]
            pqr = pb[h % 2]
            Q = Qst[h % 2]
            for kt in range(2):
                mm(p, pq, pq[:, :n], wuq, wuq[:, h, kt, :], cqn[kt], cqn[kt][:, :n], kt == 0, kt == 1)
            for kt in range(2):
                mm(p, pqr, pqr[:, :n], wuq, wuq[:, 8 + h, kt, :], cqn[kt], cqn[kt][:, :n], kt == 0, kt == 1)
            p.op('dve', lambda e, pq=pq: e.tensor_copy(qsb[0:64, :n], pq[0:64, :n]), reads=[pq], writes=[qsb])
            rope_rows(pq, pqr, tA, 64, 96, n, Q, Q[64:96, :n], MLA_SCALE)
            p.op('act', lambda e, Q=Q: e.activation(Q[0:64, :n], qsb[0:64, :n], AF.Copy, scale=MLA_SCALE), reads=[qsb], writes=[Q])
            p.dma('pool', lambda e, Q=Q, h=h, t0=t0, n=n: e.dma_start(out=scr['QT'][h, :, t0:t0 + n], in_=Q[:, :n]), reads=[Q])
        nb = n // 128
        for tb in range(nb):
            pv = pa[tb % 2]
            mm(p, pv, pv[:, 0:512], ckvn, ckvn[:, tb * 128:(tb + 1) * 128], wv, wv[:, :], True, True)
            p.op('dve', lambda e, pv=pv, tb=tb: e.tensor_copy(Vst[:, tb, :, 0:64], pv[:, 0:512].rearrange("p (h d) -> p h d", d=64)), reads=[pv], writes=[Vst])
        p.dma('pool', lambda e, t0=t0, nb=nb: e.dma_start(out=scr['V'][t0 // 128:t0 // 128 + nb].rearrange("b p h d -> p b h d"), in_=Vst[:, 0:nb]), reads=[Vst])
        for i in range(4):
            proj(pa[i % 2], 5 + i, n)
            proj(pb[i % 2], 9 + i, n)
            S = Sst[i % 2]
            rope_rows(pa[i % 2], pb[i % 2], tB, 0, 128, n, S, S[:, :n], SWA_SCALE)
            p.dma('pool', lambda e, S=S, i=i, t0=t0, n=n: e.dma_start(out=scr['QS'][i, :, t0:t0 + n], in_=S[:, :n]), reads=[S])
        for kv in range(2):
            proj(pa[kv], 13 + kv, n)
            proj(pb[kv], 15 + kv, n)
            S = Sst[kv]
            rope_rows(pa[kv], pb[kv], tB, 0, 128, n, S, S[:, :n], 1.0)
            p.dma('pool', lambda e, S=S, kv=kv, t0=t0, n=n: e.dma_start(out=scr['KS'][kv, :, t0:t0 + n], in_=S[:, :n]), reads=[S])
        for tb in range(nb):
            pv = pa[2]
            for kt in range(KT):
                mm(p, pv, pv[:, 0:128], aT[kt], aT[kt][:, tb * 128:(tb + 1) * 128], win, win[:, 17, kt, :], kt == 0, kt == KT - 1)
            p.op('dve', lambda e, pv=pv, tb=tb: e.tensor_copy(VSst[:, tb, :, 0:64], pv[:, 0:128].rearrange("p (h d) -> p h d", d=64)), reads=[pv], writes=[VSst])
        p.dma('pool', lambda e, t0=t0, nb=nb: e.dma_start(out=scr['VS'][t0 // 128:t0 // 128 + nb].rearrange("b p h d -> p b h d"), in_=VSst[:, 0:nb]), reads=[VSst])
    p.close_scope()


def normalize_store(p, C, pO, n, Osb, pR, On, dst_ap, extra_den=None, act_recip=False):
    p.op('dve', lambda e: e.tensor_copy(Osb[:, :n], pO[:, :n]), reads=[pO], writes=[Osb])
    if extra_den is not None:
        p.op('dve', lambda e: e.tensor_scalar(Osb[64:128, :n], Osb[64:128, :n], extra_den, None, ALU.add), reads=[Osb], writes=[Osb])
    if act_recip:
        p.op('act', lambda e: e.activation(Osb[64:128, :n], Osb[64:128, :n], AF.Ln), reads=[Osb], writes=[Osb])
        p.op('act', lambda e: e.activation(Osb[64:128, :n], Osb[64:128, :n], AF.Exp, scale=-1.0), reads=[Osb], writes=[Osb])
    else:
        p.op('dve', lambda e: e.reciprocal(Osb[64:128, :n], Osb[64:128, :n]), reads=[Osb], writes=[Osb])
    mm(p, pR, pR[0:64, :n], C['shiftI'], C['shiftI'][:, :], Osb, Osb[:, :n], True, True)
    p.op('dve', lambda e: e.tensor_tensor(On[0:64, :n], Osb[0:64, :n], pR[0:64, :n], ALU.mult), reads=[Osb, pR], writes=[On])
    p.dma('pool', lambda e: e.dma_start(out=dst_ap, in_=On[0:64, :n]), reads=[On])


def mla_phase(p, nc, C, chunks, T, scr):
    p.open_scope()
    NT = 512
    G3 = 3
    NKT = T // 128
    Kh = [p.sb("Kh", [96, T], BF16) for _ in range(2)]
    Vh = [p.sb("Vh", [128, NKT, 128], BF16) for _ in range(2)]
    Qc = [p.sb("Qc", [96, NT], BF16) for _ in range(2)]
    Pt = [p.sb("Pt", [128, G3, NT], BF16) for _ in range(2)]
    Osb = [p.sb("Osb", [128, NT], F32) for _ in range(2)]
    On = [p.sb("On", [64, NT], BF16) for _ in range(2)]
    pS = [p.ps("pS", (128, G3, NT)) for _ in range(2)]
    pO = p.ps("pO")
    pR = p.ps("pR")
    qi = 0
    for h in range(8):
        K = Kh[h % 2]
        V = Vh[h % 2]
        p.dma('sp', lambda e, K=K, h=h: e.dma_start(out=K[:], in_=scr['KT'][h]), writes=[K])
        p.dma('sp', lambda e, V=V, h=h: e.dma_start(out=V[:], in_=scr['V'][:, :, h, :].rearrange("b p d -> p b d")), writes=[V])
        for ci, (t0, n, kind) in enumerate(chunks):
            Q = Qc[qi % 2]
            on = On[qi % 2]
            osb = Osb[qi % 2]
            qi += 1
            p.dma('sp', lambda e, Q=Q, h=h, t0=t0, n=n: e.dma_start(out=Q[:, :n], in_=scr['QT'][h, :, t0:t0 + n]), writes=[Q])
            kts = list(range(2)) if kind == 1 else list(range(NKT))
            groups = [kts[i:i + G3] for i in range(0, len(kts), G3)]

            def smm(gi):
                ps = pS[gi % 2]
                for j, kt in enumerate(groups[gi]):
                    mm(p, ps, ps[:, j, :n], K, K[0:96, kt * 128:(kt + 1) * 128], Q, Q[0:96, :n], True, True)

            smm(0)
            nmm = len(kts)
            done = 0
            for gi, grp in enumerate(groups):
                if gi + 1 < len(groups):
                    smm(gi + 1)
                ps = pS[gi % 2]
                pt = Pt[gi % 2]
                ng = len(grp)
                p.op('act', lambda e, ps=ps, pt=pt, ng=ng, n=n: e.activation(pt[:, 0:ng, :n], ps[:, 0:ng, :n], AF.Exp), reads=[ps], writes=[pt])
                for j, kt in enumerate(grp):
                    mm(p, pO, pO[:, :n], V, V[:, kt, :], pt, pt[:, j, :n], done == 0, done == nmm - 1)
                    done += 1
            normalize_store(p, C, pO, n, osb, pR, on, scr['OT'][h * 64:(h + 1) * 64, t0:t0 + n])
    p.close_scope()


def swa_phase(p, nc, C, chunks, T, scr, sink_d):
    p.open_scope()
    NT = 512
    NKT = T // 128
    KS = p.sb("KS", [128, 2, T], BF16)
    p.dma('sp', lambda e: e.dma_start(out=KS[:], in_=scr['KS'].rearrange("k p t -> p k t")), writes=[KS])
    VS = p.sb("VS", [128, NKT, 2, 128], BF16)
    p.dma('sp', lambda e: e.dma_start(out=VS[:], in_=scr['VS'].rearrange("b p k d -> p b k d")), writes=[VS])
    esink = p.sb("esink", [128, 8], F32)
    p.dma('sp', lambda e: e.dma_start(out=esink[:], in_=sink_d), writes=[esink])
    p.op('act', lambda e: e.activation(esink[:], esink[:], AF.Exp), reads=[esink], writes=[esink])
    QS = [p.sb("QS", [128, 4, NT], BF16) for _ in range(2)]
    Pc4 = [p.sb("Pc", [128, NT], BF16) for _ in range(4)]
    Pl8 = [p.sb("Pl", [128, 384], BF16) for _ in range(8)]
    Osb2 = [p.sb("Osb", [128, NT], F32) for _ in range(2)]
    On = [p.sb("On", [64, NT], BF16) for _ in range(2)]
    pC = [p.ps("pC") for _ in range(2)]
    pL = [p.ps("pL") for _ in range(2)]
    pO = [p.ps("pO") for _ in range(2)]
    pR = p.ps("pR")
    mprev = C['mprev']
    mnext = C['mnext']
    oi = 0
    for ci, (t0, n, kind) in enumerate(chunks):
        Q = QS[ci % 2]
        p.dma('sp', lambda e, Q=Q, t0=t0, n=n: e.dma_start(out=Q[:, :, :n], in_=scr['QS'].rearrange("i p t -> p i t")[:, :, t0:t0 + n]), writes=[Q])
        nb = n // 128
        for hh in range(8):
            i, e2 = hh // 2, hh % 2
            kv = hh // 4
            lo, hi = 64 * e2, 64 * e2 + 64
            po = pO[oi % 2]
            on = On[oi % 2]
            Osb = Osb2[oi % 2]
            Pc = Pc4[2 * (oi % 2):2 * (oi % 2) + 2]
            Pl = Pl8[4 * (oi % 2):4 * (oi % 2) + 4]
            oi += 1
            for c2 in range(2):
                mm(p, pC[c2], pC[c2][:, :n], KS, KS[lo:hi, kv, c2 * 128:(c2 + 1) * 128], Q, Q[lo:hi, i, :n], True, True)
                p.op('act', lambda e, c2=c2, Pc=Pc, n=n: e.activation(Pc[c2][:, :n], pC[c2][:, :n], AF.Exp), reads=[pC[c2]], writes=[Pc[c2]])
            loc = []
            if kind == 0:
                for qb in range(nb):
                    kt_c = (t0 // 128) + qb
                    tiles = [(kt_c - 1, 0), (kt_c, 1), (kt_c + 1, 2)]
                    tiles = [(kt, s) for kt, s in tiles if 2 <= kt < NKT]
                    pl = pL[qb % 2]
                    P = Pl[qb]
                    for kt, s in tiles:
                        mm(p, pl, pl[:, s * 128:(s + 1) * 128], KS, KS[lo:hi, kv, kt * 128:(kt + 1) * 128], Q, Q[lo:hi, i, qb * 128:(qb + 1) * 128], True, True)
                    c0, c1 = tiles[0][1] * 128, tiles[-1][1] * 128 + 128
                    p.op('act', lambda e, pl=pl, P=P, c0=c0, c1=c1: e.activation(P[:, c0:c1], pl[:, c0:c1], AF.Exp), reads=[pl], writes=[P])
                    for kt, s in tiles:
                        if s == 0:
                            p.op('dve', lambda e, P=P: e.tensor_tensor(P[:, 0:128], P[:, 0:128], mprev[:], ALU.mult), reads=[P, mprev], writes=[P])
                        if s == 2:
                            p.op('dve', lambda e, P=P: e.tensor_tensor(P[:, 256:384], P[:, 256:384], mnext[:], ALU.mult), reads=[P, mnext], writes=[P])
                    loc.append(tiles)
            for qb in range(nb):
                items = [(0, Pc[0], qb * 128), (1, Pc[1], qb * 128)]
                if kind == 0:
                    for kt, s in loc[qb]:
                        items.append((kt, Pl[qb], s * 128))
                for k2, (kt, P, c0) in enumerate(items):
                    mm(p, po, po[:, qb * 128:(qb + 1) * 128], VS, VS[:, kt, kv, :], P, P[:, c0:c0 + 128], k2 == 0, k2 == len(items) - 1)
            normalize_store(p, C, po, n, Osb, pR, on, scr['OT'][(8 + hh) * 64:(9 + hh) * 64, t0:t0 + n], extra_den=esink[64:128, hh:hh + 1], act_recip=True)
    p.close_scope()


def mixout_phase(p, nc, C, y_src, wout_s, h_in, h_out, co, s, chunks, nkt_in):
    p.open_scope()
    NT = 512
    hTf = [p.sb("hT", [128, KT, NT], F32) for _ in range(2)]
    hTs = [p.views(b, KT) for b in hTf]
    yTf = [p.sb("yT", [128, nkt_in, NT], BF16) for _ in range(2)]
    sq = p.views(p.sb("sq", [128, KT, NT], BF16), KT)
    rstd2 = p.sb("rstd2", [128, NT], F32)
    tmps = [p.sb("tmp", [128, NT], F32) for _ in range(3)]
    ysb = p.views(p.sb("ysb", [128, KT, NT], F32), KT)
    wo = p.sb("wo", [128, KT, nkt_in, 128], BF16)
    p.dma('sp', lambda e: e.dma_start(out=wo[:].rearrange("p i a b -> p i (a b)"), in_=wout_s.rearrange("i p x -> p i x")), writes=[wo])
    pstat = p.ps("pstat")
    py = [p.ps("py") for _ in range(2)]
    hin_v = h_in.rearrange("(kt p) t -> p kt t", p=128)
    hout_v = h_out.rearrange("(kt p) t -> p kt t", p=128)
    ysrc_v = y_src.rearrange("(kt p) t -> p kt t", p=128)
    for ci, (t0, n, kind) in enumerate(chunks):
        hT = hTs[ci % 2]
        hf = hTf[ci % 2]
        yT = yTf[ci % 2]
        p.dma('sp', lambda e, hf=hf, t0=t0, n=n: e.dma_start(out=hf[:, :, :n], in_=hin_v[:, :, t0:t0 + n]), writes=hT)
        p.dma('sp', lambda e, yT=yT, t0=t0, n=n: e.dma_start(out=yT[:, :, :n], in_=ysrc_v[:, :, t0:t0 + n]), writes=[yT])
        for i in range(KT):
            y = py[i % 2]
            for j in range(nkt_in):
                mm(p, y, y[:, :n], wo, wo[:, i, j, :], yT, yT[:, j, :n], j == 0, j == nkt_in - 1)
            p.op('dve', lambda e, y=y, i=i: e.tensor_copy(ysb[i][:, :n], y[:, :n]), reads=[y], writes=[ysb[i]])
            p.op('act', lambda e, i=i: e.activation(sq[i][:, :n], ysb[i][:, :n], AF.Square), reads=[ysb[i]], writes=[sq[i]])
        post_residual(p, C, hT, ysb, n, co, s, kind, sq, pstat, rstd2, tmps)
        p.dma('pool', lambda e, hf=hf, t0=t0, n=n: e.dma_start(out=hout_v[:, :, t0:t0 + n], in_=hf[:, :, :n]), reads=hT)
    p.close_scope()


S5T = 8
MAGIC = 12582912.0
TWO_PI = 6.283185307179586
PI = 3.141592653589793


def s5_in_phase(p, nc, C, h_in, co, chunks, w5_s, dsk_d, uT_d, y0_d):
    p.open_scope()
    NT = 512
    ident = C['ident']
    hTf = [p.sb("hT", [128, KT, NT], F32) for _ in range(2)]
    hTs = [p.views(b, KT) for b in hTf]
    aT = p.views(p.sb("aT", [128, KT, NT], BF16), KT)
    sq = p.views(p.sb("sq", [128, KT, NT], BF16), KT)
    rstd = p.sb("rstd", [128, NT], F32)
    tmps = [p.sb("tmp", [128, NT], F32) for _ in range(3)]
    pstat = p.ps("pstat")
    pu = [p.ps("pu") for _ in range(2)]
    w5 = p.sb("w5", [128, 8, KT, 128], BF16)
    p.dma('sp', lambda e: e.dma_start(out=w5[:].rearrange("p g a b -> p g (a b)"), in_=w5_s.rearrange("g p x -> p g x")), writes=[w5])
    stage = p.sb("stage", [128, 128], F32)
    dsk = p.sb("dsk", [128, 8], F32)
    load_cols(p, dsk, dsk[:], dsk_d, 8, ident, pstat, stage)
    ub = [p.sb("ub", [128, NT], BF16) for _ in range(2)]
    y0 = [p.sb("y0", [128, NT], F32) for _ in range(2)]
    hin_v = h_in.rearrange("(kt p) t -> p kt t", p=128)
    for ci, (t0, n, kind) in enumerate(chunks):
        hT = hTs[ci % 2]
        hf = hTf[ci % 2]
        p.dma('sp', lambda e, hf=hf, t0=t0, n=n: e.dma_start(out=hf[:, :, :n], in_=hin_v[:, :, t0:t0 + n]), writes=hT)
        modulate(p, C, hT, n, co, 1, kind, sq, pstat, rstd, tmps, aT)
        for ct in range(8):
            ps = pu[ct % 2]
            for kt in range(KT):
                mm(p, ps, ps[:, :n], w5, w5[:, ct, kt, :], aT[kt], aT[kt][:, :n], kt == 0, kt == KT - 1)
            u = ub[ct % 2]
            p.op('dve', lambda e, ps=ps, u=u: e.tensor_copy(u[:, :n], ps[:, :n]), reads=[ps], writes=[u])
            p.dma('pool', lambda e, u=u, ct=ct, t0=t0, n=n: e.dma_start(out=uT_d[ct * 128:(ct + 1) * 128, t0:t0 + n], in_=u[:, :n]), reads=[u])
            if kind == 0:
                y = y0[ct % 2]
                p.op('dve', lambda e, ps=ps, y=y, ct=ct: e.tensor_scalar(y[:, :n], ps[:, :n], dsk[:, ct:ct + 1], None, ALU.mult), reads=[ps, dsk], writes=[y])
                p.dma('pool', lambda e, y=y, ct=ct, t0=t0, n=n: e.dma_start(out=y0_d[ct * 128:(ct + 1) * 128, t0 - NCTX:t0 - NCTX + n], in_=y[:, :n]), reads=[y])
    p.close_scope()


def s5_scan_phase(p, nc, C, T, SEQ, uT_d, y0_d, gT_d, lam_d, B12_d, C1_d, J_d, rmask_d):
    p.open_scope()
    ident = C['ident']
    NSC = T // S5T
    NCC = NCTX // S5T
    NLC = SEQ // S5T
    XW = NSC + 2
    NB = NSC // 8
    assert NB * 8 == NSC
    NRB = 0
    while (1 << NRB) < NB:
        NRB += 1
    NG = 8
    KLIST = list(range(1, 9)) + [8 * m for m in range(2, 8)] + [64 << r for r in range(NRB)]
    KIDX = {k: i for i, k in enumerate(KLIST)}
    NK = len(KLIST)
    NUL = 56 + NRB * 8
    Jt = p.sb("Jt", [128, 128], F32)
    p.dma('sp', lambda e: e.dma_start(out=Jt[:], in_=J_d), writes=[Jt])
    rmask = p.sb("rmask", [128, 8], F32)
    p.dma('sp', lambda e: e.dma_start(out=rmask[:], in_=rmask_d), writes=[rmask])
    pb = [p.ps("pb%d" % i) for i in range(8)]
    u_de = p.sb("u_de", [128, S5T, NSC], BF16)
    Y = p.sb("Y", [128, SEQ], F32)
    Xb = p.sb("Xb", [128, 8, NSC], BF16)
    Xbg = p.views(Xb, 8)
    Hin = p.sb("Hin", [128, 8, NSC], BF16)
    Hing = p.views(Hin, 8)
    Wf = p.sb("Wf", [128, 8, NB], F32)
    Wfg = p.views(Wf, 8)
    Wb = p.sb("Wb", [128, 8, NB + 2], BF16)
    Wbg = p.views(Wb, 8)
    p.op('pool', lambda e: e.memset(Wb[:], 0.0), writes=[Wb] + Wbg)
    identb = p.sb("identb", [128, 128], BF16)
    p.op('dve', lambda e: e.tensor_copy(identb[:], ident[:]), reads=[ident], writes=[identb])
    CtL = p.sb("CtL", [128, 8, 8, 128], BF16)
    p.op('pool', lambda e: e.memset(CtL[:], 0.0), writes=[CtL])
    CtLv = [[Buf("CtLv", CtL.t[:, t, g, :]) for g in range(NG)] for t in range(8)]
    UL = p.sb("UL", [128, max(NUL, 64), 128], BF16)
    ULv = [Buf("ULv", UL.t[:, i, :]) for i in range(max(NUL, 64))]
    KdL = p.sb("KdL", [128, 8, 128], BF16)
    MBm = p.sb("MBm", [128, 8, 128], F32)
    p.op('pool', lambda e: e.memset(MBm[:], 0.0), writes=[MBm])
    MBv = p.views(MBm, 8)
    ctf = p.sb("ctf", [128, 9, 8, 16], F32)
    ctfv = [[Buf("ctfv", ctf.t[:, k, g, :]) for g in range(NG)] for k in range(9)]
    Qk = [p.sb("Qk", [128, 128], F32) for _ in range(8)]
    Bs1 = p.sb("Bs1", [128, 8, 16], F32)
    Bs2 = p.sb("Bs2", [128, 8, 16], F32)
    Cs1 = p.sb("Cs1", [128, 8, 16], F32)
    Cs2 = p.sb("Cs2", [128, 8, 16], F32)
    tb = {nm: p.sb("tb_" + nm, [128, 64], F32) for nm in
          ['lre', 'lim', 'lst', 'dt', 'lr', 'lrdt', 'mag', 'ang', 'angk', 'red', 'sin', 'cos', 'ar', 'ai', 'nai', 'den', 'am1', 'fr', 'fi', 'nfi', 't1', 't2']}
    tbd = [{nm: p.sb("tbd_" + nm, [128, 64], F32) for nm in ['fr', 'S2']} for _ in range(2)]
    pw = [[{nm: p.sb("pw_" + nm, [128, 64], F32) for nm in (['AR', 'T5', 'T3', 'T4'] if KLIST[ki] <= 8 else ['T3', 'T4'])}
           for ki in range(NK)] for _ in range(2)]

    def ew(eng, fn, reads, writes):
        p.op(eng, fn, reads=reads, writes=writes)

    def reduce_sin(dst, src, shift):
        r = tb['red']
        ew('dve', lambda e: e.tensor_scalar(r[:], src[:], float(shift), None, ALU.add), [src], [r])
        ew('dve', lambda e: e.tensor_scalar(tb['t2'][:], r[:], float(1.0 / TWO_PI), MAGIC, ALU.mult, ALU.add), [r], [tb['t2']])
        ew('dve', lambda e: e.tensor_scalar(tb['t2'][:], tb['t2'][:], -MAGIC, None, ALU.add), [tb['t2']], [tb['t2']])
        ew('dve', lambda e: e.scalar_tensor_tensor(r[:], tb['t2'][:], float(-TWO_PI), r[:], ALU.mult, ALU.add), [tb['t2'], r], [r])
        ew('dve', lambda e: e.tensor_scalar(r[:], r[:], float(PI), float(-PI), ALU.min, ALU.max), [r], [r])
        ew('act', lambda e: e.activation(dst[:], r[:], AF.Sin), [r], [dst])

    def tt(dst, a, b, op, eng='dve'):
        ew(eng, lambda e: e.tensor_tensor(dst[:], a[:], b[:], op), [a, b], [dst])

    def half_copy(dst, top, bot):
        ew('dve', lambda e: e.tensor_copy(dst[0:64, :], top[0:64, :]), [top], [dst])
        ew('dve', lambda e: e.tensor_copy(dst[64:128, :], bot[64:128, :]), [bot], [dst])

    def power(k, dst):
        ew('dve', lambda e: e.tensor_scalar(tb['t1'][:], tb['lrdt'][:], float(k), None, ALU.mult), [tb['lrdt']], [tb['t1']])
        ew('act', lambda e: e.activation(tb['mag'][:], tb['t1'][:], AF.Exp), [tb['t1']], [tb['mag']])
        ew('dve', lambda e: e.tensor_scalar(tb['angk'][:], tb['ang'][:], float(k), None, ALU.mult), [tb['ang']], [tb['angk']])
        reduce_sin(tb['sin'], tb['angk'], 0.0)
        reduce_sin(tb['cos'], tb['angk'], PI / 2)
        tt(tb['ar'], tb['mag'], tb['cos'], ALU.mult)
        tt(tb['ai'], tb['mag'], tb['sin'], ALU.mult)
        ew('dve', lambda e: e.tensor_scalar(tb['nai'][:], tb['ai'][:], -1.0, None, ALU.mult), [tb['ai']], [tb['nai']])

    for d in range(2):
        for i, nm in enumerate(['lre', 'lim', 'lst']):
            p.dma('sp', lambda e, i=i, nm=nm, d=d: e.dma_start(out=tb[nm][:], in_=lam_d[d, i]), writes=[tb[nm]])
        ew('act', lambda e: e.activation(tb['dt'][:], tb['lst'][:], AF.Exp), [tb['lst']], [tb['dt']])
        ew('dve', lambda e: e.tensor_scalar(tb['lr'][:], tb['lre'][:], -1e-4, None, ALU.min), [tb['lre']], [tb['lr']])
        tt(tb['lrdt'], tb['lr'], tb['dt'], ALU.mult)
        tt(tb['ang'], tb['lim'], tb['dt'], ALU.mult)
        for ki, k in enumerate(KLIST):
            power(k, None)
            t = pw[d][ki]
            if 'AR' in t:
                ew('pool', lambda e, t=t: e.tensor_copy(t['AR'][:], tb['ar'][:]), [tb['ar']], [t['AR']])
                half_copy(t['T5'], tb['ai'], tb['nai'])
            half_copy(t['T3'], tb['ar'], tb['nai'])
            half_copy(t['T4'], tb['ai'], tb['ar'])
            if k == 1:
                tt(tb['den'], tb['lr'], tb['lr'], ALU.mult)
                tt(tb['t1'], tb['lim'], tb['lim'], ALU.mult)
                tt(tb['den'], tb['den'], tb['t1'], ALU.add)
                ew('dve', lambda e: e.reciprocal(tb['den'][:], tb['den'][:]), [tb['den']], [tb['den']])
                ew('dve', lambda e: e.tensor_scalar(tb['am1'][:], tb['ar'][:], -1.0, None, ALU.add), [tb['ar']], [tb['am1']])
                tt(tb['t1'], tb['am1'], tb['lr'], ALU.mult)
                tt(tb['t2'], tb['ai'], tb['lim'], ALU.mult)
                tt(tb['t1'], tb['t1'], tb['t2'], ALU.add)
                tt(tbd[d]['fr'], tb['t1'], tb['den'], ALU.mult)
                tt(tb['t1'], tb['ai'], tb['lr'], ALU.mult)
                tt(tb['t2'], tb['am1'], tb['lim'], ALU.mult)
                tt(tb['t1'], tb['t1'], tb['t2'], ALU.subtract)
                tt(tb['fi'], tb['t1'], tb['den'], ALU.mult)
                ew('dve', lambda e: e.tensor_scalar(tb['nfi'][:], tb['fi'][:], -1.0, None, ALU.mult), [tb['fi']], [tb['nfi']])
                half_copy(tbd[d]['S2'], tb['nfi'], tb['fi'])

    Xbflat = Xb[:].rearrange("p a b -> p (a b)")
    for ct in range(8):
        for d in range(2):
            fwd = (d == 0)
            if fwd:
                pieces = [(0, NCC, 0)]
                c = 0
                while c < NLC:
                    w = min(512, NLC - c)
                    pieces.append((NCC + c, w, NCTX + c * S5T))
                    c += w
            else:
                pieces = []
                c = 0
                while c < NLC:
                    w = min(512, NLC - c)
                    pieces.append((c, w, NCTX + c * S5T))
                    c += w
                pieces.append((NLC, NCC, 0))
            if d == 0:
                p.sync_all()
                p.dma('sp', lambda e, ct=ct: e.dma_start(out=Xbflat[:, 0:T], in_=uT_d[ct * 128:(ct + 1) * 128, :]), writes=[Xb])
                uv = Xbflat[:, 0:T].rearrange("p (c s) -> p c s", s=S5T)
                for s in range(S5T):
                    eng = 'dve' if s % 2 == 0 else 'act'
                    if eng == 'dve':
                        ew('dve', lambda e, s=s, uv=uv: e.tensor_copy(u_de[:, s, :], uv[:, :, s]), [Xb], [u_de])
                    else:
                        ew('act', lambda e, s=s, uv=uv: e.copy(u_de[:, s, :], uv[:, :, s]), [Xb], [u_de])
                p.dma('sp', lambda e, ct=ct: e.dma_start(out=Y[:], in_=y0_d[ct * 128:(ct + 1) * 128, :]), writes=[Y])
                p.sync_all()
            p.dma('sp', lambda e, ct=ct, d=d: e.dma_start(out=Bs1[:], in_=B12_d[d, 0, :, ct * 8:(ct + 1) * 8, :]), writes=[Bs1])
            p.dma('sp', lambda e, ct=ct, d=d: e.dma_start(out=Bs2[:], in_=B12_d[d, 1, :, ct * 8:(ct + 1) * 8, :]), writes=[Bs2])
            p.dma('sp', lambda e, ct=ct, d=d: e.dma_start(out=Cs1[:], in_=C1_d[d, 0, :, ct * 8:(ct + 1) * 8, :]), writes=[Cs1])
            p.dma('sp', lambda e, ct=ct, d=d: e.dma_start(out=Cs2[:], in_=C1_d[d, 1, :, ct * 8:(ct + 1) * 8, :]), writes=[Cs2])
            ew('pool', lambda e: e.tensor_scalar(Cs1[64:128], Cs1[64:128], -1.0, None, ALU.mult), [Cs1], [Cs1])
            ew('pool', lambda e: e.tensor_scalar(Cs2[0:64], Cs2[0:64], -1.0, None, ALU.mult), [Cs2], [Cs2])
            t_fr, t_S2 = tbd[d]['fr'], tbd[d]['S2']
            for g in range(NG):
                gg = ct * 8 + g
                blk = slice(16 * g, 16 * g + 16)
                ew('dve', lambda e, g=g, gg=gg, blk=blk, t_fr=t_fr: e.tensor_scalar(MBm[:, g, blk], Bs1[:, g, :], t_fr[:, gg:gg + 1], None, ALU.mult), [Bs1, t_fr], [MBv[g]])
                ew('dve', lambda e, g=g, gg=gg, blk=blk, t_S2=t_S2: e.scalar_tensor_tensor(MBm[:, g, blk], Bs2[:, g, :], t_S2[:, gg:gg + 1], MBm[:, g, blk], ALU.mult, ALU.add), [Bs2, t_S2, MBv[g]], [MBv[g]])
                ew('dve', lambda e, g=g: e.tensor_copy(ctf[:, 0, g, :], Cs1[:, g, :]), [Cs1], [ctfv[0][g]])
            BsT = [p.view(pb[3 + k // 4], pb[3 + k // 4].t[:, (k % 4) * 128:(k % 4 + 1) * 128]) for k in range(8)]
            KdA = [p.view(pb[5 + k // 4], pb[5 + k // 4].t[:, (k % 4) * 128:(k % 4 + 1) * 128]) for k in range(8)]
            qi = 0
            for k in range(0, 9):
                tk = pw[d][k - 1] if k >= 1 else None
                for g in range(NG):
                    gg = ct * 8 + g
                    blk = slice(16 * g, 16 * g + 16)
                    if k >= 1:
                        cv = ctfv[k][g]
                        ew('dve', lambda e, k=k, g=g, gg=gg, tk=tk: e.tensor_scalar(ctf[:, k, g, :], Cs1[:, g, :], tk['AR'][:, gg:gg + 1], None, ALU.mult), [Cs1, tk['AR']], [cv])
                        ew('dve', lambda e, k=k, g=g, gg=gg, tk=tk: e.scalar_tensor_tensor(ctf[:, k, g, :], Cs2[:, g, :], tk['T5'][:, gg:gg + 1], ctf[:, k, g, :], ALU.mult, ALU.add), [Cs2, tk['T5'], cv], [cv])
                        t_idx = (k - 1) if fwd else (8 - k)
                        ew('act', lambda e, k=k, g=g, t_idx=t_idx, blk=blk: e.copy(CtL[:, t_idx, g, blk], ctf[:, k, g, :]), [cv], [CtLv[t_idx][g]])
                    if k <= 7:
                        if k == 0:
                            Qb_, Qap = ident, ident[:, :]
                        else:
                            Qb_ = Qk[qi % 8]
                            qi += 1
                            if qi % 2 == 0:
                                ew('act', lambda e, Qb_=Qb_, gg=gg, tk=tk: e.activation(Qb_[:, 0:64], Jt[:, 0:64], AF.Copy, scale=tk['T3'][:, gg:gg + 1]), [Jt, tk['T3']], [Qb_])
                                ew('act', lambda e, Qb_=Qb_, gg=gg, tk=tk: e.activation(Qb_[:, 64:128], Jt[:, 64:128], AF.Copy, scale=tk['T4'][:, gg:gg + 1]), [Jt, tk['T4']], [Qb_])
                            else:
                                ew('dve', lambda e, Qb_=Qb_, gg=gg, tk=tk: e.tensor_scalar(Qb_[:, 0:64], Jt[:, 0:64], tk['T3'][:, gg:gg + 1], None, ALU.mult), [Jt, tk['T3']], [Qb_])
                                ew('dve', lambda e, Qb_=Qb_, gg=gg, tk=tk: e.tensor_scalar(Qb_[:, 64:128], Jt[:, 64:128], tk['T4'][:, gg:gg + 1], None, ALU.mult), [Jt, tk['T4']], [Qb_])
                            Qap = Qb_[:, :]
                        mm(p, BsT[k], BsT[k][:, :], MBv[g], MBm[:, g, :], Qb_, Qap, g == 0, g == NG - 1)
                        mm(p, KdA[k], KdA[k][:, blk], MBv[g], MBm[:, g, :], ctfv[k][g], ctf[:, k, g, :], True, True)
            p.sync_all()
            for k in range(8):
                s_idx = (7 - k) if fwd else k
                for g in range(NG):
                    if k < 4:
                        ew('dve', lambda e, k=k, g=g, s_idx=s_idx: e.tensor_scalar(UL[:, s_idx * 8 + g, :], BsT[k][:, :], rmask[:, g:g + 1], None, ALU.mult), [BsT[k], rmask], [ULv[s_idx * 8 + g]])
                    else:
                        ew('act', lambda e, k=k, g=g, s_idx=s_idx: e.activation(UL[:, s_idx * 8 + g, :], BsT[k][:, :], AF.Copy, scale=rmask[:, g:g + 1]), [BsT[k], rmask], [ULv[s_idx * 8 + g]])
                if k < 4:
                    ew('dve', lambda e, k=k: e.tensor_copy(KdL[:, k, :], KdA[k][:, :]), [KdA[k]], [KdL])
                else:
                    ew('act', lambda e, k=k: e.copy(KdL[:, k, :], KdA[k][:, :]), [KdA[k]], [KdL])
            p.sync_all()
            bi = 0
            for g in range(NG):
                for (c0, ncol, tok0) in pieces:
                    ps = pb[bi % 4]
                    bi += 1
                    j0 = tok0 // S5T
                    for s in range(S5T):
                        mm(p, ps, ps[:, :ncol], ULv[s * 8 + g], UL[:, s * 8 + g, :], u_de, u_de[:, s, j0:j0 + ncol], s == 0, s == S5T - 1)
                    ew('dve', lambda e, ps=ps, g=g, c0=c0, ncol=ncol: e.tensor_copy(Xb[:, g, c0:c0 + ncol], ps[:, :ncol]), [ps], [Xbg[g]])
            p.sync_all()
            def build_tile(idx, k, g, gg, par):
                tk = pw[d][KIDX[k]]
                uv_ = ULv[idx]
                if par != 2:
                    ew('dve', lambda e: e.tensor_scalar(UL[:, idx, 0:64], Jt[:, 0:64], tk['T3'][:, gg:gg + 1], None, ALU.mult), [Jt, tk['T3']], [uv_])
                    ew('dve', lambda e: e.tensor_scalar(UL[:, idx, 64:128], Jt[:, 64:128], tk['T4'][:, gg:gg + 1], None, ALU.mult), [Jt, tk['T4']], [uv_])
                else:
                    ew('act', lambda e: e.activation(UL[:, idx, 0:64], Jt[:, 0:64], AF.Copy, scale=tk['T3'][:, gg:gg + 1]), [Jt, tk['T3']], [uv_])
                    ew('act', lambda e: e.activation(UL[:, idx, 64:128], Jt[:, 64:128], AF.Copy, scale=tk['T4'][:, gg:gg + 1]), [Jt, tk['T4']], [uv_])

            cntb = 0
            for m in range(1, 8):
                for g in range(NG):
                    build_tile((m - 1) * 8 + g, 8 * m, g, ct * 8 + g, cntb % 3)
                    cntb += 1
            for r in range(NRB):
                for g in range(NG):
                    build_tile(56 + r * 8 + g, 64 << r, g, ct * 8 + g, cntb % 3)
                    cntb += 1
            p.sync_all()

            def LA(m, g):
                if m == 0:
                    return identb, identb[:, :]
                return ULv[(m - 1) * 8 + g], UL[:, (m - 1) * 8 + g, :]

            xv = [Xb[:, g, :].rearrange("p (c j) -> p c j", j=8) for g in range(NG)]
            hv = [Hin[:, g, :].rearrange("p (c j) -> p c j", j=8) for g in range(NG)]
            bi = 0
            for g in range(NG):
                ps = pb[bi % 4]
                bi += 1
                for j in range(8):
                    lb, lap = LA((7 - j) if fwd else j, g)
                    mm(p, ps, ps[:, :NB], lb, lap, Xbg[g], xv[g][:, :, j], j == 0, j == 7)
                ew('dve', lambda e, ps=ps, g=g: e.tensor_copy(Wf[:, g, :], ps[:, :NB]), [ps], [Wfg[g]])
                ew('act', lambda e, g=g: e.copy(Wb[:, g, 1:1 + NB], Wf[:, g, :]), [Wfg[g]], [Wbg[g]])
            for r in range(NRB):
                sh = 1 << r
                L = NB - sh
                if L <= 0:
                    continue
                if fwd:
                    dst0, src0 = sh, 0
                else:
                    dst0, src0 = 0, sh
                for g in range(NG):
                    ps = pb[bi % 4]
                    bi += 1
                    mm(p, ps, ps[:, :L], ULv[56 + r * 8 + g], UL[:, 56 + r * 8 + g, :], Wbg[g], Wb[:, g, 1 + src0:1 + src0 + L], True, True)
                    ew('dve', lambda e, ps=ps, g=g, L=L, dst0=dst0: e.tensor_tensor(Wf[:, g, dst0:dst0 + L], Wf[:, g, dst0:dst0 + L], ps[:, :L], ALU.add), [Wfg[g], ps], [Wfg[g]])
                    ew('act', lambda e, g=g, L=L, dst0=dst0: e.copy(Wb[:, g, 1 + dst0:1 + dst0 + L], Wf[:, g, dst0:dst0 + L]), [Wfg[g]], [Wbg[g]])
            ge0 = 0 if fwd else 2
            for g in range(NG):
                for j in range(8):
                    mG = j if fwd else (7 - j)
                    others = list(range(0, j)) if fwd else list(range(j + 1, 8))
                    if mG == 0 and not others:
                        ew('act', lambda e, g=g, j=j, ge0=ge0, hvg=hv[g]: e.copy(hvg[:, :, j], Wb[:, g, ge0:ge0 + NB]), [Wbg[g]], [Hing[g]])
                        continue
                    ps = pb[bi % 4]
                    bi += 1
                    terms = [(mG, Wbg[g], Wb[:, g, ge0:ge0 + NB])]
                    for i in others:
                        mi = (j - 1 - i) if fwd else (i - j - 1)
                        terms.append((mi, Xbg[g], xv[g][:, :, i]))
                    for ti, (m_, rb, rap) in enumerate(terms):
                        lb, lap = LA(m_, g)
                        mm(p, ps, ps[:, :NB], lb, lap, rb, rap, ti == 0, ti == len(terms) - 1)
                    if (g + j) % 2 == 0:
                        ew('dve', lambda e, ps=ps, g=g, j=j, hvg=hv[g]: e.tensor_copy(hvg[:, :, j], ps[:, :NB]), [ps], [Hing[g]])
                    else:
                        ew('act', lambda e, ps=ps, g=g, j=j, hvg=hv[g]: e.copy(hvg[:, :, j], ps[:, :NB]), [ps], [Hing[g]])
            p.sync_all()
            bi = 0
            for (c0, ncol, tok0) in pieces:
                if tok0 < NCTX:
                    continue
                j0 = tok0 // S5T
                yv = Y[:, tok0 - NCTX:tok0 - NCTX + S5T * ncol].rearrange("p (c s) -> p c s", s=S5T)
                for t in range(S5T):
                    ps = pb[bi % 4]
                    bi += 1
                    ss = list(range(0, t + 1)) if fwd else list(range(t, S5T))
                    nmm = NG + len(ss)
                    k2 = 0
                    for g in range(NG):
                        mm(p, ps, ps[:, :ncol], CtLv[t][g], CtL[:, t, g, :], Hing[g], Hin[:, g, c0:c0 + ncol], k2 == 0, k2 == nmm - 1)
                        k2 += 1
                    for s in ss:
                        mm(p, ps, ps[:, :ncol], KdL, KdL[:, abs(t - s), :], u_de, u_de[:, s, j0:j0 + ncol], k2 == 0, k2 == nmm - 1)
                        k2 += 1
                    ew('dve', lambda e, ps=ps, t=t, yv=yv, ncol=ncol: e.tensor_tensor(yv[:, :, t], yv[:, :, t], ps[:, :ncol], ALU.add), [Y, ps], [Y])
            p.sync_all()
        p.op('act', lambda e: e.activation(Xbflat[:, 0:SEQ], Y[:], AF.Gelu), reads=[Y], writes=[Xb])
        p.dma('pool', lambda e, ct=ct: e.dma_start(out=gT_d[ct * 128:(ct + 1) * 128, :], in_=Xbflat[:, 0:SEQ]), reads=[Xb])
    p.close_scope()


def glu_phase(p, nc, C, gT_d, wg_s, h_in, h_out, co, chunks, out_off):
    p.open_scope()
    NT = 512
    hTf = [p.sb("hT", [128, KT, NT], F32) for _ in range(2)]
    hTs = [p.views(b, KT) for b in hTf]
    gTf = [p.sb("gT", [128, KT, NT], BF16) for _ in range(2)]
    sq = p.views(p.sb("sq", [128, KT, NT], BF16), KT)
    rstd2 = p.sb("rstd2", [128, NT], F32)
    tmps = [p.sb("tmp", [128, NT], F32) for _ in range(3)]
    sgs = [p.sb("sg", [128, NT], F32) for _ in range(2)]
    ysb = p.views(p.sb("ysb", [128, KT, NT], F32), KT)
    wg = p.sb("wg", [128, KT, KT, 256], BF16)
    p.dma('sp', lambda e: e.dma_start(out=wg[:].rearrange("p i a b -> p i (a b)"), in_=wg_s.rearrange("i p x -> p i x")), writes=[wg])
    pstat = p.ps("pstat")
    pa = [p.ps("pa") for _ in range(2)]
    pg = [p.ps("pg") for _ in range(2)]
    hin_v = h_in.rearrange("(kt p) t -> p kt t", p=128)
    hout_v = h_out.rearrange("(kt p) t -> p kt t", p=128)
    g_v = gT_d.rearrange("(kt p) t -> p kt t", p=128)
    for ci, (t0, n, kind) in enumerate(chunks):
        hT = hTs[ci % 2]
        hf = hTf[ci % 2]
        gT = gTf[ci % 2]
        p.dma('sp', lambda e, hf=hf, t0=t0, n=n: e.dma_start(out=hf[:, :, :n], in_=hin_v[:, :, t0:t0 + n]), writes=hT)
        p.dma('sp', lambda e, gT=gT, t0=t0, n=n: e.dma_start(out=gT[:, :, :n], in_=g_v[:, :, t0 - NCTX:t0 - NCTX + n]), writes=[gT])
        for i in range(KT):
            a_, g_ = pa[i % 2], pg[i % 2]
            for kt in range(KT):
                mm(p, a_, a_[:, :n], wg, wg[:, i, kt, 0:128], gT, gT[:, kt, :n], kt == 0, kt == KT - 1)
            for kt in range(KT):
                mm(p, g_, g_[:, :n], wg, wg[:, i, kt, 128:256], gT, gT[:, kt, :n], kt == 0, kt == KT - 1)
            sg = sgs[i % 2]
            p.op('act', lambda e, g_=g_, sg=sg: e.activation(sg[:, :n], g_[:, :n], AF.Sigmoid), reads=[g_], writes=[sg])
            p.op('dve', lambda e, a_=a_, sg=sg, i=i: e.tensor_tensor(ysb[i][:, :n], sg[:, :n], a_[:, :n], ALU.mult), reads=[sg, a_], writes=[ysb[i]])
            p.op('act', lambda e, i=i: e.activation(sq[i][:, :n], ysb[i][:, :n], AF.Square), reads=[ysb[i]], writes=[sq[i]])
        post_residual(p, C, hT, ysb, n, co, 1, kind, sq, pstat, rstd2, tmps)
        p.dma('pool', lambda e, hf=hf, t0=t0, n=n: e.dma_start(out=hout_v[:, :, t0 - out_off:t0 - out_off + n], in_=hf[:, :, :n]), reads=hT)
    p.close_scope()


import numpy as np
GRID_W = 64
ROPE_BASE = 10000.0

def rope_tables(seq, nctx):
    t = np.arange(seq)
    rows = (t // GRID_W).astype(np.float32)
    cols = (t % GRID_W).astype(np.float32)
    def tab(d_rot):
        d_axis = d_rot // 2
        inv = (ROPE_BASE ** (-np.arange(0, d_axis, 2, dtype=np.float32) / d_axis)).astype(np.float32)
        ang = np.concatenate([rows[:, None] * inv, cols[:, None] * inv], axis=-1).astype(np.float32)
        return np.cos(ang).astype(np.float32), np.sin(ang).astype(np.float32)
    T = nctx + seq
    ca, sa = tab(32)
    cb, sb_ = tab(64)
    A = np.zeros((32, 2, T), np.float32); A[:, 0, :] = 1.0
    A[:, 0, nctx:] = np.concatenate([ca.T, ca.T], 0)
    A[:, 1, nctx:] = np.concatenate([sa.T, sa.T], 0)
    B = np.zeros((128, 2, T), np.float32); B[:, 0, :] = 1.0
    B[:, 0, nctx:] = np.concatenate([cb.T, cb.T, cb.T, cb.T], 0)
    B[:, 1, nctx:] = np.concatenate([sb_.T, sb_.T, sb_.T, sb_.T], 0)
    return A, B

def const_inputs():
    ident = np.eye(128, dtype=np.float32)
    shiftI = np.zeros((128, 64), np.float32); shiftI[64 + np.arange(64), np.arange(64)] = 1.0
    j = np.arange(128)[:, None]; q = np.arange(128)[None, :]
    mprev = (j >= q).astype(np.float32)
    mnext = (j <= q).astype(np.float32)
    return dict(ident=ident, shiftI=shiftI, mprev=mprev, mnext=mnext)

def s5_layouts(inp):
    lre = inp['s5_lambda_re'][0]; lim = inp['s5_lambda_im'][0]; lst = inp['s5_log_step'][0]
    lam = np.zeros((2, 3, 128, 64), np.float32)
    for d in range(2):
        lam[d, 0] = np.concatenate([lre[d].T, lre[d].T], 0)
        lam[d, 1] = np.concatenate([lim[d].T, lim[d].T], 0)
        lam[d, 2] = np.broadcast_to(lst[d][None, :], (128, 64))
    br = inp['s5_b_re'][0].transpose(0, 2, 1, 3)
    bi = inp['s5_b_im'][0].transpose(0, 2, 1, 3)
    B12 = np.stack([np.concatenate([br, bi], 1), np.concatenate([bi, br], 1)], 1)
    cr = inp['s5_c_re'][0].transpose(0, 3, 1, 2)
    ci = inp['s5_c_im'][0].transpose(0, 3, 1, 2)
    C1 = np.stack([np.concatenate([cr, ci], 1), np.concatenate([ci, cr], 1)], 1)
    r = np.arange(128)
    J = (r[:, None] % 64 == r[None, :] % 64).astype(np.float32)
    rmask = (r[:, None] // 16 == np.arange(8)[None, :]).astype(np.float32)
    return dict(s5_lam=np.ascontiguousarray(lam), s5_B12=np.ascontiguousarray(B12), s5_C1=np.ascontiguousarray(C1), J128=J, rmask=rmask)


SEQ_FULL = 8192
N_CORES = 8


def build_program(SEQ=SEQ_FULL):
    T = NCTX + SEQ
    nc = bass.Bass("TRN2", target_bir_lowering=False)

    def inp(name, shape, dt=F32):
        return nc.dram_tensor(name, list(shape), dt, kind="ExternalInput").ap()

    h0 = inp("h0", [D, T]); cvec = inp("cvec", [16, 128])
    ident = inp("ident", [128, 128]); shiftI = inp("shiftI", [128, 64]); mprev = inp("mprev", [128, 128]); mnext = inp("mnext", [128, 128])
    modw = inp("mod_w", [2, D, 9 * D]); modb = inp("mod_b", [2, 72, 128]); npre = inp("norm_pre", [2, 24, 128]); npost = inp("norm_post", [2, 24, 128])
    w13 = inp("ffn_w13", [2, 2, D, 2 * DFF]); w2 = inp("ffn_w2", [2, 2, DFF, D])
    w_in = inp("attn_w_in", [1, D, 1184]); qn = inp("mla_q_norm", [2, 128]); wuq = inp("mla_w_uq", [1, 256, 768])
    kvn = inp("mla_kv_norm", [1, 128]); wukv = inp("mla_w_ukv", [1, 128, 1024]); sink = inp("sink128", [128, 8]); wout = inp("attn_w_out", [1, D, D])
    ropeA = inp("ropeA", [32, 2, T]); ropeB = inp("ropeB", [128, 2, T])
    w5in = inp("s5_w_in", [1, D, D]); dsk = inp("s5_d", [8, 128]); wglu = inp("s5_w_glu", [1, D, 2 * D])
    lam = inp("s5_lam", [2, 3, 128, 64]); B12 = inp("s5_B12", [2, 2, 128, 64, 16]); C1 = inp("s5_C1", [2, 2, 128, 64, 16])
    J = inp("J128", [128, 128]); rmask = inp("rmask", [128, 8])
    out = nc.dram_tensor("outT", [D, SEQ], F32, kind="ExternalOutput").ap()

    hA = nc.dram_tensor("hA", [D, T], F32).ap()
    hB = nc.dram_tensor("hB", [D, T], F32).ap()
    p = Prog(nc)
    C = build_consts(p, nc, ident, shiftI, mprev, mnext)
    CO = mod_phase(p, nc, C, cvec, modw, modb, npre, npost, 2)
    chunks = [(0, NCTX, 1)] + [(NCTX + 512 * i, 512, 0) for i in range(SEQ // 512)]
    lat_chunks = chunks[1:]
    g13 = [[(j * 128, 128), (DFF + j * 128, 128)] for j in range(JT)]
    g2 = [[(i * 128, 128)] for i in range(KT)]

    gattn_k = [[(128 * h, 64) for h in range(8)]]
    gattn_v = [[(128 * h + 64, 64) for h in range(8)]]
    gglu = [[(i * 128, 128), (D + i * 128, 128)] for i in range(KT)]
    W = {}

    def mk(names):
        def fn(stg):
            gens = []
            for (name, src, K, groups) in names:
                scr_, g_ = prep_weight_gen(p, nc, name, src, K, groups, stg)
                W[name] = scr_
                gens.append(g_)
            return gens
        return fn

    def ffn_w(l, f):
        return [("w13s_%d%d" % (l, f), w13[l, f], D, g13), ("w2s_%d%d" % (l, f), w2[l, f], DFF, g2)]

    attn_w = [("win_s", w_in[0], D, attn_weight_groups()), ("wuq_s", wuq[0], 256, wuq_groups()),
              ("wk_s", wukv[0], 128, gattn_k), ("wv_s", wukv[0], 128, gattn_v), ("wout_s", wout[0], D, g2)]
    s5_w = [("w5_s", w5in[0], D, g2), ("wg_s", wglu[0], D, gglu)]

    for (name, src, K, groups) in ffn_w(0, 0):
        W[name] = prep_weight(p, nc, name, src, K, groups)
    ffn_phase(p, nc, C, h0, hA, W["w13s_00"], W["w2s_00"], CO[0], 0, chunks, 0, mk(attn_w + ffn_w(0, 1)))
    scr = dict(QT=nc.dram_tensor("QT", [8, 96, T], BF16).ap(), KT=nc.dram_tensor("KTs", [8, 96, T], BF16).ap(),
               V=nc.dram_tensor("Vs", [T // 128, 128, 8, 128], BF16).ap(), QS=nc.dram_tensor("QS", [4, 128, T], BF16).ap(),
               KS=nc.dram_tensor("KS", [2, 128, T], BF16).ap(), VS=nc.dram_tensor("VS", [T // 128, 128, 2, 128], BF16).ap(),
               OT=nc.dram_tensor("OT", [D, T], BF16).ap())
    attn_proj_phase(p, nc, C, hA, CO[0], chunks, T, W["win_s"], W["wuq_s"], W["wk_s"], W["wv_s"], qn, kvn, ropeA, ropeB, scr)
    mla_phase(p, nc, C, chunks, T, scr)
    swa_phase(p, nc, C, chunks, T, scr, sink)
    mixout_phase(p, nc, C, scr['OT'], W["wout_s"], hA, hB, CO[0], 1, chunks, 8)
    ffn_phase(p, nc, C, hB, hA, W["w13s_01"], W["w2s_01"], CO[0], 2, chunks, 0, mk(ffn_w(1, 0) + s5_w))
    ffn_phase(p, nc, C, hA, hB, W["w13s_10"], W["w2s_10"], CO[1], 0, chunks, 0, mk(ffn_w(1, 1)))
    uT = nc.dram_tensor("uT", [D, T], BF16).ap()
    y0 = nc.dram_tensor("y0T", [D, SEQ], F32).ap()
    gT = nc.dram_tensor("gT", [D, SEQ], BF16).ap()
    s5_in_phase(p, nc, C, hB, CO[1], chunks, W["w5_s"], dsk, uT, y0)
    s5_scan_phase(p, nc, C, T, SEQ, uT, y0, gT, lam, B12, C1, J, rmask)
    glu_phase(p, nc, C, gT, W["wg_s"], hB, hA, CO[1], lat_chunks, 0)
    ffn_phase(p, nc, C, hA, out, W["w13s_11"], W["w2s_11"], CO[1], 2, lat_chunks, NCTX)
    p.finish()
    return nc


def make_in_maps(inputs, SEQ=SEQ_FULL, n_cores=N_CORES):
    inp = {k: np.asarray(v) for k, v in inputs.items()}
    rA, rB = rope_tables(SEQ, NCTX)
    shared = {"mod_w": inp['mod_w'], "mod_b": inp['mod_b'].reshape(2, 72, 128),
              "norm_pre": inp['norm_pre'].reshape(2, 24, 128), "norm_post": inp['norm_post'].reshape(2, 24, 128),
              "ffn_w13": inp['ffn_w13'], "ffn_w2": inp['ffn_w2'],
              "attn_w_in": inp['attn_w_in'], "mla_q_norm": inp['mla_q_norm'].reshape(2, 128), "mla_w_uq": inp['mla_w_uq'],
              "mla_kv_norm": inp['mla_kv_norm'].reshape(1, 128), "mla_w_ukv": inp['mla_w_ukv'],
              "sink128": np.ascontiguousarray(np.tile(inp['swa_sink'][0][None], (128, 1))), "attn_w_out": inp['attn_w_out'],
              "ropeA": rA, "ropeB": rB,
              "s5_w_in": inp['s5_w_in'], "s5_d": inp['s5_d'].reshape(8, 128), "s5_w_glu": inp['s5_w_glu']}
    shared.update(s5_layouts(inp))
    shared.update(const_inputs())
    shared = {k: np.ascontiguousarray(v, dtype=np.float32) for k, v in shared.items()}
    maps = []
    for b in range(n_cores):
        m = dict(shared)
        m["h0"] = np.ascontiguousarray(np.concatenate([inp['ctx'][b].T, inp['x'][b, :SEQ].T], axis=1), dtype=np.float32)
        m["cvec"] = np.ascontiguousarray(np.concatenate([inp['c'][b].reshape(8, 128), inp['c_ctx'].reshape(8, 128)], 0), dtype=np.float32)
        maps.append(m)
    return maps


def kernel(**inputs):
    nc = build_program(SEQ_FULL)
    maps = make_in_maps(inputs, SEQ_FULL, N_CORES)
    res = run_bass_kernel_spmd(nc, maps, core_ids=list(range(N_CORES)))
    out = np.stack([np.ascontiguousarray(r["outT"].T) for r in res.results], axis=0)
    return out.astype(np.float32)
```

```python
import numpy as np
import concourse.bass as bass
import concourse.mybir as mybir
from concourse.bass_utils import run_bass_kernel_spmd
from contextlib import ExitStack

F32 = mybir.dt.float32
BF16 = mybir.dt.bfloat16
AF = mybir.ActivationFunctionType
ALU = mybir.AluOpType
AX = mybir.AxisListType
ENGS = ['pe', 'act', 'dve', 'pool', 'sp']

D = 1024
KT = 8
DFF = 2816
JT = 22
NCTX = 256
EPS = 1e-6


class Buf:
    _serial = [0]

    def __init__(self, name, t=None):
        Buf._serial[0] += 1
        self.uid = Buf._serial[0]
        self.name = name
        self.t = t
        self.w = None
        self.r = []
        self.dsem = None
        self.dcnt = 0

    def __getitem__(self, idx):
        return self.t[idx]


class DSem:
    _serial = [0]

    def __init__(self):
        DSem._serial[0] += 1
        self.uid = -DSem._serial[0]
        self.dsem = None
        self.dcnt = 0


class Prog:
    def __init__(self, nc):
        self.nc = nc
        self.es = ExitStack()
        self.sem = {e: self.es.enter_context(nc.semaphore("s_" + e)) for e in ENGS}
        self.cnt = {e: 0 for e in ENGS}
        self.ops = {e: [] for e in ENGS}
        self.waited = {e: {} for e in ENGS}
        self.dbufs = []
        self.scopes = []
        self.nm = 0
        self.dsem_pool = {'hw': [], 'sw': []}

    def _stack(self):
        return self.scopes[-1] if self.scopes else self.es

    def sb(self, name, shape, dt):
        self.nm += 1
        t = self._stack().enter_context(self.nc.sbuf_tensor("%s_%d" % (name, self.nm), list(shape), dt))
        return Buf(name, t)

    def ps(self, name, shape=(128, 512), dt=F32):
        self.nm += 1
        t = self._stack().enter_context(self.nc.psum_tensor("%s_%d" % (name, self.nm), list(shape), dt))
        return Buf(name, t)

    def view(self, buf, t):
        return Buf(buf.name + "_v", t)

    def views(self, buf, k):
        return [Buf("%s_%d" % (buf.name, i), buf.t[:, i, :]) for i in range(k)]

    def open_scope(self):
        self.scopes.append(ExitStack())

    def close_scope(self):
        self.flush()
        depth = len(self.scopes)
        keep = []
        for ds in self.dbufs:
            if ds.scope_depth >= depth:
                self.dsem_pool[ds.qk].append((ds.dsem, ds.dcnt))
                ds.owner.dsems.pop(ds.qk, None)
            else:
                keep.append(ds)
        self.dbufs = keep
        self.scopes.pop().close()

    def _need(self, eng, dep, waits):
        if dep is None:
            return
        kind, key, val = dep
        if kind == 'e' and key == 'pe' and eng == 'pe':
            return
        k = (kind, key if kind == 'e' else key.uid)
        if self.waited[eng].get(k, 0) >= val:
            return
        self.waited[eng][k] = val
        sem = self.sem[key] if kind == 'e' else key.dsem
        waits.append((sem, val))

    def _deps(self, eng, reads, writes):
        waits = []
        for b in reads:
            self._need(eng, b.w, waits)
        for b in writes:
            self._need(eng, b.w, waits)
            for r in b.r:
                self._need(eng, r, waits)
        return waits

    def op(self, eng, fn, reads=(), writes=(), inc=True):
        waits = self._deps(eng, reads, writes)
        if inc:
            self.cnt[eng] += 1
            tick = ('e', eng, self.cnt[eng])
        else:
            tick = ('e', eng, self.cnt[eng] + 1)
        self.ops[eng].append((waits, fn, (self.sem[eng], 1) if inc else None))
        for b in reads:
            b.r.append(tick)
        for b in writes:
            b.w = tick
            b.r = []
        return tick

    def dma(self, eng, fn, reads=(), writes=(), owner=None):
        waits = self._deps(eng, reads, writes)
        if owner is None:
            owner = (list(writes) + list(reads))[0]
        qk = 'sw' if eng == 'pool' else 'hw'
        if not hasattr(owner, 'dsems'):
            owner.dsems = {}
        ds = owner.dsems.get(qk)
        if ds is None:
            ds = DSem()
            if self.dsem_pool[qk]:
                ds.dsem, ds.dcnt = self.dsem_pool[qk].pop()
            else:
                self.nsem = getattr(self, 'nsem', 0) + 1
                ds.dsem = self.es.enter_context(self.nc.semaphore("d%d" % self.nsem))
            ds.qk = qk
            ds.owner = owner
            ds.scope_depth = len(self.scopes)
            owner.dsems[qk] = ds
            self.dbufs.append(ds)
        ds.dcnt += 16
        tick = ('d', ds, ds.dcnt)
        self.ops[eng].append((waits, fn, (ds.dsem, 16)))
        for b in reads:
            b.r.append(tick)
        for b in writes:
            b.w = tick
            b.r = []
        return tick

    def barrier(self):
        for e in ENGS:
            waits = []
            for e2 in ENGS:
                if e2 != e and self.cnt[e2] > 0:
                    self._need(e, ('e', e2, self.cnt[e2]), waits)
            for b in self.dbufs:
                self._need(e, ('d', b, b.dcnt), waits)
            if waits:
                self.ops[e].append((waits, None, None))

    def sync_all(self):
        self.barrier()

    def flush(self):
        nc = self.nc
        self.barrier()
        ops = self.ops
        self.ops = {e: [] for e in ENGS}
        self.simulate(ops)

        def run(lst, e):
            for waits, fn, inc in lst:
                for sem, val in waits:
                    e.wait_ge(sem, val)
                if fn is not None:
                    ins = fn(e)
                    if inc is not None:
                        ins.then_inc(inc[0], inc[1])

        with nc.Block() as block:
            @block.tensor
            def _(e):
                run(ops['pe'], e)

            @block.scalar
            def _(e):
                run(ops['act'], e)

            @block.vector
            def _(e):
                run(ops['dve'], e)

            @block.gpsimd
            def _(e):
                run(ops['pool'], e)

            @block.sync
            def _(e):
                run(ops['sp'], e)

    def simulate(self, ops):
        if not hasattr(self, 'simv'):
            self.simv = {}
        ptr = {e: 0 for e in ENGS}
        while True:
            prog = False
            for e in ENGS:
                while ptr[e] < len(ops[e]):
                    waits, fn, inc = ops[e][ptr[e]]
                    if all(self.simv.get(id(s), 0) >= v for s, v in waits):
                        if inc is not None:
                            self.simv[id(inc[0])] = self.simv.get(id(inc[0]), 0) + inc[1]
                        ptr[e] += 1
                        prog = True
                    else:
                        break
            if all(ptr[e] == len(ops[e]) for e in ENGS):
                return
            if not prog:
                for e in ENGS:
                    if ptr[e] < len(ops[e]):
                        waits, fn, inc = ops[e][ptr[e]]
                        print("DEADLOCK", e, ptr[e], [(s, v, self.simv.get(id(s), 0)) for s, v in waits])
                raise RuntimeError("deadlock in sync plan")

    def finish(self):
        self.flush()
        self.es.close()


def mm(p, out, out_ap, lhs, lhs_ap, rhs, rhs_ap, start, stop):
    p.op('pe', lambda e: e.matmul(out_ap, lhs_ap, rhs_ap, start=start, stop=stop),
         reads=[lhs, rhs], writes=[out], inc=stop)


_rr = [0]


def cast_any(p, out, out_ap, in_, in_ap):
    i = _rr[0] % 3
    _rr[0] += 1
    if i == 0:
        p.op('dve', lambda e: e.tensor_copy(out_ap, in_ap), reads=[in_], writes=[out])
    elif i == 1:
        p.op('pool', lambda e: e.tensor_copy(out_ap, in_ap), reads=[in_], writes=[out])
    else:
        p.op('act', lambda e: e.copy(out_ap, in_ap), reads=[in_], writes=[out])


PREP_MAX = 2816


def prep_weight_gen(p, nc, name, W, K, groups, stg):
    ktin = K // 128
    wtot = sum(g[1] for g in groups[0])
    G = len(groups)
    scr = nc.dram_tensor(name, [G, 128, ktin * wtot], BF16).ap()
    Wv = W.rearrange("(kt p) n -> p kt n", p=128)
    sz = ktin * wtot
    assert sz <= PREP_MAX

    def gen():
        for g, grp in enumerate(groups):
            i = stg['cnt'][0] % 2
            stg['cnt'][0] += 1
            fb, bb = stg['f'][i], stg['b'][i]
            ft = fb.t[:, 0:sz].rearrange("p (a b) -> p a b", a=ktin)
            bt = bb.t[:, 0:sz]
            off = 0
            negs = []
            for it in grp:
                c0, w = it[0], it[1]
                sgn = it[2] if len(it) > 2 else 1
                p.dma('sp', lambda e, ft=ft, off=off, c0=c0, w=w: e.dma_start(out=ft[:, :, off:off + w], in_=Wv[:, :, c0:c0 + w]),
                      writes=[fb])
                if sgn < 0:
                    negs.append((off, w))
                off += w
            for (o2, w2) in negs:
                p.op('dve', lambda e, ft=ft, o2=o2, w2=w2: e.tensor_scalar(ft[:, :, o2:o2 + w2], ft[:, :, o2:o2 + w2], -1.0, None, ALU.mult),
                     reads=[fb], writes=[fb])
            cast_any(p, bb, bt, fb, fb.t[:, 0:sz])
            p.dma('pool', lambda e, bt=bt, g=g: e.dma_start(out=scr[g], in_=bt), reads=[bb])
            yield

    return scr, gen()


def prep_staging(p):
    return dict(f=[p.sb("pwf", [128, PREP_MAX], F32) for _ in range(2)],
                b=[p.sb("pwb", [128, PREP_MAX], BF16) for _ in range(2)], cnt=[0])


def prep_weight(p, nc, name, W, K, groups):
    p.open_scope()
    stg = prep_staging(p)
    scr, gen = prep_weight_gen(p, nc, name, W, K, groups, stg)
    for _ in gen:
        pass
    p.close_scope()
    return scr


def advance(bg):
    while bg:
        try:
            next(bg[0])
            return
        except StopIteration:
            bg.pop(0)


def load_cols(p, dst, dst_ap, src_ap, R, ident, ps, stage):
    p.dma('sp', lambda e: e.dma_start(out=stage[0:R, :], in_=src_ap), writes=[stage])
    mm(p, ps, ps[:, 0:R], stage, stage[0:R, :], ident, ident[0:R, 0:R], True, True)
    p.op('dve', lambda e: e.tensor_copy(dst_ap, ps[:, 0:R]), reads=[ps], writes=[dst])


def build_consts(p, nc, ident_d, shift_d=None, mprev_d=None, mnext_d=None):
    c = {}
    if shift_d is not None:
        c['shiftI'] = p.sb("shiftI", [128, 64], F32)
        p.dma('sp', lambda e: e.dma_start(out=c['shiftI'][:], in_=shift_d), writes=[c['shiftI']])
        c['mprev'] = p.sb("mprev", [128, 128], F32)
        p.dma('sp', lambda e: e.dma_start(out=c['mprev'][:], in_=mprev_d), writes=[c['mprev']])
        c['mnext'] = p.sb("mnext", [128, 128], F32)
        p.dma('sp', lambda e: e.dma_start(out=c['mnext'][:], in_=mnext_d), writes=[c['mnext']])
    c['ident'] = p.sb("ident", [128, 128], F32)
    p.dma('sp', lambda e: e.dma_start(out=c['ident'][:], in_=ident_d), writes=[c['ident']])
    c['ones_bf'] = p.sb("ones_bf", [128, 128], BF16)
    p.op('dve', lambda e: e.memset(c['ones_bf'][:], 1.0), writes=[c['ones_bf']])
    c['eps'] = p.sb("epsc", [128, 1], F32)
    p.op('dve', lambda e: e.memset(c['eps'][:], EPS), writes=[c['eps']])
    return c


def mod_phase(p, nc, C, cvec_d, modw_d, modb_d, npre_d, npost_d, L):
    CO = [p.sb("CO%d" % l, [128, 3, 3, 2, 8], F32) for l in range(L)]
    p.open_scope()
    ident = C['ident']
    stage = p.sb("stage", [128, 128], F32)
    pst = p.ps("pst")
    cT = p.sb("cT", [128, 16], F32)
    load_cols(p, cT, cT[:], cvec_d, 16, ident, pst, stage)
    sc = p.sb("sc", [128, 8, 2], F32)
    for kind in range(2):
        p.op('act', lambda e, kind=kind: e.activation(sc[:, :, kind], cT[:, kind * 8:(kind + 1) * 8], AF.Silu),
             reads=[cT], writes=[sc])
    wbuf = [p.sb("mwb", [128, 8, 1024], F32) for _ in range(2)]
    psM = p.ps("psM")
    M = p.sb("M", [128, 72, 2], F32)
    mb = p.sb("mb", [128, 72], F32)
    gpre = p.sb("gpre", [128, 24], F32)
    gpost = p.sb("gpost", [128, 24], F32)
    t8 = p.sb("t8", [128, 8], F32)
    for l in range(L):
        mwv = modw_d[l].rearrange("(kt p) n -> p kt n", p=128)
        for j in range(9):
            wb = wbuf[(l * 9 + j) % 2]
            p.dma('sp', lambda e, wb=wb, j=j, mwv=mwv: e.dma_start(out=wb[:], in_=mwv[:, :, j * 1024:(j + 1) * 1024]), writes=[wb])
            for q in range(8):
                col = (j * 8 + q) * 2
                for kt in range(8):
                    mm(p, psM, psM[:, col:col + 2], wb, wb[:, kt, q * 128:(q + 1) * 128], sc, sc[:, kt, :], kt == 0, kt == 7)
        load_cols(p, mb, mb[:], modb_d[l], 72, ident, pst, stage)
        load_cols(p, gpre, gpre[:], npre_d[l], 24, ident, pst, stage)
        load_cols(p, gpost, gpost[:], npost_d[l], 24, ident, pst, stage)
        for kind in range(2):
            p.op('dve', lambda e, kind=kind: e.tensor_tensor(M[:, :, kind], psM[:, 0:144].rearrange("p (q k) -> p q k", k=2)[:, :, kind], mb[:], ALU.add),
                 reads=[psM, mb], writes=[M])
        co = CO[l]
        for s in range(3):
            coef = 1.0 if s == 1 else 0.5
            for kind in range(2):
                p.op('dve', lambda e, s=s, kind=kind: e.tensor_scalar(t8[:], M[:, (3 * s + 1) * 8:(3 * s + 2) * 8, kind], 1.0, None, ALU.add),
                     reads=[M], writes=[t8])
                p.op('dve', lambda e, s=s, kind=kind, co=co: e.tensor_tensor(co[:, s, 0, kind, :], t8[:], gpre[:, s * 8:(s + 1) * 8], ALU.mult),
                     reads=[t8, gpre], writes=[co])
                p.op('dve', lambda e, s=s, kind=kind, co=co: e.tensor_copy(co[:, s, 1, kind, :], M[:, (3 * s) * 8:(3 * s + 1) * 8, kind]),
                     reads=[M], writes=[co])
                p.op('dve', lambda e, s=s, kind=kind, coef=coef: e.tensor_scalar(t8[:], M[:, (3 * s + 2) * 8:(3 * s + 3) * 8, kind], coef, None, ALU.mult),
                     reads=[M], writes=[t8])
                p.op('dve', lambda e, s=s, kind=kind, co=co: e.tensor_tensor(co[:, s, 2, kind, :], t8[:], gpost[:, s * 8:(s + 1) * 8], ALU.mult),
                     reads=[t8, gpost], writes=[co])
    p.close_scope()
    return CO


def norm_stats(p, C, sq, nk, n, pstat, rstd, inv_dim):
    for kt in range(nk):
        mm(p, pstat, pstat[:, :n], C['ones_bf'], C['ones_bf'][:], sq[kt], sq[kt][:, :n], kt == 0, kt == nk - 1)
    p.op('act', lambda e: e.activation(rstd[:, :n], pstat[:, :n], AF.Sqrt, bias=C['eps'][:, 0:1], scale=inv_dim),
         reads=[pstat, C['eps']], writes=[rstd])
    p.op('dve', lambda e: e.reciprocal(rstd[:, :n], rstd[:, :n]), reads=[rstd], writes=[rstd])


def modulate(p, C, hT, n, co, s, kind, sq, pstat, rstd, tmps, aT):
    for kt in range(KT):
        p.op('act', lambda e, kt=kt: e.activation(sq[kt][:, :n], hT[kt][:, :n], AF.Square), reads=[hT[kt]], writes=[sq[kt]])
    norm_stats(p, C, sq, KT, n, pstat, rstd, 1.0 / D)
    for kt in range(KT):
        tmp = tmps[kt % len(tmps)]
        p.op('dve', lambda e, kt=kt, tmp=tmp: e.tensor_tensor(tmp[:, :n], hT[kt][:, :n], rstd[:, :n], ALU.mult),
             reads=[hT[kt], rstd], writes=[tmp])
        p.op('pool', lambda e, kt=kt, tmp=tmp: e.tensor_scalar(aT[kt][:, :n], tmp[:, :n], co[:, s, 0, kind, kt:kt + 1], co[:, s, 1, kind, kt:kt + 1], ALU.mult, ALU.add),
             reads=[tmp, co], writes=[aT[kt]])


def post_residual(p, C, hT, ysb, n, co, s, kind, sq, pstat, rstd, tmps):
    norm_stats(p, C, sq, KT, n, pstat, rstd, 1.0 / D)
    for kt in range(KT):
        tmp = tmps[kt % len(tmps)]
        p.op('pool', lambda e, kt=kt, tmp=tmp: e.tensor_tensor(tmp[:, :n], ysb[kt][:, :n], rstd[:, :n], ALU.mult),
             reads=[ysb[kt], rstd], writes=[tmp])
        p.op('dve', lambda e, kt=kt, tmp=tmp: e.scalar_tensor_tensor(hT[kt][:, :n], tmp[:, :n], co[:, s, 2, kind, kt:kt + 1], hT[kt][:, :n], ALU.mult, ALU.add),
             reads=[tmp, co, hT[kt]], writes=[hT[kt]])


def ffn_phase(p, nc, C, h_in, h_out, w13s, w2s, co, s, chunks, out_off=0, bg_fn=None):
    p.open_scope()
    NT = 512
    hTf = [p.sb("hT", [128, KT, NT], F32) for _ in range(2)]
    hTs = [p.views(b, KT) for b in hTf]
    aTs = [p.views(p.sb("aT", [128, KT, NT], BF16), KT) for _ in range(2)]
    sq = p.views(p.sb("sq", [128, KT, NT], BF16), KT)
    rstd = p.sb("rstd", [128, NT], F32)
    rstd2 = p.sb("rstd2", [128, NT], F32)
    tmps = [p.sb("tmp", [128, NT], F32) for _ in range(3)]
    sgs = [p.sb("sg", [128, NT], F32) for _ in range(2)]
    HT = p.sb("HT", [128, JT, NT], BF16)
    HTj = [p.view(HT, HT.t[:, j, :]) for j in range(JT)]
    ysb = p.views(p.sb("ysb", [128, KT, NT], F32), KT)
    w13b = [p.sb("w13b", [128, KT, 256], BF16) for _ in range(3)]
    w2b = [p.sb("w2b", [128, JT, 128], BF16) for _ in range(2)]
    pstat = p.ps("pstat")
    pg = [p.ps("pg") for _ in range(2)]
    pu = [p.ps("pu") for _ in range(2)]
    py = [p.ps("py") for _ in range(2)]
    hin_v = h_in.rearrange("(kt p) t -> p kt t", p=128)
    hout_v = h_out.rearrange("(kt p) t -> p kt t", p=128)
    bg = bg_fn(prep_staging(p)) if bg_fn is not None else []

    def pre(ci):
        t0, n, kind = chunks[ci]
        hT = hTs[ci % 2]
        hf = hTf[ci % 2]
        p.dma('sp', lambda e: e.dma_start(out=hf[:, :, :n], in_=hin_v[:, :, t0:t0 + n]), writes=hT)
        modulate(p, C, hT, n, co, s, kind, sq, pstat, rstd, tmps, aTs[ci % 2])

    wcnt = [0, 0]
    pre(0)
    for ci, (t0, n, kind) in enumerate(chunks):
        hT = hTs[ci % 2]
        aT = aTs[ci % 2]
        for j in range(JT):
            wb = w13b[wcnt[0] % 3]
            wcnt[0] += 1
            p.dma('sp', lambda e, wb=wb, j=j: e.dma_start(out=wb[:].rearrange("p a b -> p (a b)"), in_=w13s[j]), writes=[wb])
            g = pg[j % 2]
            u = pu[j % 2]
            if j % 2 == 0:
                advance(bg)
            for kt in range(KT):
                mm(p, g, g[:, :n], wb, wb[:, kt, 0:128], aT[kt], aT[kt][:, :n], kt == 0, kt == KT - 1)
            for kt in range(KT):
                mm(p, u, u[:, :n], wb, wb[:, kt, 128:256], aT[kt], aT[kt][:, :n], kt == 0, kt == KT - 1)
            sg = sgs[j % 2]
            p.op('act', lambda e, g=g, sg=sg: e.activation(sg[:, :n], g[:, :n], AF.Silu), reads=[g], writes=[sg])
            p.op('dve', lambda e, u=u, sg=sg, j=j: e.tensor_tensor(HT[:, j, :n], sg[:, :n], u[:, :n], ALU.mult),
                 reads=[sg, u], writes=[HTj[j]])
        if ci + 1 < len(chunks):
            pre(ci + 1)
        for i in range(KT):
            wb = w2b[wcnt[1] % 2]
            wcnt[1] += 1
            p.dma('sp', lambda e, wb=wb, i=i: e.dma_start(out=wb[:].rearrange("p a b -> p (a b)"), in_=w2s[i]), writes=[wb])
            y = py[i % 2]
            for j in range(JT):
                mm(p, y, y[:, :n], wb, wb[:, j, :], HTj[j], HT[:, j, :n], j == 0, j == JT - 1)
            p.op('dve', lambda e, y=y, i=i: e.tensor_copy(ysb[i][:, :n], y[:, :n]), reads=[y], writes=[ysb[i]])
            p.op('act', lambda e, i=i: e.activation(sq[i][:, :n], ysb[i][:, :n], AF.Square), reads=[ysb[i]], writes=[sq[i]])
        post_residual(p, C, hT, ysb, n, co, s, kind, sq, pstat, rstd2, tmps)
        hf = hTf[ci % 2]
        p.dma('pool', lambda e, hf=hf, t0=t0, n=n: e.dma_start(out=hout_v[:, :, t0 - out_off:t0 - out_off + n], in_=hf[:, :, :n]), reads=hT)
    while bg:
        advance(bg)
    p.close_scope()


MLA_SCALE = 96 ** -0.5
SWA_SCALE = 64 ** -0.5
W_CQ, W_CKV, W_KPE, W_QS, W_KS, W_VS = 0, 256, 384, 416, 928, 1056


def attn_weight_groups():
    g = []
    g.append([(0, 128)])
    g.append([(128, 128)])
    g.append([(256, 128)])
    g.append([(320, 96), (0, 32)])
    g.append([(320, 64), (400, 16, -1), (384, 16), (0, 32)])
    for i in range(4):
        g.append([(W_QS + 128 * i, 128)])
    for i in range(4):
        b = W_QS + 128 * i
        g.append([(b + 32, 32, -1), (b, 32), (b + 96, 32, -1), (b + 64, 32)])
    for kv in range(2):
        b = W_KS + 64 * kv
        g.append([(b, 64), (b, 64)])
    for kv in range(2):
        b = W_KS + 64 * kv
        g.append([(b + 32, 32, -1), (b, 32), (b + 32, 32, -1), (b, 32)])
    g.append([(W_VS, 128)])
    return g


def wuq_groups():
    g = []
    for h in range(8):
        g.append([(96 * h, 96), (0, 32)])
    for h in range(8):
        g.append([(96 * h, 64), (96 * h + 80, 16, -1), (96 * h + 64, 16), (0, 32)])
    return g


def attn_proj_phase(p, nc, C, h_in, co, chunks, T, win_s, wuq_s, wk_s, wv_s, qn_d, kvn_d, ropeA, ropeB, scr):
    p.open_scope()
    NT = 512
    ident = C['ident']
    hTf = [p.sb("hT", [128, KT, NT], F32) for _ in range(2)]
    hTs = [p.views(b, KT) for b in hTf]
    aT = p.views(p.sb("aT", [128, KT, NT], BF16), KT)
    sq = p.views(p.sb("sq", [128, KT, NT], BF16), KT)
    rstd = p.sb("rstd", [128, NT], F32)
    rq = p.sb("rq", [128, NT], F32)
    rkv = p.sb("rkv", [128, NT], F32)
    tmps = [p.sb("tmp", [128, NT], F32) for _ in range(3)]
    pstat = p.ps("pstat")
    pa = [p.ps("pa") for _ in range(3)]
    pb = [p.ps("pb") for _ in range(3)]
    win = p.sb("win", [128, 18, KT, 128], BF16)
    p.dma('sp', lambda e: e.dma_start(out=win[:].rearrange("p g a b -> p g (a b)"), in_=win_s.rearrange("g p x -> p g x")), writes=[win])
    wuq = p.sb("wuq", [128, 16, 2, 128], BF16)
    p.dma('sp', lambda e: e.dma_start(out=wuq[:].rearrange("p g a b -> p g (a b)"), in_=wuq_s.rearrange("g p x -> p g x")), writes=[wuq])
    wk = p.sb("wk", [128, 512], BF16)
    p.dma('sp', lambda e: e.dma_start(out=wk[:], in_=wk_s[0]), writes=[wk])
    wv = p.sb("wv", [128, 512], BF16)
    p.dma('sp', lambda e: e.dma_start(out=wv[:], in_=wv_s[0]), writes=[wv])
    stage = p.sb("stage", [128, 128], F32)
    qn = p.sb("qn", [128, 2], F32)
    kvn = p.sb("kvn", [128, 1], F32)
    load_cols(p, qn, qn[:], qn_d, 2, ident, pstat, stage)
    load_cols(p, kvn, kvn[:], kvn_d, 1, ident, pstat, stage)
    cqn = p.views(p.sb("cqn", [128, 2, NT], BF16), 2)
    ckvn = p.sb("ckvn", [128, NT], BF16)
    tA = p.sb("tA", [128, 2, NT], F32)
    tB = p.sb("tB", [128, 2, NT], F32)
    qsb = p.sb("qsb", [128, NT], F32)
    r1 = p.sb("r1", [128, NT], F32)
    r2 = p.sb("r2", [128, NT], F32)
    r3 = p.sb("r3", [128, NT], F32)
    Qst = [p.sb("Qst", [96, NT], BF16) for _ in range(2)]
    Kst = p.sb("Kst", [96, 8, NT], BF16)
    kpe = p.sb("kpe", [96, NT], BF16)
    Vst = p.sb("Vst", [128, 4, 8, 128], BF16)
    VSst = p.sb("VSst", [128, 4, 2, 128], BF16)
    p.op('pool', lambda e: e.memset(Vst[:], 1.0), writes=[Vst])
    p.op('pool', lambda e: e.memset(VSst[:], 1.0), writes=[VSst])
    Sst = [p.sb("Sst", [128, NT], BF16) for _ in range(2)]
    hin_v = h_in.rearrange("(kt p) t -> p kt t", p=128)

    def proj(dst_ps, gidx, n):
        for kt in range(KT):
            mm(p, dst_ps, dst_ps[:, :n], win, win[:, gidx, kt, :], aT[kt], aT[kt][:, :n], kt == 0, kt == KT - 1)

    def rope_rows(ps_x, ps_r, tab, lo, hi, n, out_buf, out_ap, scale):
        p.op('dve', lambda e: e.tensor_tensor(r1[lo:hi, :n], ps_x[lo:hi, :n], tab[lo:hi, 0, :n], ALU.mult), reads=[ps_x, tab], writes=[r1])
        p.op('dve', lambda e: e.tensor_tensor(r2[lo:hi, :n], ps_r[lo:hi, :n], tab[lo:hi, 1, :n], ALU.mult), reads=[ps_r, tab], writes=[r2])
        p.op('dve', lambda e: e.tensor_tensor(r3[lo:hi, :n], r1[lo:hi, :n], r2[lo:hi, :n], ALU.add), reads=[r1, r2], writes=[r3])
        p.op('act', lambda e: e.activation(out_ap, r3[lo:hi, :n], AF.Copy, scale=scale), reads=[r3], writes=[out_buf])

    for ci, (t0, n, kind) in enumerate(chunks):
        hT = hTs[ci % 2]
        hf = hTf[ci % 2]
        p.dma('sp', lambda e, hf=hf, t0=t0, n=n: e.dma_start(out=hf[:, :, :n], in_=hin_v[:, :, t0:t0 + n]), writes=hT)
        p.dma('sp', lambda e, t0=t0, n=n: e.dma_start(out=tA[64:96, :, :n], in_=ropeA[:, :, t0:t0 + n]), writes=[tA])
        p.dma('sp', lambda e, t0=t0, n=n: e.dma_start(out=tB[:, :, :n], in_=ropeB[:, :, t0:t0 + n]), writes=[tB])
        modulate(p, C, hT, n, co, 1, kind, sq, pstat, rstd, tmps, aT)
        for i in range(2):
            proj(pa[i], i, n)
            p.op('dve', lambda e, i=i: e.tensor_copy(tmps[i][:, :n], pa[i][:, :n]), reads=[pa[i]], writes=[tmps[i]])
            p.op('act', lambda e, i=i: e.activation(sq[i][:, :n], tmps[i][:, :n], AF.Square), reads=[tmps[i]], writes=[sq[i]])
        norm_stats(p, C, sq, 2, n, pstat, rq, 1.0 / 256)
        for i in range(2):
            p.op('dve', lambda e, i=i: e.tensor_tensor(tmps[i][:, :n], tmps[i][:, :n], rq[:, :n], ALU.mult), reads=[tmps[i], rq], writes=[tmps[i]])
            p.op('pool', lambda e, i=i: e.tensor_scalar(cqn[i][:, :n], tmps[i][:, :n], qn[:, i:i + 1], None, ALU.mult), reads=[tmps[i], qn], writes=[cqn[i]])
        proj(pa[2], 2, n)
        p.op('dve', lambda e: e.tensor_copy(tmps[2][:, :n], pa[2][:, :n]), reads=[pa[2]], writes=[tmps[2]])
        p.op('act', lambda e: e.activation(sq[2][:, :n], tmps[2][:, :n], AF.Square), reads=[tmps[2]], writes=[sq[2]])
        norm_stats(p, C, sq[2:3], 1, n, pstat, rkv, 1.0 / 128)
        p.op('dve', lambda e: e.tensor_tensor(tmps[2][:, :n], tmps[2][:, :n], rkv[:, :n], ALU.mult), reads=[tmps[2], rkv], writes=[tmps[2]])
        p.op('pool', lambda e: e.tensor_scalar(ckvn[:, :n], tmps[2][:, :n], kvn[:, 0:1], None, ALU.mult), reads=[tmps[2], kvn], writes=[ckvn])
        proj(pa[0], 3, n)
        proj(pb[0], 4, n)
        rope_rows(pa[0], pb[0], tA, 64, 96, n, kpe, kpe[64:96, :n], 1.0)
        for h in range(8):
            pk = pa[1 + h % 2]
            mm(p, pk, pk[0:64, :n], wk, wk[:, 64 * h:64 * h + 64], ckvn, ckvn[:, :n], True, True)
            p.op('dve', lambda e, pk=pk, h=h: e.tensor_copy(Kst[0:64, h, :n], pk[0:64, :n]), reads=[pk], writes=[Kst])
            p.op('act', lambda e, h=h, n=n: e.copy(Kst[64:96, h, :n], kpe[64:96, :n]), reads=[kpe], writes=[Kst])
        p.dma('pool', lambda e, t0=t0, n=n: e.dma_start(out=scr['KT'].rearrange("h r t -> r h t")[:, :, t0:t0 + n], in_=Kst[:, :, :n]), reads=[Kst])
        for h in range(8):
            pq = pa[h % 2]
            pqr = pb[h % 2]
            Q = Qst[h % 2]
            for kt in range(2):
                mm(p, pq, pq[:, :n], wuq, wuq[:, h, kt, :], cqn[kt], cqn[kt][:, :n], kt == 0, kt == 1)
            for kt in range(2):
                mm(p, pqr, pqr[:, :n], wuq, wuq[:, 8 + h, kt, :], cqn[kt], cqn[kt][:, :n], kt == 0, kt == 1)
            p.op('dve', lambda e, pq=pq: e.tensor_copy(qsb[0:64, :n], pq[0:64, :n]), reads=[pq], writes=[qsb])
            rope_rows(pq, pqr, tA, 64, 96, n, Q, Q[64:96, :n], MLA_SCALE)
            p.op('act', lambda e, Q=Q: e.activation(Q[0:64, :n], qsb[0:64, :n], AF.Copy, scale=MLA_SCALE), reads=[qsb], writes=[Q])
            p.dma('pool', lambda e, Q=Q, h=h, t0=t0, n=n: e.dma_start(out=scr['QT'][h, :, t0:t0 + n], in_=Q[:, :n]), reads=[Q])
        nb = n // 128
        for tb in range(nb):
            pv = pa[tb % 2]
            mm(p, pv, pv[:, 0:512], ckvn, ckvn[:, tb * 128:(tb + 1) * 128], wv, wv[:, :], True, True)
            p.op('dve', lambda e, pv=pv, tb=tb: e.tensor_copy(Vst[:, tb, :, 0:64], pv[:, 0:512].rearrange("p (h d) -> p h d", d=64)), reads=[pv], writes=[Vst])
        p.dma('pool', lambda e, t0=t0, nb=nb: e.dma_start(out=scr['V'][t0 // 128:t0 // 128 + nb].rearrange("b p h d -> p b h d"), in_=Vst[:, 0:nb]), reads=[Vst])
        for i in range(4):
            proj(pa[i % 2], 5 + i, n)
            proj(pb[i % 2], 9 + i, n)
            S = Sst[i % 2]
            rope_rows(pa[i % 2], pb[i % 2], tB, 0, 128, n, S, S[:, :n], SWA_SCALE)
            p.dma('pool', lambda e, S=S, i=i, t0=t0, n=n: e.dma_start(out=scr['QS'][i, :, t0:t0 + n], in_=S[:, :n]), reads=[S])
        for kv in range(2):
            proj(pa[kv], 13 + kv, n)
            proj(pb[kv], 15 + kv, n)
            S = Sst[kv]
            rope_rows(pa[kv], pb[kv], tB, 0, 128, n, S, S[:, :n], 1.0)
            p.dma('pool', lambda e, S=S, kv=kv, t0=t0, n=n: e.dma_start(out=scr['KS'][kv, :, t0:t0 + n], in_=S[:, :n]), reads=[S])
        for tb in range(nb):
            pv = pa[2]
            for kt in range(KT):
                mm(p, pv, pv[:, 0:128], aT[kt], aT[kt][:, tb * 128:(tb + 1) * 128], win, win[:, 17, kt, :], kt == 0, kt == KT - 1)
            p.op('dve', lambda e, pv=pv, tb=tb: e.tensor_copy(VSst[:, tb, :, 0:64], pv[:, 0:128].rearrange("p (h d) -> p h d", d=64)), reads=[pv], writes=[VSst])
        p.dma('pool', lambda e, t0=t0, nb=nb: e.dma_start(out=scr['VS'][t0 // 128:t0 // 128 + nb].rearrange("b p h d -> p b h d"), in_=VSst[:, 0:nb]), reads=[VSst])
    p.close_scope()


def normalize_store(p, C, pO, n, Osb, pR, On, dst_ap, extra_den=None, act_recip=False):
    p.op('dve', lambda e: e.tensor_copy(Osb[:, :n], pO[:, :n]), reads=[pO], writes=[Osb])
    if extra_den is not None:
        p.op('dve', lambda e: e.tensor_scalar(Osb[64:128, :n], Osb[64:128, :n], extra_den, None, ALU.add), reads=[Osb], writes=[Osb])
    if act_recip:
        p.op('act', lambda e: e.activation(Osb[64:128, :n], Osb[64:128, :n], AF.Ln), reads=[Osb], writes=[Osb])
        p.op('act', lambda e: e.activation(Osb[64:128, :n], Osb[64:128, :n], AF.Exp, scale=-1.0), reads=[Osb], writes=[Osb])
    else:
        p.op('dve', lambda e: e.reciprocal(Osb[64:128, :n], Osb[64:128, :n]), reads=[Osb], writes=[Osb])
    mm(p, pR, pR[0:64, :n], C['shiftI'], C['shiftI'][:, :], Osb, Osb[:, :n], True, True)
    p.op('dve', lambda e: e.tensor_tensor(On[0:64, :n], Osb[0:64, :n], pR[0:64, :n], ALU.mult), reads=[Osb, pR], writes=[On])
    p.dma('pool', lambda e: e.dma_start(out=dst_ap, in_=On[0:64, :n]), reads=[On])


def mla_phase(p, nc, C, chunks, T, scr):
    p.open_scope()
    NT = 512
    G3 = 3
    NKT = T // 128
    Kh = [p.sb("Kh", [96, T], BF16) for _ in range(2)]
    Vh = [p.sb("Vh", [128, NKT, 128], BF16) for _ in range(2)]
    Qc = [p.sb("Qc", [96, NT], BF16) for _ in range(2)]
    Pt = [p.sb("Pt", [128, G3, NT], BF16) for _ in range(2)]
    Osb = [p.sb("Osb", [128, NT], F32) for _ in range(2)]
    On = [p.sb("On", [64, NT], BF16) for _ in range(2)]
    pS = [p.ps("pS", (128, G3, NT)) for _ in range(2)]
    pO = p.ps("pO")
    pR = p.ps("pR")
    qi = 0
    for h in range(8):
        K = Kh[h % 2]
        V = Vh[h % 2]
        p.dma('sp', lambda e, K=K, h=h: e.dma_start(out=K[:], in_=scr['KT'][h]), writes=[K])
        p.dma('sp', lambda e, V=V, h=h: e.dma_start(out=V[:], in_=scr['V'][:, :, h, :].rearrange("b p d -> p b d")), writes=[V])
        for ci, (t0, n, kind) in enumerate(chunks):
            Q = Qc[qi % 2]
            on = On[qi % 2]
            osb = Osb[qi % 2]
            qi += 1
            p.dma('sp', lambda e, Q=Q, h=h, t0=t0, n=n: e.dma_start(out=Q[:, :n], in_=scr['QT'][h, :, t0:t0 + n]), writes=[Q])
            kts = list(range(2)) if kind == 1 else list(range(NKT))
            groups = [kts[i:i + G3] for i in range(0, len(kts), G3)]

            def smm(gi):
                ps = pS[gi % 2]
                for j, kt in enumerate(groups[gi]):
                    mm(p, ps, ps[:, j, :n], K, K[0:96, kt * 128:(kt + 1) * 128], Q, Q[0:96, :n], True, True)

            smm(0)
            nmm = len(kts)
            done = 0
            for gi, grp in enumerate(groups):
                if gi + 1 < len(groups):
                    smm(gi + 1)
                ps = pS[gi % 2]
                pt = Pt[gi % 2]
                ng = len(grp)
                p.op('act', lambda e, ps=ps, pt=pt, ng=ng, n=n: e.activation(pt[:, 0:ng, :n], ps[:, 0:ng, :n], AF.Exp), reads=[ps], writes=[pt])
                for j, kt in enumerate(grp):
                    mm(p, pO, pO[:, :n], V, V[:, kt, :], pt, pt[:, j, :n], done == 0, done == nmm - 1)
                    done += 1
            normalize_store(p, C, pO, n, osb, pR, on, scr['OT'][h * 64:(h + 1) * 64, t0:t0 + n])
    p.close_scope()


def swa_phase(p, nc, C, chunks, T, scr, sink_d):
    p.open_scope()
    NT = 512
    NKT = T // 128
    KS = p.sb("KS", [128, 2, T], BF16)
    p.dma('sp', lambda e: e.dma_start(out=KS[:], in_=scr['KS'].rearrange("k p t -> p k t")), writes=[KS])
    VS = p.sb("VS", [128, NKT, 2, 128], BF16)
    p.dma('sp', lambda e: e.dma_start(out=VS[:], in_=scr['VS'].rearrange("b p k d -> p b k d")), writes=[VS])
    esink = p.sb("esink", [128, 8], F32)
    p.dma('sp', lambda e: e.dma_start(out=esink[:], in_=sink_d), writes=[esink])
    p.op('act', lambda e: e.activation(esink[:], esink[:], AF.Exp), reads=[esink], writes=[esink])
    QS = [p.sb("QS", [128, 4, NT], BF16) for _ in range(2)]
    Pc4 = [p.sb("Pc", [128, NT], BF16) for _ in range(4)]
    Pl8 = [p.sb("Pl", [128, 384], BF16) for _ in range(8)]
    Osb2 = [p.sb("Osb", [128, NT], F32) for _ in range(2)]
    On = [p.sb("On", [64, NT], BF16) for _ in range(2)]
    pC = [p.ps("pC") for _ in range(2)]
    pL = [p.ps("pL") for _ in range(2)]
    pO = [p.ps("pO") for _ in range(2)]
    pR = p.ps("pR")
    mprev = C['mprev']
    mnext = C['mnext']
    oi = 0
    for ci, (t0, n, kind) in enumerate(chunks):
        Q = QS[ci % 2]
        p.dma('sp', lambda e, Q=Q, t0=t0, n=n: e.dma_start(out=Q[:, :, :n], in_=scr['QS'].rearrange("i p t -> p i t")[:, :, t0:t0 + n]), writes=[Q])
        nb = n // 128
        for hh in range(8):
            i, e2 = hh // 2, hh % 2
            kv = hh // 4
            lo, hi = 64 * e2, 64 * e2 + 64
            po = pO[oi % 2]
            on = On[oi % 2]
            Osb = Osb2[oi % 2]
            Pc = Pc4[2 * (oi % 2):2 * (oi % 2) + 2]
            Pl = Pl8[4 * (oi % 2):4 * (oi % 2) + 4]
            oi += 1
            for c2 in range(2):
                mm(p, pC[c2], pC[c2][:, :n], KS, KS[lo:hi, kv, c2 * 128:(c2 + 1) * 128], Q, Q[lo:hi, i, :n], True, True)
                p.op('act', lambda e, c2=c2, Pc=Pc, n=n: e.activation(Pc[c2][:, :n], pC[c2][:, :n], AF.Exp), reads=[pC[c2]], writes=[Pc[c2]])
            loc = []
            if kind == 0:
                for qb in range(nb):
                    kt_c = (t0 // 128) + qb
                    tiles = [(kt_c - 1, 0), (kt_c, 1), (kt_c + 1, 2)]
                    tiles = [(kt, s) for kt, s in tiles if 2 <= kt < NKT]
                    pl = pL[qb % 2]
                    P = Pl[qb]
                    for kt, s in tiles:
                        mm(p, pl, pl[:, s * 128:(s + 1) * 128], KS, KS[lo:hi, kv, kt * 128:(kt + 1) * 128], Q, Q[lo:hi, i, qb * 128:(qb + 1) * 128], True, True)
                    c0, c1 = tiles[0][1] * 128, tiles[-1][1] * 128 + 128
                    p.op('act', lambda e, pl=pl, P=P, c0=c0, c1=c1: e.activation(P[:, c0:c1], pl[:, c0:c1], AF.Exp), reads=[pl], writes=[P])
                    for kt, s in tiles:
                        if s == 0:
                            p.op('dve', lambda e, P=P: e.tensor_tensor(P[:, 0:128], P[:, 0:128], mprev[:], ALU.mult), reads=[P, mprev], writes=[P])
                        if s == 2:
                            p.op('dve', lambda e, P=P: e.tensor_tensor(P[:, 256:384], P[:, 256:384], mnext[:], ALU.mult), reads=[P, mnext], writes=[P])
                    loc.append(tiles)
            for qb in range(nb):
                items = [(0, Pc[0], qb * 128), (1, Pc[1], qb * 128)]
                if kind == 0:
                    for kt, s in loc[qb]:
                        items.append((kt, Pl[qb], s * 128))
                for k2, (kt, P, c0) in enumerate(items):
                    mm(p, po, po[:, qb * 128:(qb + 1) * 128], VS, VS[:, kt, kv, :], P, P[:, c0:c0 + 128], k2 == 0, k2 == len(items) - 1)
            normalize_store(p, C, po, n, Osb, pR, on, scr['OT'][(8 + hh) * 64:(9 + hh) * 64, t0:t0 + n], extra_den=esink[64:128, hh:hh + 1], act_recip=True)
    p.close_scope()


def mixout_phase(p, nc, C, y_src, wout_s, h_in, h_out, co, s, chunks, nkt_in):
    p.open_scope()
    NT = 512
    hTf = [p.sb("hT", [128, KT, NT], F32) for _ in range(2)]
    hTs = [p.views(b, KT) for b in hTf]
    yTf = [p.sb("yT", [128, nkt_in, NT], BF16) for _ in range(2)]
    sq = p.views(p.sb("sq", [128, KT, NT], BF16), KT)
    rstd2 = p.sb("rstd2", [128, NT], F32)
    tmps = [p.sb("tmp", [128, NT], F32) for _ in range(3)]
    ysb = p.views(p.sb("ysb", [128, KT, NT], F32), KT)
    wo = p.sb("wo", [128, KT, nkt_in, 128], BF16)
    p.dma('sp', lambda e: e.dma_start(out=wo[:].rearrange("p i a b -> p i (a b)"), in_=wout_s.rearrange("i p x -> p i x")), writes=[wo])
    pstat = p.ps("pstat")
    py = [p.ps("py") for _ in range(2)]
    hin_v = h_in.rearrange("(kt p) t -> p kt t", p=128)
    hout_v = h_out.rearrange("(kt p) t -> p kt t", p=128)
    ysrc_v = y_src.rearrange("(kt p) t -> p kt t", p=128)
    for ci, (t0, n, kind) in enumerate(chunks):
        hT = hTs[ci % 2]
        hf = hTf[ci % 2]
        yT = yTf[ci % 2]
        p.dma('sp', lambda e, hf=hf, t0=t0, n=n: e.dma_start(out=hf[:, :, :n], in_=hin_v[:, :, t0:t0 + n]), writes=hT)
        p.dma('sp', lambda e, yT=yT, t0=t0, n=n: e.dma_start(out=yT[:, :, :n], in_=ysrc_v[:, :, t0:t0 + n]), writes=[yT])
        for i in range(KT):
            y = py[i % 2]
            for j in range(nkt_in):
                mm(p, y, y[:, :n], wo, wo[:, i, j, :], yT, yT[:, j, :n], j == 0, j == nkt_in - 1)
            p.op('dve', lambda e, y=y, i=i: e.tensor_copy(ysb[i][:, :n], y[:, :n]), reads=[y], writes=[ysb[i]])
            p.op('act', lambda e, i=i: e.activation(sq[i][:, :n], ysb[i][:, :n], AF.Square), reads=[ysb[i]], writes=[sq[i]])
        post_residual(p, C, hT, ysb, n, co, s, kind, sq, pstat, rstd2, tmps)
        p.dma('pool', lambda e, hf=hf, t0=t0, n=n: e.dma_start(out=hout_v[:, :, t0:t0 + n], in_=hf[:, :, :n]), reads=hT)
    p.close_scope()


S5T = 8
MAGIC = 12582912.0
TWO_PI = 6.283185307179586
PI = 3.141592653589793


def s5_in_phase(p, nc, C, h_in, co, chunks, w5_s, dsk_d, uT_d, y0_d):
    p.open_scope()
    NT = 512
    ident = C['ident']
    hTf = [p.sb("hT", [128, KT, NT], F32) for _ in range(2)]
    hTs = [p.views(b, KT) for b in hTf]
    aT = p.views(p.sb("aT", [128, KT, NT], BF16), KT)
    sq = p.views(p.sb("sq", [128, KT, NT], BF16), KT)
    rstd = p.sb("rstd", [128, NT], F32)
    tmps = [p.sb("tmp", [128, NT], F32) for _ in range(3)]
    pstat = p.ps("pstat")
    pu = [p.ps("pu") for _ in range(2)]
    w5 = p.sb("w5", [128, 8, KT, 128], BF16)
    p.dma('sp', lambda e: e.dma_start(out=w5[:].rearrange("p g a b -> p g (a b)"), in_=w5_s.rearrange("g p x -> p g x")), writes=[w5])
    stage = p.sb("stage", [128, 128], F32)
    dsk = p.sb("dsk", [128, 8], F32)
    load_cols(p, dsk, dsk[:], dsk_d, 8, ident, pstat, stage)
    ub = [p.sb("ub", [128, NT], BF16) for _ in range(2)]
    y0 = [p.sb("y0", [128, NT], F32) for _ in range(2)]
    hin_v = h_in.rearrange("(kt p) t -> p kt t", p=128)
    for ci, (t0, n, kind) in enumerate(chunks):
        hT = hTs[ci % 2]
        hf = hTf[ci % 2]
        p.dma('sp', lambda e, hf=hf, t0=t0, n=n: e.dma_start(out=hf[:, :, :n], in_=hin_v[:, :, t0:t0 + n]), writes=hT)
        modulate(p, C, hT, n, co, 1, kind, sq, pstat, rstd, tmps, aT)
        for ct in range(8):
            ps = pu[ct % 2]
            for kt in range(KT):
                mm(p, ps, ps[:, :n], w5, w5[:, ct, kt, :], aT[kt], aT[kt][:, :n], kt == 0, kt == KT - 1)
            u = ub[ct % 2]
            p.op('dve', lambda e, ps=ps, u=u: e.tensor_copy(u[:, :n], ps[:, :n]), reads=[ps], writes=[u])
            p.dma('pool', lambda e, u=u, ct=ct, t0=t0, n=n: e.dma_start(out=uT_d[ct * 128:(ct + 1) * 128, t0:t0 + n], in_=u[:, :n]), reads=[u])
            if kind == 0:
                y = y0[ct % 2]
                p.op('dve', lambda e, ps=ps, y=y, ct=ct: e.tensor_scalar(y[:, :n], ps[:, :n], dsk[:, ct:ct + 1], None, ALU.mult), reads=[ps, dsk], writes=[y])
                p.dma('pool', lambda e, y=y, ct=ct, t0=t0, n=n: e.dma_start(out=y0_d[ct * 128:(ct + 1) * 128, t0 - NCTX:t0 - NCTX + n], in_=y[:, :n]), reads=[y])
    p.close_scope()


def s5_scan_phase(p, nc, C, T, SEQ, uT_d, y0_d, gT_d, lam_d, B12_d, C1_d, J_d, rmask_d):
    p.open_scope()
    ident = C['ident']
    NSC = T // S5T
    NCC = NCTX // S5T
    NLC = SEQ // S5T
    XW = NSC + 2
    NB = NSC // 8
    assert NB * 8 == NSC
    NRB = 0
    while (1 << NRB) < NB:
        NRB += 1
    NG = 8
    KLIST = list(range(1, 9)) + [8 * m for m in range(2, 8)] + [64 << r for r in range(NRB)]
    KIDX = {k: i for i, k in enumerate(KLIST)}
    NK = len(KLIST)
    NUL = 56 + NRB * 8
    Jt = p.sb("Jt", [128, 128], F32)
    p.dma('sp', lambda e: e.dma_start(out=Jt[:], in_=J_d), writes=[Jt])
    rmask = p.sb("rmask", [128, 8], F32)
    p.dma('sp', lambda e: e.dma_start(out=rmask[:], in_=rmask_d), writes=[rmask])
    pb = [p.ps("pb%d" % i) for i in range(8)]
    u_de = p.sb("u_de", [128, S5T, NSC], BF16)
    Y = p.sb("Y", [128, SEQ], F32)
    Xb = p.sb("Xb", [128, 8, NSC], BF16)
    Xbg = p.views(Xb, 8)
    Hin = p.sb("Hin", [128, 8, NSC], BF16)
    Hing = p.views(Hin, 8)
    Wf = p.sb("Wf", [128, 8, NB], F32)
    Wfg = p.views(Wf, 8)
    Wb = p.sb("Wb", [128, 8, NB + 2], BF16)
    Wbg = p.views(Wb, 8)
    p.op('pool', lambda e: e.memset(Wb[:], 0.0), writes=[Wb] + Wbg)
    identb = p.sb("identb", [128, 128], BF16)
    p.op('dve', lambda e: e.tensor_copy(identb[:], ident[:]), reads=[ident], writes=[identb])
    CtL = p.sb("CtL", [128, 8, 8, 128], BF16)
    p.op('pool', lambda e: e.memset(CtL[:], 0.0), writes=[CtL])
    CtLv = [[Buf("CtLv", CtL.t[:, t, g, :]) for g in range(NG)] for t in range(8)]
    UL = p.sb("UL", [128, max(NUL, 64), 128], BF16)
    ULv = [Buf("ULv", UL.t[:, i, :]) for i in range(max(NUL, 64))]
    KdL = p.sb("KdL", [128, 8, 128], BF16)
    MBm = p.sb("MBm", [128, 8, 128], F32)
    p.op('pool', lambda e: e.memset(MBm[:], 0.0), writes=[MBm])
    MBv = p.views(MBm, 8)
    ctf = p.sb("ctf", [128, 9, 8, 16], F32)
    ctfv = [[Buf("ctfv", ctf.t[:, k, g, :]) for g in range(NG)] for k in range(9)]
    Qk = [p.sb("Qk", [128, 128], F32) for _ in range(8)]
    Bs1 = p.sb("Bs1", [128, 8, 16], F32)
    Bs2 = p.sb("Bs2", [128, 8, 16], F32)
    Cs1 = p.sb("Cs1", [128, 8, 16], F32)
    Cs2 = p.sb("Cs2", [128, 8, 16], F32)
    tb = {nm: p.sb("tb_" + nm, [128, 64], F32) for nm in
          ['lre', 'lim', 'lst', 'dt', 'lr', 'lrdt', 'mag', 'ang', 'angk', 'red', 'sin', 'cos', 'ar', 'ai', 'nai', 'den', 'am1', 'fr', 'fi', 'nfi', 't1', 't2']}
    tbd = [{nm: p.sb("tbd_" + nm, [128, 64], F32) for nm in ['fr', 'S2']} for _ in range(2)]
    pw = [[{nm: p.sb("pw_" + nm, [128, 64], F32) for nm in (['AR', 'T5', 'T3', 'T4'] if KLIST[ki] <= 8 else ['T3', 'T4'])}
           for ki in range(NK)] for _ in range(2)]

    def ew(eng, fn, reads, writes):
        p.op(eng, fn, reads=reads, writes=writes)

    def reduce_sin(dst, src, shift):
        r = tb['red']
        ew('dve', lambda e: e.tensor_scalar(r[:], src[:], float(shift), None, ALU.add), [src], [r])
        ew('dve', lambda e: e.tensor_scalar(tb['t2'][:], r[:], float(1.0 / TWO_PI), MAGIC, ALU.mult, ALU.add), [r], [tb['t2']])
        ew('dve', lambda e: e.tensor_scalar(tb['t2'][:], tb['t2'][:], -MAGIC, None, ALU.add), [tb['t2']], [tb['t2']])
        ew('dve', lambda e: e.scalar_tensor_tensor(r[:], tb['t2'][:], float(-TWO_PI), r[:], ALU.mult, ALU.add), [tb['t2'], r], [r])
        ew('dve', lambda e: e.tensor_scalar(r[:], r[:], float(PI), float(-PI), ALU.min, ALU.max), [r], [r])
        ew('act', lambda e: e.activation(dst[:], r[:], AF.Sin), [r], [dst])

    def tt(dst, a, b, op, eng='dve'):
        ew(eng, lambda e: e.tensor_tensor(dst[:], a[:], b[:], op), [a, b], [dst])

    def half_copy(dst, top, bot):
        ew('dve', lambda e: e.tensor_copy(dst[0:64, :], top[0:64, :]), [top], [dst])
        ew('dve', lambda e: e.tensor_copy(dst[64:128, :], bot[64:128, :]), [bot], [dst])

    def power(k, dst):
        ew('dve', lambda e: e.tensor_scalar(tb['t1'][:], tb['lrdt'][:], float(k), None, ALU.mult), [tb['lrdt']], [tb['t1']])
        ew('act', lambda e: e.activation(tb['mag'][:], tb['t1'][:], AF.Exp), [tb['t1']], [tb['mag']])
        ew('dve', lambda e: e.tensor_scalar(tb['angk'][:], tb['ang'][:], float(k), None, ALU.mult), [tb['ang']], [tb['angk']])
        reduce_sin(tb['sin'], tb['angk'], 0.0)
        reduce_sin(tb['cos'], tb['angk'], PI / 2)
        tt(tb['ar'], tb['mag'], tb['cos'], ALU.mult)
        tt(tb['ai'], tb['mag'], tb['sin'], ALU.mult)
        ew('dve', lambda e: e.tensor_scalar(tb['nai'][:], tb['ai'][:], -1.0, None, ALU.mult), [tb['ai']], [tb['nai']])

    for d in range(2):
        for i, nm in enumerate(['lre', 'lim', 'lst']):
            p.dma('sp', lambda e, i=i, nm=nm, d=d: e.dma_start(out=tb[nm][:], in_=lam_d[d, i]), writes=[tb[nm]])
        ew('act', lambda e: e.activation(tb['dt'][:], tb['lst'][:], AF.Exp), [tb['lst']], [tb['dt']])
        ew('dve', lambda e: e.tensor_scalar(tb['lr'][:], tb['lre'][:], -1e-4, None, ALU.min), [tb['lre']], [tb['lr']])
        tt(tb['lrdt'], tb['lr'], tb['dt'], ALU.mult)
        tt(tb['ang'], tb['lim'], tb['dt'], ALU.mult)
        for ki, k in enumerate(KLIST):
            power(k, None)
            t = pw[d][ki]
            if 'AR' in t:
                ew('pool', lambda e, t=t: e.tensor_copy(t['AR'][:], tb['ar'][:]), [tb['ar']], [t['AR']])
                half_copy(t['T5'], tb['ai'], tb['nai'])
            half_copy(t['T3'], tb['ar'], tb['nai'])
            half_copy(t['T4'], tb['ai'], tb['ar'])
            if k == 1:
                tt(tb['den'], tb['lr'], tb['lr'], ALU.mult)
                tt(tb['t1'], tb['lim'], tb['lim'], ALU.mult)
                tt(tb['den'], tb['den'], tb['t1'], ALU.add)
                ew('dve', lambda e: e.reciprocal(tb['den'][:], tb['den'][:]), [tb['den']], [tb['den']])
                ew('dve', lambda e: e.tensor_scalar(tb['am1'][:], tb['ar'][:], -1.0, None, ALU.add), [tb['ar']], [tb['am1']])
                tt(tb['t1'], tb['am1'], tb['lr'], ALU.mult)
                tt(tb['t2'], tb['ai'], tb['lim'], ALU.mult)
                tt(tb['t1'], tb['t1'], tb['t2'], ALU.add)
                tt(tbd[d]['fr'], tb['t1'], tb['den'], ALU.mult)
                tt(tb['t1'], tb['ai'], tb['lr'], ALU.mult)
                tt(tb['t2'], tb['am1'], tb['lim'], ALU.mult)
                tt(tb['t1'], tb['t1'], tb['t2'], ALU.subtract)
                tt(tb['fi'], tb['t1'], tb['den'], ALU.mult)
                ew('dve', lambda e: e.tensor_scalar(tb['nfi'][:], tb['fi'][:], -1.0, None, ALU.mult), [tb['fi']], [tb['nfi']])
                half_copy(tbd[d]['S2'], tb['nfi'], tb['fi'])

    Xbflat = Xb[:].rearrange("p a b -> p (a b)")
    for ct in range(8):
        for d in range(2):
            fwd = (d == 0)
            if fwd:
                pieces = [(0, NCC, 0)]
                c = 0
                while c < NLC:
                    w = min(512, NLC - c)
                    pieces.append((NCC + c, w, NCTX + c * S5T))
                    c += w
            else:
                pieces = []
                c = 0
                while c < NLC:
                    w = min(512, NLC - c)
                    pieces.append((c, w, NCTX + c * S5T))
                    c += w
                pieces.append((NLC, NCC, 0))
            if d == 0:
                p.sync_all()
                p.dma('sp', lambda e, ct=ct: e.dma_start(out=Xbflat[:, 0:T], in_=uT_d[ct * 128:(ct + 1) * 128, :]), writes=[Xb])
                uv = Xbflat[:, 0:T].rearrange("p (c s) -> p c s", s=S5T)
                for s in range(S5T):
                    eng = 'dve' if s % 2 == 0 else 'act'
                    if eng == 'dve':
                        ew('dve', lambda e, s=s, uv=uv: e.tensor_copy(u_de[:, s, :], uv[:, :, s]), [Xb], [u_de])
                    else:
                        ew('act', lambda e, s=s, uv=uv: e.copy(u_de[:, s, :], uv[:, :, s]), [Xb], [u_de])
                p.dma('sp', lambda e, ct=ct: e.dma_start(out=Y[:], in_=y0_d[ct * 128:(ct + 1) * 128, :]), writes=[Y])
                p.sync_all()
            p.dma('sp', lambda e, ct=ct, d=d: e.dma_start(out=Bs1[:], in_=B12_d[d, 0, :, ct * 8:(ct + 1) * 8, :]), writes=[Bs1])
            p.dma('sp', lambda e, ct=ct, d=d: e.dma_start(out=Bs2[:], in_=B12_d[d, 1, :, ct * 8:(ct + 1) * 8, :]), writes=[Bs2])
            p.dma('sp', lambda e, ct=ct, d=d: e.dma_start(out=Cs1[:], in_=C1_d[d, 0, :, ct * 8:(ct + 1) * 8, :]), writes=[Cs1])
            p.dma('sp', lambda e, ct=ct, d=d: e.dma_start(out=Cs2[:], in_=C1_d[d, 1, :, ct * 8:(ct + 1) * 8, :]), writes=[Cs2])
            ew('pool', lambda e: e.tensor_scalar(Cs1[64:128], Cs1[64:128], -1.0, None, ALU.mult), [Cs1], [Cs1])
            ew('pool', lambda e: e.tensor_scalar(Cs2[0:64], Cs2[0:64], -1.0, None, ALU.mult), [Cs2], [Cs2])
            t_fr, t_S2 = tbd[d]['fr'], tbd[d]['S2']
            for g in range(NG):
                gg = ct * 8 + g
                blk = slice(16 * g, 16 * g + 16)
                ew('dve', lambda e, g=g, gg=gg, blk=blk, t_fr=t_fr: e.tensor_scalar(MBm[:, g, blk], Bs1[:, g, :], t_fr[:, gg:gg + 1], None, ALU.mult), [Bs1, t_fr], [MBv[g]])
                ew('dve', lambda e, g=g, gg=gg, blk=blk, t_S2=t_S2: e.scalar_tensor_tensor(MBm[:, g, blk], Bs2[:, g, :], t_S2[:, gg:gg + 1], MBm[:, g, blk], ALU.mult, ALU.add), [Bs2, t_S2, MBv[g]], [MBv[g]])
                ew('dve', lambda e, g=g: e.tensor_copy(ctf[:, 0, g, :], Cs1[:, g, :]), [Cs1], [ctfv[0][g]])
            BsT = [p.view(pb[3 + k // 4], pb[3 + k // 4].t[:, (k % 4) * 128:(k % 4 + 1) * 128]) for k in range(8)]
            KdA = [p.view(pb[5 + k // 4], pb[5 + k // 4].t[:, (k % 4) * 128:(k % 4 + 1) * 128]) for k in range(8)]
            qi = 0
            for k in range(0, 9):
                tk = pw[d][k - 1] if k >= 1 else None
                for g in range(NG):
                    gg = ct * 8 + g
                    blk = slice(16 * g, 16 * g + 16)
                    if k >= 1:
                        cv = ctfv[k][g]
                        ew('dve', lambda e, k=k, g=g, gg=gg, tk=tk: e.tensor_scalar(ctf[:, k, g, :], Cs1[:, g, :], tk['AR'][:, gg:gg + 1], None, ALU.mult), [Cs1, tk['AR']], [cv])
                        ew('dve', lambda e, k=k, g=g, gg=gg, tk=tk: e.scalar_tensor_tensor(ctf[:, k, g, :], Cs2[:, g, :], tk['T5'][:, gg:gg + 1], ctf[:, k, g, :], ALU.mult, ALU.add), [Cs2, tk['T5'], cv], [cv])
                        t_idx = (k - 1) if fwd else (8 - k)
                        ew('act', lambda e, k=k, g=g, t_idx=t_idx, blk=blk: e.copy(CtL[:, t_idx, g, blk], ctf[:, k, g, :]), [cv], [CtLv[t_idx][g]])
                    if k <= 7:
                        if k == 0:
                            Qb_, Qap = ident, ident[:, :]
                        else:
                            Qb_ = Qk[qi % 8]
                            qi += 1
                            if qi % 2 == 0:
                                ew('act', lambda e, Qb_=Qb_, gg=gg, tk=tk: e.activation(Qb_[:, 0:64], Jt[:, 0:64], AF.Copy, scale=tk['T3'][:, gg:gg + 1]), [Jt, tk['T3']], [Qb_])
                                ew('act', lambda e, Qb_=Qb_, gg=gg, tk=tk: e.activation(Qb_[:, 64:128], Jt[:, 64:128], AF.Copy, scale=tk['T4'][:, gg:gg + 1]), [Jt, tk['T4']], [Qb_])
                            else:
                                ew('dve', lambda e, Qb_=Qb_, gg=gg, tk=tk: e.tensor_scalar(Qb_[:, 0:64], Jt[:, 0:64], tk['T3'][:, gg:gg + 1], None, ALU.mult), [Jt, tk['T3']], [Qb_])
                                ew('dve', lambda e, Qb_=Qb_, gg=gg, tk=tk: e.tensor_scalar(Qb_[:, 64:128], Jt[:, 64:128], tk['T4'][:, gg:gg + 1], None, ALU.mult), [Jt, tk['T4']], [Qb_])
                            Qap = Qb_[:, :]
                        mm(p, BsT[k], BsT[k][:, :], MBv[g], MBm[:, g, :], Qb_, Qap, g == 0, g == NG - 1)
                        mm(p, KdA[k], KdA[k][:, blk], MBv[g], MBm[:, g, :], ctfv[k][g], ctf[:, k, g, :], True, True)
            p.sync_all()
            for k in range(8):
                s_idx = (7 - k) if fwd else k
                for g in range(NG):
                    if k < 4:
                        ew('dve', lambda e, k=k, g=g, s_idx=s_idx: e.tensor_scalar(UL[:, s_idx * 8 + g, :], BsT[k][:, :], rmask[:, g:g + 1], None, ALU.mult), [BsT[k], rmask], [ULv[s_idx * 8 + g]])
                    else:
                        ew('act', lambda e, k=k, g=g, s_idx=s_idx: e.activation(UL[:, s_idx * 8 + g, :], BsT[k][:, :], AF.Copy, scale=rmask[:, g:g + 1]), [BsT[k], rmask], [ULv[s_idx * 8 + g]])
                if k < 4:
                    ew('dve', lambda e, k=k: e.tensor_copy(KdL[:, k, :], KdA[k][:, :]), [KdA[k]], [KdL])
                else:
                    ew('act', lambda e, k=k: e.copy(KdL[:, k, :], KdA[k][:, :]), [KdA[k]], [KdL])
            p.sync_all()
            bi = 0
            for g in range(NG):
                for (c0, ncol, tok0) in pieces:
                    ps = pb[bi % 4]
                    bi += 1
                    j0 = tok0 // S5T
                    for s in range(S5T):
                        mm(p, ps, ps[:, :ncol], ULv[s * 8 + g], UL[:, s * 8 + g, :], u_de, u_de[:, s, j0:j0 + ncol], s == 0, s == S5T - 1)
                    ew('dve', lambda e, ps=ps, g=g, c0=c0, ncol=ncol: e.tensor_copy(Xb[:, g, c0:c0 + ncol], ps[:, :ncol]), [ps], [Xbg[g]])
            p.sync_all()
            def build_tile(idx, k, g, gg, par):
                tk = pw[d][KIDX[k]]
                uv_ = ULv[idx]
                if par != 2:
                    ew('dve', lambda e: e.tensor_scalar(UL[:, idx, 0:64], Jt[:, 0:64], tk['T3'][:, gg:gg + 1], None, ALU.mult), [Jt, tk['T3']], [uv_])
                    ew('dve', lambda e: e.tensor_scalar(UL[:, idx, 64:128], Jt[:, 64:128], tk['T4'][:, gg:gg + 1], None, ALU.mult), [Jt, tk['T4']], [uv_])
                else:
                    ew('act', lambda e: e.activation(UL[:, idx, 0:64], Jt[:, 0:64], AF.Copy, scale=tk['T3'][:, gg:gg + 1]), [Jt, tk['T3']], [uv_])
                    ew('act', lambda e: e.activation(UL[:, idx, 64:128], Jt[:, 64:128], AF.Copy, scale=tk['T4'][:, gg:gg + 1]), [Jt, tk['T4']], [uv_])

            cntb = 0
            for m in range(1, 8):
                for g in range(NG):
                    build_tile((m - 1) * 8 + g, 8 * m, g, ct * 8 + g, cntb % 3)
                    cntb += 1
            for r in range(NRB):
                for g in range(NG):
                    build_tile(56 + r * 8 + g, 64 << r, g, ct * 8 + g, cntb % 3)
                    cntb += 1
            p.sync_all()

            def LA(m, g):
                if m == 0:
                    return identb, identb[:, :]
                return ULv[(m - 1) * 8 + g], UL[:, (m - 1) * 8 + g, :]

            xv = [Xb[:, g, :].rearrange("p (c j) -> p c j", j=8) for g in range(NG)]
            hv = [Hin[:, g, :].rearrange("p (c j) -> p c j", j=8) for g in range(NG)]
            bi = 0
            for g in range(NG):
                ps = pb[bi % 4]
                bi += 1
                for j in range(8):
                    lb, lap = LA((7 - j) if fwd else j, g)
                    mm(p, ps, ps[:, :NB], lb, lap, Xbg[g], xv[g][:, :, j], j == 0, j == 7)
                ew('dve', lambda e, ps=ps, g=g: e.tensor_copy(Wf[:, g, :], ps[:, :NB]), [ps], [Wfg[g]])
                ew('act', lambda e, g=g: e.copy(Wb[:, g, 1:1 + NB], Wf[:, g, :]), [Wfg[g]], [Wbg[g]])
            for r in range(NRB):
                sh = 1 << r
                L = NB - sh
                if L <= 0:
                    continue
                if fwd:
                    dst0, src0 = sh, 0
                else:
                    dst0, src0 = 0, sh
                for g in range(NG):
                    ps = pb[bi % 4]
                    bi += 1
                    mm(p, ps, ps[:, :L], ULv[56 + r * 8 + g], UL[:, 56 + r * 8 + g, :], Wbg[g], Wb[:, g, 1 + src0:1 + src0 + L], True, True)
                    ew('dve', lambda e, ps=ps, g=g, L=L, dst0=dst0: e.tensor_tensor(Wf[:, g, dst0:dst0 + L], Wf[:, g, dst0:dst0 + L], ps[:, :L], ALU.add), [Wfg[g], ps], [Wfg[g]])
                    ew('act', lambda e, g=g, L=L, dst0=dst0: e.copy(Wb[:, g, 1 + dst0:1 + dst0 + L], Wf[:, g, dst0:dst0 + L]), [Wfg[g]], [Wbg[g]])
            ge0 = 0 if fwd else 2
            for g in range(NG):
                for j in range(8):
                    mG = j if fwd else (7 - j)
                    others = list(range(0, j)) if fwd else list(range(j + 1, 8))
                    if mG == 0 and not others:
                        ew('act', lambda e, g=g, j=j, ge0=ge0, hvg=hv[g]: e.copy(hvg[:, :, j], Wb[:, g, ge0:ge0 + NB]), [Wbg[g]], [Hing[g]])
                        continue
                    ps = pb[bi % 4]
                    bi += 1
                    terms = [(mG, Wbg[g], Wb[:, g, ge0:ge0 + NB])]
                    for i in others:
                        mi = (j - 1 - i) if fwd else (i - j - 1)
                        terms.append((mi, Xbg[g], xv[g][:, :, i]))
                    for ti, (m_, rb, rap) in enumerate(terms):
                        lb, lap = LA(m_, g)
                        mm(p, ps, ps[:, :NB], lb, lap, rb, rap, ti == 0, ti == len(terms) - 1)
                    if (g + j) % 2 == 0:
                        ew('dve', lambda e, ps=ps, g=g, j=j, hvg=hv[g]: e.tensor_copy(hvg[:, :, j], ps[:, :NB]), [ps], [Hing[g]])
                    else:
                        ew('act', lambda e, ps=ps, g=g, j=j, hvg=hv[g]: e.copy(hvg[:, :, j], ps[:, :NB]), [ps], [Hing[g]])
            p.sync_all()
            bi = 0
            for (c0, ncol, tok0) in pieces:
                if tok0 < NCTX:
                    continue
                j0 = tok0 // S5T
                yv = Y[:, tok0 - NCTX:tok0 - NCTX + S5T * ncol].rearrange("p (c s) -> p c s", s=S5T)
                for t in range(S5T):
                    ps = pb[bi % 4]
                    bi += 1
                    ss = list(range(0, t + 1)) if fwd else list(range(t, S5T))
                    nmm = NG + len(ss)
                    k2 = 0
                    for g in range(NG):
                        mm(p, ps, ps[:, :ncol], CtLv[t][g], CtL[:, t, g, :], Hing[g], Hin[:, g, c0:c0 + ncol], k2 == 0, k2 == nmm - 1)
                        k2 += 1
                    for s in ss:
                        mm(p, ps, ps[:, :ncol], KdL, KdL[:, abs(t - s), :], u_de, u_de[:, s, j0:j0 + ncol], k2 == 0, k2 == nmm - 1)
                        k2 += 1
                    ew('dve', lambda e, ps=ps, t=t, yv=yv, ncol=ncol: e.tensor_tensor(yv[:, :, t], yv[:, :, t], ps[:, :ncol], ALU.add), [Y, ps], [Y])
            p.sync_all()
        p.op('act', lambda e: e.activation(Xbflat[:, 0:SEQ], Y[:], AF.Gelu), reads=[Y], writes=[Xb])
        p.dma('pool', lambda e, ct=ct: e.dma_start(out=gT_d[ct * 128:(ct + 1) * 128, :], in_=Xbflat[:, 0:SEQ]), reads=[Xb])
    p.close_scope()


def glu_phase(p, nc, C, gT_d, wg_s, h_in, h_out, co, chunks, out_off):
    p.open_scope()
    NT = 512
    hTf = [p.sb("hT", [128, KT, NT], F32) for _ in range(2)]
    hTs = [p.views(b, KT) for b in hTf]
    gTf = [p.sb("gT", [128, KT, NT], BF16) for _ in range(2)]
    sq = p.views(p.sb("sq", [128, KT, NT], BF16), KT)
    rstd2 = p.sb("rstd2", [128, NT], F32)
    tmps = [p.sb("tmp", [128, NT], F32) for _ in range(3)]
    sgs = [p.sb("sg", [128, NT], F32) for _ in range(2)]
    ysb = p.views(p.sb("ysb", [128, KT, NT], F32), KT)
    wg = p.sb("wg", [128, KT, KT, 256], BF16)
    p.dma('sp', lambda e: e.dma_start(out=wg[:].rearrange("p i a b -> p i (a b)"), in_=wg_s.rearrange("i p x -> p i x")), writes=[wg])
    pstat = p.ps("pstat")
    pa = [p.ps("pa") for _ in range(2)]
    pg = [p.ps("pg") for _ in range(2)]
    hin_v = h_in.rearrange("(kt p) t -> p kt t", p=128)
    hout_v = h_out.rearrange("(kt p) t -> p kt t", p=128)
    g_v = gT_d.rearrange("(kt p) t -> p kt t", p=128)
    for ci, (t0, n, kind) in enumerate(chunks):
        hT = hTs[ci % 2]
        hf = hTf[ci % 2]
        gT = gTf[ci % 2]
        p.dma('sp', lambda e, hf=hf, t0=t0, n=n: e.dma_start(out=hf[:, :, :n], in_=hin_v[:, :, t0:t0 + n]), writes=hT)
        p.dma('sp', lambda e, gT=gT, t0=t0, n=n: e.dma_start(out=gT[:, :, :n], in_=g_v[:, :, t0 - NCTX:t0 - NCTX + n]), writes=[gT])
        for i in range(KT):
            a_, g_ = pa[i % 2], pg[i % 2]
            for kt in range(KT):
                mm(p, a_, a_[:, :n], wg, wg[:, i, kt, 0:128], gT, gT[:, kt, :n], kt == 0, kt == KT - 1)
            for kt in range(KT):
                mm(p, g_, g_[:, :n], wg, wg[:, i, kt, 128:256], gT, gT[:, kt, :n], kt == 0, kt == KT - 1)
            sg = sgs[i % 2]
            p.op('act', lambda e, g_=g_, sg=sg: e.activation(sg[:, :n], g_[:, :n], AF.Sigmoid), reads=[g_], writes=[sg])
            p.op('dve', lambda e, a_=a_, sg=sg, i=i: e.tensor_tensor(ysb[i][:, :n], sg[:, :n], a_[:, :n], ALU.mult), reads=[sg, a_], writes=[ysb[i]])
            p.op('act', lambda e, i=i: e.activation(sq[i][:, :n], ysb[i][:, :n], AF.Square), reads=[ysb[i]], writes=[sq[i]])
        post_residual(p, C, hT, ysb, n, co, 1, kind, sq, pstat, rstd2, tmps)
        p.dma('pool', lambda e, hf=hf, t0=t0, n=n: e.dma_start(out=hout_v[:, :, t0 - out_off:t0 - out_off + n], in_=hf[:, :, :n]), reads=hT)
    p.close_scope()


import numpy as np
GRID_W = 64
ROPE_BASE = 10000.0

def rope_tables(seq, nctx):
    t = np.arange(seq)
    rows = (t // GRID_W).astype(np.float32)
    cols = (t % GRID_W).astype(np.float32)
    def tab(d_rot):
        d_axis = d_rot // 2
        inv = (ROPE_BASE ** (-np.arange(0, d_axis, 2, dtype=np.float32) / d_axis)).astype(np.float32)
        ang = np.concatenate([rows[:, None] * inv, cols[:, None] * inv], axis=-1).astype(np.float32)
        return np.cos(ang).astype(np.float32), np.sin(ang).astype(np.float32)
    T = nctx + seq
    ca, sa = tab(32)
    cb, sb_ = tab(64)
    A = np.zeros((32, 2, T), np.float32); A[:, 0, :] = 1.0
    A[:, 0, nctx:] = np.concatenate([ca.T, ca.T], 0)
    A[:, 1, nctx:] = np.concatenate([sa.T, sa.T], 0)
    B = np.zeros((128, 2, T), np.float32); B[:, 0, :] = 1.0
    B[:, 0, nctx:] = np.concatenate([cb.T, cb.T, cb.T, cb.T], 0)
    B[:, 1, nctx:] = np.concatenate([sb_.T, sb_.T, sb_.T, sb_.T], 0)
    return A, B

def const_inputs():
    ident = np.eye(128, dtype=np.float32)
    shiftI = np.zeros((128, 64), np.float32); shiftI[64 + np.arange(64), np.arange(64)] = 1.0
    j = np.arange(128)[:, None]; q = np.arange(128)[None, :]
    mprev = (j >= q).astype(np.float32)
    mnext = (j <= q).astype(np.float32)
    return dict(ident=ident, shiftI=shiftI, mprev=mprev, mnext=mnext)

def s5_layouts(inp):
    lre = inp['s5_lambda_re'][0]; lim = inp['s5_lambda_im'][0]; lst = inp['s5_log_step'][0]
    lam = np.zeros((2, 3, 128, 64), np.float32)
    for d in range(2):
        lam[d, 0] = np.concatenate([lre[d].T, lre[d].T], 0)
        lam[d, 1] = np.concatenate([lim[d].T, lim[d].T], 0)
        lam[d, 2] = np.broadcast_to(lst[d][None, :], (128, 64))
    br = inp['s5_b_re'][0].transpose(0, 2, 1, 3)
    bi = inp['s5_b_im'][0].transpose(0, 2, 1, 3)
    B12 = np.stack([np.concatenate([br, bi], 1), np.concatenate([bi, br], 1)], 1)
    cr = inp['s5_c_re'][0].transpose(0, 3, 1, 2)
    ci = inp['s5_c_im'][0].transpose(0, 3, 1, 2)
    C1 = np.stack([np.concatenate([cr, ci], 1), np.concatenate([ci, cr], 1)], 1)
    r = np.arange(128)
    J = (r[:, None] % 64 == r[None, :] % 64).astype(np.float32)
    rmask = (r[:, None] // 16 == np.arange(8)[None, :]).astype(np.float32)
    return dict(s5_lam=np.ascontiguousarray(lam), s5_B12=np.ascontiguousarray(B12), s5_C1=np.ascontiguousarray(C1), J128=J, rmask=rmask)


SEQ_FULL = 8192
N_CORES = 8


def build_program(SEQ=SEQ_FULL):
    T = NCTX + SEQ
    nc = bass.Bass("TRN2", target_bir_lowering=False)

    def inp(name, shape, dt=F32):
        return nc.dram_tensor(name, list(shape), dt, kind="ExternalInput").ap()

    h0 = inp("h0", [D, T]); cvec = inp("cvec", [16, 128])
    ident = inp("ident", [128, 128]); shiftI = inp("shiftI", [128, 64]); mprev = inp("mprev", [128, 128]); mnext = inp("mnext", [128, 128])
    modw = inp("mod_w", [2, D, 9 * D]); modb = inp("mod_b", [2, 72, 128]); npre = inp("norm_pre", [2, 24, 128]); npost = inp("norm_post", [2, 24, 128])
    w13 = inp("ffn_w13", [2, 2, D, 2 * DFF]); w2 = inp("ffn_w2", [2, 2, DFF, D])
    w_in = inp("attn_w_in", [1, D, 1184]); qn = inp("mla_q_norm", [2, 128]); wuq = inp("mla_w_uq", [1, 256, 768])
    kvn = inp("mla_kv_norm", [1, 128]); wukv = inp("mla_w_ukv", [1, 128, 1024]); sink = inp("sink128", [128, 8]); wout = inp("attn_w_out", [1, D, D])
    ropeA = inp("ropeA", [32, 2, T]); ropeB = inp("ropeB", [128, 2, T])
    w5in = inp("s5_w_in", [1, D, D]); dsk = inp("s5_d", [8, 128]); wglu = inp("s5_w_glu", [1, D, 2 * D])
    lam = inp("s5_lam", [2, 3, 128, 64]); B12 = inp("s5_B12", [2, 2, 128, 64, 16]); C1 = inp("s5_C1", [2, 2, 128, 64, 16])
    J = inp("J128", [128, 128]); rmask = inp("rmask", [128, 8])
    out = nc.dram_tensor("outT", [D, SEQ], F32, kind="ExternalOutput").ap()

    hA = nc.dram_tensor("hA", [D, T], F32).ap()
    hB = nc.dram_tensor("hB", [D, T], F32).ap()
    p = Prog(nc)
    C = build_consts(p, nc, ident, shiftI, mprev, mnext)
    CO = mod_phase(p, nc, C, cvec, modw, modb, npre, npost, 2)
    chunks = [(0, NCTX, 1)] + [(NCTX + 512 * i, 512, 0) for i in range(SEQ // 512)]
    lat_chunks = chunks[1:]
    g13 = [[(j * 128, 128), (DFF + j * 128, 128)] for j in range(JT)]
    g2 = [[(i * 128, 128)] for i in range(KT)]

    gattn_k = [[(128 * h, 64) for h in range(8)]]
    gattn_v = [[(128 * h + 64, 64) for h in range(8)]]
    gglu = [[(i * 128, 128), (D + i * 128, 128)] for i in range(KT)]
    W = {}

    def mk(names):
        def fn(stg):
            gens = []
            for (name, src, K, groups) in names:
                scr_, g_ = prep_weight_gen(p, nc, name, src, K, groups, stg)
                W[name] = scr_
                gens.append(g_)
            return gens
        return fn

    def ffn_w(l, f):
        return [("w13s_%d%d" % (l, f), w13[l, f], D, g13), ("w2s_%d%d" % (l, f), w2[l, f], DFF, g2)]

    attn_w = [("win_s", w_in[0], D, attn_weight_groups()), ("wuq_s", wuq[0], 256, wuq_groups()),
              ("wk_s", wukv[0], 128, gattn_k), ("wv_s", wukv[0], 128, gattn_v), ("wout_s", wout[0], D, g2)]
    s5_w = [("w5_s", w5in[0], D, g2), ("wg_s", wglu[0], D, gglu)]

    for (name, src, K, groups) in ffn_w(0, 0):
        W[name] = prep_weight(p, nc, name, src, K, groups)
    ffn_phase(p, nc, C, h0, hA, W["w13s_00"], W["w2s_00"], CO[0], 0, chunks, 0, mk(attn_w + ffn_w(0, 1)))
    scr = dict(QT=nc.dram_tensor("QT", [8, 96, T], BF16).ap(), KT=nc.dram_tensor("KTs", [8, 96, T], BF16).ap(),
               V=nc.dram_tensor("Vs", [T // 128, 128, 8, 128], BF16).ap(), QS=nc.dram_tensor("QS", [4, 128, T], BF16).ap(),
               KS=nc.dram_tensor("KS", [2, 128, T], BF16).ap(), VS=nc.dram_tensor("VS", [T // 128, 128, 2, 128], BF16).ap(),
               OT=nc.dram_tensor("OT", [D, T], BF16).ap())
    attn_proj_phase(p, nc, C, hA, CO[0], chunks, T, W["win_s"], W["wuq_s"], W["wk_s"], W["wv_s"], qn, kvn, ropeA, ropeB, scr)
    mla_phase(p, nc, C, chunks, T, scr)
    swa_phase(p, nc, C, chunks, T, scr, sink)
    mixout_phase(p, nc, C, scr['OT'], W["wout_s"], hA, hB, CO[0], 1, chunks, 8)
    ffn_phase(p, nc, C, hB, hA, W["w13s_01"], W["w2s_01"], CO[0], 2, chunks, 0, mk(ffn_w(1, 0) + s5_w))
    ffn_phase(p, nc, C, hA, hB, W["w13s_10"], W["w2s_10"], CO[1], 0, chunks, 0, mk(ffn_w(1, 1)))
    uT = nc.dram_tensor("uT", [D, T], BF16).ap()
    y0 = nc.dram_tensor("y0T", [D, SEQ], F32).ap()
    gT = nc.dram_tensor("gT", [D, SEQ], BF16).ap()
    s5_in_phase(p, nc, C, hB, CO[1], chunks, W["w5_s"], dsk, uT, y0)
    s5_scan_phase(p, nc, C, T, SEQ, uT, y0, gT, lam, B12, C1, J, rmask)
    glu_phase(p, nc, C, gT, W["wg_s"], hB, hA, CO[1], lat_chunks, 0)
    ffn_phase(p, nc, C, hA, out, W["w13s_11"], W["w2s_11"], CO[1], 2, lat_chunks, NCTX)
    p.finish()
    return nc


def make_in_maps(inputs, SEQ=SEQ_FULL, n_cores=N_CORES):
    inp = {k: np.asarray(v) for k, v in inputs.items()}
    rA, rB = rope_tables(SEQ, NCTX)
    shared = {"mod_w": inp['mod_w'], "mod_b": inp['mod_b'].reshape(2, 72, 128),
              "norm_pre": inp['norm_pre'].reshape(2, 24, 128), "norm_post": inp['norm_post'].reshape(2, 24, 128),
              "ffn_w13": inp['ffn_w13'], "ffn_w2": inp['ffn_w2'],
              "attn_w_in": inp['attn_w_in'], "mla_q_norm": inp['mla_q_norm'].reshape(2, 128), "mla_w_uq": inp['mla_w_uq'],
              "mla_kv_norm": inp['mla_kv_norm'].reshape(1, 128), "mla_w_ukv": inp['mla_w_ukv'],
              "sink128": np.ascontiguousarray(np.tile(inp['swa_sink'][0][None], (128, 1))), "attn_w_out": inp['attn_w_out'],
              "ropeA": rA, "ropeB": rB,
              "s5_w_in": inp['s5_w_in'], "s5_d": inp['s5_d'].reshape(8, 128), "s5_w_glu": inp['s5_w_glu']}
    shared.update(s5_layouts(inp))
    shared.update(const_inputs())
    shared = {k: np.ascontiguousarray(v, dtype=np.float32) for k, v in shared.items()}
    maps = []
    for b in range(n_cores):
        m = dict(shared)
        m["h0"] = np.ascontiguousarray(np.concatenate([inp['ctx'][b].T, inp['x'][b, :SEQ].T], axis=1), dtype=np.float32)
        m["cvec"] = np.ascontiguousarray(np.concatenate([inp['c'][b].reshape(8, 128), inp['c_ctx'].reshape(8, 128)], 0), dtype=np.float32)
        maps.append(m)
    return maps


def kernel(**inputs):
    nc = build_program(SEQ_FULL)
    maps = make_in_maps(inputs, SEQ_FULL, N_CORES)
    res = run_bass_kernel_spmd(nc, maps, core_ids=list(range(N_CORES)))
    out = np.stack([np.ascontiguousarray(r["outT"].T) for r in res.results], axis=0)
    return out.astype(np.float32)
```
